# Optimizing a Trainium2 kernel written in Bass

```python
import jax, jax.numpy as jnp
from jax import lax
import numpy as np

D_MODEL = 1024
BATCH = 8
SEQ = 4096
DEPTH = 2

CHUNK = 64
NORM_EPS = 1e-5

RWKV_HEAD_DIM = 64
RWKV_WIDTH = D_MODEL
RWKV_HEADS = RWKV_WIDTH // RWKV_HEAD_DIM
RWKV_DECAY_RANK = 64
RWKV_ICL_RANK = 64
RWKV_GATE_RANK = 160
RWKV_GN_EPS = 64e-5

HGRN_HEAD_DIM = 128
HGRN_WIDTH = D_MODEL
HGRN_HEADS = HGRN_WIDTH // HGRN_HEAD_DIM

SSM_WIDTH = 2 * D_MODEL
SSM_HEAD_DIM = 64
SSM_HEADS = SSM_WIDTH // SSM_HEAD_DIM
SSM_GROUPS = 4
SSM_HEADS_PER_GROUP = SSM_HEADS // SSM_GROUPS
SSM_STATE = 128
SSM_CONV_WIDTH = 4
SSM_CONV_DIM = SSM_WIDTH + 2 * SSM_GROUPS * SSM_STATE

N_BRANCHES = 3
FFN_HIDDEN = ((8 * D_MODEL + 3 * 256 - 1) // (3 * 256)) * 256

RWKV_COLS = 3 * RWKV_WIDTH + RWKV_DECAY_RANK + RWKV_ICL_RANK + RWKV_GATE_RANK
HGRN_COLS = 4 * HGRN_WIDTH
SSM_COLS = SSM_WIDTH + SSM_CONV_DIM + SSM_HEADS
GATE_COLS = N_BRANCHES * D_MODEL
OFF_HGRN = RWKV_COLS
OFF_SSM = OFF_HGRN + HGRN_COLS
OFF_GATE = OFF_SSM + SSM_COLS
IN_COLS = OFF_GATE + GATE_COLS
BRANCH_ROWS = RWKV_WIDTH + HGRN_WIDTH + SSM_WIDTH

kernel_name = 'hybrid_rwkv7_hgrn2_mamba2_gated_merge'


def rmsnorm(x, gain):
    xf = x.astype(jnp.float32)
    y = xf * lax.rsqrt(jnp.mean(xf * xf, axis=-1, keepdims=True) + NORM_EPS)
    return (y * gain.astype(jnp.float32)).astype(x.dtype)


def to_chunks(t):
    b, s = t.shape[:2]
    return jnp.swapaxes(t.reshape(b, s // CHUNK, CHUNK, *t.shape[2:]), 0, 1)


def from_chunks(t):
    n, b = t.shape[:2]
    return jnp.swapaxes(t, 0, 1).reshape(b, n * CHUNK, *t.shape[3:])


def causal_mask():
    return jnp.tril(jnp.ones((CHUNK, CHUNK), dtype=bool))


def rwkv7_mixer(u, mu, w0, w_up, a0, a_up, g_up, k_k, k_a, r_k, gn_w, gn_b):
    b, s, _ = u.shape
    f32 = jnp.float32
    hd = (RWKV_HEADS, RWKV_HEAD_DIM)
    u_prev = jnp.pad(u, ((0, 0), (1, 0), (0, 0)))[:, :-1]
    u = u + (u_prev - u) * mu
    bounds = np.cumsum([RWKV_WIDTH, RWKV_WIDTH, RWKV_WIDTH, RWKV_DECAY_RANK, RWKV_ICL_RANK]).tolist()
    r, k, v, xw, xa, xg = jnp.split(u, bounds, axis=-1)
    log_w = -jnp.exp(-jax.nn.softplus(-(w0 + jnp.tanh(xw) @ w_up)) - 0.5)
    a = jax.nn.sigmoid(a0 + xa @ a_up)
    g = jax.nn.sigmoid(xg) @ g_up
    heads = lambda t: t.astype(f32).reshape(b, s, *hd)
    r, k, v, a, log_w = heads(r), heads(k), heads(v), heads(a), heads(log_w)
    kk = k * k_k.astype(f32).reshape(hd)
    kk = kk / jnp.maximum(jnp.sqrt(jnp.sum(kk * kk, axis=-1, keepdims=True)), 1e-12)
    k = k * (1.0 + (a - 1.0) * k_a.astype(f32).reshape(hd))

    def step(state, inp):
        r_t, w_t, k_t, v_t, ka_t, kb_t = inp
        sa = jnp.einsum('bhvk,bhk->bhv', state, ka_t)
        state = (state * w_t[:, :, None, :] + sa[..., None] * kb_t[:, :, None, :]
                 + v_t[..., None] * k_t[:, :, None, :])
        return state, jnp.einsum('bhvk,bhk->bhv', state, r_t)

    xs = tuple(jnp.moveaxis(t, 1, 0) for t in (r, jnp.exp(log_w), k, v, -kk, kk * a))
    state0 = jnp.zeros((b, RWKV_HEADS, RWKV_HEAD_DIM, RWKV_HEAD_DIM), f32)
    _, o = lax.scan(step, state0, xs)
    o = jnp.moveaxis(o, 0, 1)
    mean = jnp.mean(o, axis=-1, keepdims=True)
    var = jnp.mean(jnp.square(o - mean), axis=-1, keepdims=True)
    o = (o - mean) * lax.rsqrt(var + RWKV_GN_EPS) * gn_w.astype(f32).reshape(hd) + gn_b.astype(f32).reshape(hd)
    o = o + jnp.sum(r * k * r_k.astype(f32), axis=-1, keepdims=True) * v
    return o.reshape(b, s, RWKV_WIDTH).astype(u.dtype) * g


def hgrn2_mixer(u, lb, gn_w):
    b, s, _ = u.shape
    f32 = jnp.float32
    hd = (HGRN_HEADS, HGRN_HEAD_DIM)
    q, f_pre, i, g = jnp.split(u, 4, axis=-1)
    heads = lambda t: t.astype(f32).reshape(b, s, *hd)
    lb = lb.astype(f32).reshape(hd)
    log_f = jnp.logaddexp(jnp.log(lb), jnp.log1p(-lb) + jax.nn.log_sigmoid(heads(f_pre)))
    k = -jnp.expm1(log_f)
    q = jax.nn.silu(heads(q))
    v = heads(i)
    mask = causal_mask()[None, :, :, None, None]

    def step(state, inp):
        q_c, k_c, v_c, lf_c = inp
        cum = jnp.cumsum(lf_c, axis=1)
        decay = jnp.exp(jnp.where(mask, cum[:, :, None] - cum[:, None, :], -jnp.inf))
        scores = jnp.einsum('bthd,btshd,bshd->bths', q_c, decay, k_c)
        o = (jnp.einsum('bths,bshv->bthv', scores, v_c)
             + jnp.einsum('bthd,bhdv->bthv', q_c * jnp.exp(cum), state))
        last = cum[:, -1]
        state = (jnp.exp(last)[..., None] * state
                 + jnp.einsum('bshd,bshv->bhdv', k_c * jnp.exp(last[:, None] - cum), v_c))
        return state, o

    state0 = jnp.zeros((b, HGRN_HEADS, HGRN_HEAD_DIM, HGRN_HEAD_DIM), f32)
    _, o = lax.scan(step, state0, tuple(to_chunks(t) for t in (q, k, v, log_f)))
    o = from_chunks(o)
    o = o * lax.rsqrt(jnp.mean(o * o, axis=-1, keepdims=True) + NORM_EPS) * gn_w.astype(f32).reshape(hd)
    return o.reshape(b, s, HGRN_WIDTH).astype(u.dtype) * jax.nn.sigmoid(g)


def mamba2_mixer(u, conv_w, conv_b, dt_bias, a_log, d_skip, gn_w):
    b, s, _ = u.shape
    f32 = jnp.float32
    gh = (SSM_GROUPS, SSM_HEADS_PER_GROUP)
    z, xbc, dt = jnp.split(u, [SSM_WIDTH, SSM_WIDTH + SSM_CONV_DIM], axis=-1)
    xpad = jnp.pad(xbc, ((0, 0), (SSM_CONV_WIDTH - 1, 0), (0, 0)))
    conv = conv_b + sum(xpad[:, j:j + s] * conv_w[:, j] for j in range(SSM_CONV_WIDTH))
    xbc = jax.nn.silu(conv)
    x, bm, cm = jnp.split(xbc, [SSM_WIDTH, SSM_WIDTH + SSM_GROUPS * SSM_STATE], axis=-1)
    x = x.astype(f32).reshape(b, s, *gh, SSM_HEAD_DIM)
    bm = bm.astype(f32).reshape(b, s, SSM_GROUPS, SSM_STATE)
    cm = cm.astype(f32).reshape(b, s, SSM_GROUPS, SSM_STATE)
    dt = jax.nn.softplus(dt.astype(f32) + dt_bias.astype(f32)).reshape(b, s, *gh)
    log_a = dt * (-jnp.exp(a_log.astype(f32))).reshape(gh)
    mask = causal_mask()[None, :, :, None, None]

    def step(state, inp):
        x_c, b_c, c_c, dt_c, la_c = inp
        cum = jnp.cumsum(la_c, axis=1)
        decay = jnp.exp(jnp.where(mask, cum[:, :, None] - cum[:, None, :], -jnp.inf))
        cb = jnp.einsum('btgn,bsgn->btsg', c_c, b_c)
        y = jnp.einsum('btsg,btsgh,bsgh,bsghp->btghp', cb, decay, dt_c, x_c)
        y = y + jnp.einsum('btgn,bghpn->btghp', c_c, state) * jnp.exp(cum)[..., None]
        last = cum[:, -1]
        w_s = jnp.exp(last[:, None] - cum) * dt_c
        state = (jnp.exp(last)[..., None, None] * state
                 + jnp.einsum('bsgn,bsgh,bsghp->bghpn', b_c, w_s, x_c))
        return state, y

    state0 = jnp.zeros((b, *gh, SSM_HEAD_DIM, SSM_STATE), f32)
    _, y = lax.scan(step, state0, tuple(to_chunks(t) for t in (x, bm, cm, dt, log_a)))
    y = from_chunks(y) + d_skip.astype(f32).reshape(*gh, 1) * x
    y = y.reshape(b, s, SSM_WIDTH) * jax.nn.silu(z.astype(f32))
    y = y.reshape(b, s, SSM_GROUPS, SSM_WIDTH // SSM_GROUPS)
    y = y * lax.rsqrt(jnp.mean(y * y, axis=-1, keepdims=True) + NORM_EPS)
    return (y.reshape(b, s, SSM_WIDTH) * gn_w.astype(f32)).astype(u.dtype)


def setup_inputs(seed: int = 0) -> dict:
    key = jax.random.key(seed)
    ks = iter(jax.random.split(key, 40))
    L, D = DEPTH, D_MODEL
    nrm = lambda shape, scale: scale * jax.random.normal(next(ks), shape, jnp.float32)
    unif = lambda shape, lo, hi: jax.random.uniform(next(ks), shape, jnp.float32, lo, hi)
    x = nrm((BATCH, SEQ, D), 1.0)
    norm_mix = 1.0 + nrm((L, D), 0.02)
    w_in = nrm((L, D, IN_COLS), D ** -0.5)
    rwkv_mu = unif((L, RWKV_COLS), 0.0, 1.0)
    rwkv_w0 = unif((L, RWKV_WIDTH), -4.0, 0.0)
    rwkv_w_up = nrm((L, RWKV_DECAY_RANK, RWKV_WIDTH), 0.5 * RWKV_DECAY_RANK ** -0.5)
    rwkv_a0 = nrm((L, RWKV_WIDTH), 0.5)
    rwkv_a_up = nrm((L, RWKV_ICL_RANK, RWKV_WIDTH), 0.5 * RWKV_ICL_RANK ** -0.5)
    rwkv_g_up = nrm((L, RWKV_GATE_RANK, RWKV_WIDTH), RWKV_GATE_RANK ** -0.5)
    rwkv_k_k = 0.85 + nrm((L, RWKV_WIDTH), 0.02)
    rwkv_k_a = 1.0 + nrm((L, RWKV_WIDTH), 0.02)
    rwkv_r_k = nrm((L, RWKV_HEADS, RWKV_HEAD_DIM), 0.1)
    rwkv_gn_w = 1.0 + nrm((L, RWKV_WIDTH), 0.02)
    rwkv_gn_b = nrm((L, RWKV_WIDTH), 0.02)
    hgrn_lb_logits = nrm((L, HGRN_WIDTH), 0.5)
    hgrn_gn_w = 1.0 + nrm((L, HGRN_WIDTH), 0.02)
    ssm_conv_w = nrm((L, SSM_CONV_DIM, SSM_CONV_WIDTH), SSM_CONV_WIDTH ** -0.5)
    ssm_conv_b = nrm((L, SSM_CONV_DIM), 0.02)
    dt0 = jnp.exp(unif((L, SSM_HEADS), float(np.log(1e-3)), float(np.log(1e-1))))
    ssm_dt_bias = dt0 + jnp.log(-jnp.expm1(-dt0))
    ssm_a_log = jnp.log(unif((L, SSM_HEADS), 1.0, 16.0))
    ssm_d = 1.0 + nrm((L, SSM_HEADS), 0.1)
    ssm_gn_w = 1.0 + nrm((L, SSM_WIDTH), 0.02)
    w_branch = jnp.concatenate([nrm((L, RWKV_WIDTH, D), RWKV_WIDTH ** -0.5),
                                nrm((L, HGRN_WIDTH, D), HGRN_WIDTH ** -0.5),
                                nrm((L, SSM_WIDTH, D), SSM_WIDTH ** -0.5)], axis=1)
    w_out = nrm((L, D, D), D ** -0.5)
    norm_ffn = 1.0 + nrm((L, D), 0.02)
    w_ffn_in = nrm((L, D, 2 * FFN_HIDDEN), D ** -0.5)
    w_ffn_out = nrm((L, FFN_HIDDEN, D), FFN_HIDDEN ** -0.5)
    norm_final = 1.0 + nrm((D,), 0.02)
    return {'x': x, 'norm_mix': norm_mix, 'w_in': w_in,
            'rwkv_mu': rwkv_mu, 'rwkv_w0': rwkv_w0, 'rwkv_w_up': rwkv_w_up, 'rwkv_a0': rwkv_a0,
            'rwkv_a_up': rwkv_a_up, 'rwkv_g_up': rwkv_g_up, 'rwkv_k_k': rwkv_k_k, 'rwkv_k_a': rwkv_k_a,
            'rwkv_r_k': rwkv_r_k, 'rwkv_gn_w': rwkv_gn_w, 'rwkv_gn_b': rwkv_gn_b,
            'hgrn_lb_logits': hgrn_lb_logits, 'hgrn_gn_w': hgrn_gn_w,
            'ssm_conv_w': ssm_conv_w, 'ssm_conv_b': ssm_conv_b, 'ssm_dt_bias': ssm_dt_bias,
            'ssm_a_log': ssm_a_log, 'ssm_d': ssm_d, 'ssm_gn_w': ssm_gn_w,
            'w_branch': w_branch, 'w_out': w_out, 'norm_ffn': norm_ffn,
            'w_ffn_in': w_ffn_in, 'w_ffn_out': w_ffn_out, 'norm_final': norm_final}


def reference(x, norm_mix, w_in, rwkv_mu, rwkv_w0, rwkv_w_up, rwkv_a0, rwkv_a_up, rwkv_g_up,
              rwkv_k_k, rwkv_k_a, rwkv_r_k, rwkv_gn_w, rwkv_gn_b, hgrn_lb_logits, hgrn_gn_w,
              ssm_conv_w, ssm_conv_b, ssm_dt_bias, ssm_a_log, ssm_d, ssm_gn_w,
              w_branch, w_out, norm_ffn, w_ffn_in, w_ffn_out, norm_final):
    b, s, _ = x.shape
    cs = jnp.cumsum(jax.nn.softmax(hgrn_lb_logits.astype(jnp.float32), axis=0), axis=0)
    lbs = cs - cs[:1]
    for l in range(DEPTH):
        h = rmsnorm(x, norm_mix[l])
        wi = w_in[l]
        y_a = rwkv7_mixer(h @ wi[:, :OFF_HGRN], rwkv_mu[l], rwkv_w0[l], rwkv_w_up[l], rwkv_a0[l],
                          rwkv_a_up[l], rwkv_g_up[l], rwkv_k_k[l], rwkv_k_a[l], rwkv_r_k[l],
                          rwkv_gn_w[l], rwkv_gn_b[l])
        y_b = hgrn2_mixer(h @ wi[:, OFF_HGRN:OFF_SSM], lbs[l], hgrn_gn_w[l])
        y_c = mamba2_mixer(h @ wi[:, OFF_SSM:OFF_GATE], ssm_conv_w[l], ssm_conv_b[l], ssm_dt_bias[l],
                           ssm_a_log[l], ssm_d[l], ssm_gn_w[l])
        gates = jax.nn.sigmoid(h @ wi[:, OFF_GATE:]).reshape(b, s, N_BRANCHES, D_MODEL)
        wb = w_branch[l]
        merged = (gates[:, :, 0] * (y_a @ wb[:RWKV_WIDTH])
                  + gates[:, :, 1] * (y_b @ wb[RWKV_WIDTH:RWKV_WIDTH + HGRN_WIDTH])
                  + gates[:, :, 2] * (y_c @ wb[RWKV_WIDTH + HGRN_WIDTH:]))
        x = x + merged @ w_out[l]
        h = rmsnorm(x, norm_ffn[l])
        gate, up = jnp.split(h @ w_ffn_in[l], 2, axis=-1)
        x = x + (jax.nn.silu(gate) * up) @ w_ffn_out[l]
    return rmsnorm(x, norm_final)
```

```python
import numpy as np
from contextlib import ExitStack
import concourse.bass as bass
import concourse.mybir as mybir

F32 = mybir.dt.float32
BF16 = mybir.dt.bfloat16
AF = mybir.ActivationFunctionType
ALU = mybir.AluOpType
AX = mybir.AxisListType


class Buf:
    __slots__ = ("name", "w", "r", "dsem", "dcount")

    def __init__(self, name):
        self.name = name
        self.w = None
        self.r = []
        self.dsem = None
        self.dcount = 0


class Tl:
    __slots__ = ("ap", "buf")

    def __init__(self, ap, buf):
        self.ap = ap
        self.buf = buf

    def __getitem__(self, k):
        return Tl(self.ap[k], self.buf)

    def v(self, fn):
        return Tl(fn(self.ap), self.buf)

    @property
    def shape(self):
        return self.ap.shape


class Prog:
    ENGS = ("pe", "act", "dve", "pool", "sp")

    def __init__(self, nc, same_eng_sync=True):
        self.nc = nc
        self.es = ExitStack()
        self.streams = {e: [] for e in self.ENGS}
        self.count = {e: 0 for e in self.ENGS}
        self.seen = {e: {} for e in self.ENGS}
        self.sem = {}
        for e in self.ENGS:
            self.sem[e] = self.es.enter_context(nc.semaphore("sem_" + e))
        self.same_eng_sync = same_eng_sync
        self.nbuf = 0
        self.n_wait = 0
        self.n_ins = 0

    def sbuf(self, name, shape, dtype):
        t = self.es.enter_context(self.nc.sbuf_tensor(name, list(shape), dtype))
        return Tl(t[:], Buf(name))

    def psum(self, name, shape, dtype=F32):
        t = self.es.enter_context(self.nc.psum_tensor(name, list(shape), dtype))
        return Tl(t[:], Buf(name))

    def dram(self, name, shape, dtype, kind="Internal"):
        t = self.nc.dram_tensor(name, list(shape), dtype, kind=kind)
        return Tl(t.ap(), Buf(name))

    def newbuf(self, name="b"):
        self.nbuf += 1
        return Buf(f"{name}{self.nbuf}")

    def _dsem(self, buf):
        if buf.dsem is None:
            buf.dsem = self.es.enter_context(self.nc.semaphore("d_" + buf.name))
        return buf.dsem

    def _need(self, eng, reads, writes):
        need = {}

        def add(tok):
            if tok is None:
                return
            kind = tok[0]
            if kind == "E":
                _, e2, seq = tok
                if e2 == eng and (eng == "pe" or not self.same_eng_sync):
                    return
                key = ("E", e2)
                sem, val = self.sem[e2], seq
            else:
                _, b, n = tok
                key = ("D", id(b))
                sem, val = b.dsem, 16 * n
            if key not in need or need[key][1] < val:
                need[key] = (sem, val)

        for b in reads:
            add(b.w)
        for b in writes:
            add(b.w)
            for t in b.r:
                add(t)
        out = []
        seen = self.seen[eng]
        for key, (sem, val) in need.items():
            if seen.get(key, 0) >= val:
                continue
            seen[key] = val
            out.append((sem, val))
        return out

    def op(self, eng, fn, reads=(), writes=()):
        reads = [t.buf for t in reads if t is not None and isinstance(t, Tl)]
        writes = [t.buf for t in writes if t is not None and isinstance(t, Tl)]
        for sem, val in self._need(eng, reads, writes):
            self.streams[eng].append(("w", sem, val))
            self.n_wait += 1
        self.count[eng] += 1
        seq = self.count[eng]
        self.streams[eng].append(("i", fn, self.sem[eng], 1))
        self.n_ins += 1
        tok = ("E", eng, seq)
        for b in reads:
            b.r.append(tok)
        for b in writes:
            b.w = tok
            b.r = []
        return tok

    def dma(self, q, out, in_, sem_tl):
        reads = [in_.buf]
        writes = [out.buf]
        sb = sem_tl.buf
        sem = self._dsem(sb)
        for s, val in self._need(q, reads, writes):
            self.streams[q].append(("w", s, val))
            self.n_wait += 1
        sb.dcount += 1
        oap, iap = out.ap, in_.ap
        self.streams[q].append(("i", lambda e: e.dma_start(out=oap, in_=iap), sem, 16))
        self.n_ins += 1
        tok = ("D", sb, sb.dcount)
        in_.buf.r.append(tok)
        out.buf.w = tok
        out.buf.r = []
        return tok

    def wait_all_dma(self, q, tls):
        for t in tls:
            b = t.buf
            if b.dsem is not None and b.dcount > 0:
                self.streams[q].append(("w", b.dsem, 16 * b.dcount))

    def mm(self, out, lhsT, rhs, start=True, stop=True):
        o, l, r = out.ap, lhsT.ap, rhs.ap
        return self.op("pe", lambda e: e.matmul(o, lhsT=l, rhs=r, start=start, stop=stop),
                       reads=[lhsT, rhs], writes=[out])

    def transpose(self, out, in_, ident):
        o, i, d = out.ap, in_.ap, ident.ap
        return self.op("pe", lambda e: e.transpose(o, i, d), reads=[in_, ident], writes=[out])

    def act(self, out, in_, func, bias=0.0, scale=1.0, accum=None, eng="act"):
        o, i = out.ap, in_.ap
        b = bias.ap if isinstance(bias, Tl) else bias
        s = scale.ap if isinstance(scale, Tl) else scale
        reads = [in_] + [x for x in (bias, scale) if isinstance(x, Tl)]
        writes = [out]
        if accum is not None:
            a = accum.ap
            writes.append(accum)
            return self.op(eng, lambda e: e.activation(out=o, in_=i, func=func, bias=b, scale=s, accum_out=a),
                           reads=reads, writes=writes)
        return self.op(eng, lambda e: e.activation(out=o, in_=i, func=func, bias=b, scale=s),
                       reads=reads, writes=writes)

    def tt(self, eng, out, a, b, op):
        o, x, y = out.ap, a.ap, b.ap
        return self.op(eng, lambda e: e.tensor_tensor(out=o, in0=x, in1=y, op=op), reads=[a, b], writes=[out])

    def ts(self, eng, out, a, s1, op0, s2=None, op1=None):
        o, x = out.ap, a.ap
        v1 = s1.ap if isinstance(s1, Tl) else s1
        v2 = s2.ap if isinstance(s2, Tl) else s2
        reads = [a] + [x_ for x_ in (s1, s2) if isinstance(x_, Tl)]
        if op1 is None:
            return self.op(eng, lambda e: e.tensor_scalar(out=o, in0=x, scalar1=v1, scalar2=None, op0=op0),
                           reads=reads, writes=[out])
        return self.op(eng, lambda e: e.tensor_scalar(out=o, in0=x, scalar1=v1, scalar2=v2, op0=op0, op1=op1),
                       reads=reads, writes=[out])

    def stt(self, eng, out, a, s, b, op0, op1):
        o, x, y = out.ap, a.ap, b.ap
        sv = s.ap if isinstance(s, Tl) else s
        reads = [a, b] + ([s] if isinstance(s, Tl) else [])
        return self.op(eng, lambda e: e.scalar_tensor_tensor(out=o, in0=x, scalar=sv, in1=y, op0=op0, op1=op1),
                       reads=reads, writes=[out])

    def copy(self, eng, out, in_):
        o, i = out.ap, in_.ap
        if eng == "act":
            return self.op(eng, lambda e: e.copy(out=o, in_=i), reads=[in_], writes=[out])
        return self.op(eng, lambda e: e.tensor_copy(out=o, in_=i), reads=[in_], writes=[out])

    def memset(self, eng, out, val):
        o = out.ap
        return self.op(eng, lambda e: e.memset(o, val), reads=[], writes=[out])

    def recip(self, out, in_):
        o, i = out.ap, in_.ap
        return self.op("dve", lambda e: e.reciprocal(out=o, in_=i), reads=[in_], writes=[out])

    def emit(self, final_waits=()):
        nc = self.nc
        streams = self.streams
        for e in self.ENGS:
            if e != "sp" and self.count[e] > 0:
                streams["sp"].append(("w", self.sem[e], self.count[e]))
        self.wait_all_dma("sp", final_waits)

        def run(eng_handle, lst):
            for it in lst:
                if it[0] == "w":
                    eng_handle.wait_ge(it[1], it[2])
                else:
                    _, fn, sem, inc = it
                    fn(eng_handle).then_inc(sem, inc)

        with nc.Block() as block:
            @block.tensor
            def _(e):
                run(e, streams["pe"])

            @block.scalar
            def _(e):
                run(e, streams["act"])

            @block.vector
            def _(e):
                run(e, streams["dve"])

            @block.gpsimd
            def _(e):
                run(e, streams["pool"])

            @block.sync
            def _(e):
                run(e, streams["sp"])
        self.es.close()


from concourse.bass_utils import run_bass_kernel_spmd

D = 1024
KC_D = 8
NL = 2
RW_COLS = 3360
OFF_HGRN = 3360
OFF_SSM = 7456
OFF_GATE = 12608
IN_COLS = 15680
FFN_H = 2816
EPS = 1e-5

WNAMES = ["w_in", "w_branch", "w_out", "w_ffn_in", "w_ffn_out", "rwkv_w_up", "rwkv_a_up", "rwkv_g_up"]
WSHAPES = {"w_in": (1024, IN_COLS), "w_branch": (4096, 1024), "w_out": (1024, 1024),
           "w_ffn_in": (1024, 2 * FFN_H), "w_ffn_out": (FFN_H, 1024),
           "rwkv_w_up": (64, 1024), "rwkv_a_up": (64, 1024), "rwkv_g_up": (160, 1024)}
VEC_SHAPES = {"norm_mix": (NL, 1024), "rwkv_mu": (NL, 3360), "rwkv_w0": (NL, 1024), "rwkv_a0": (NL, 1024),
              "rwkv_k_k": (NL, 1024), "rwkv_k_a": (NL, 1024), "rwkv_r_k": (NL, 16, 64),
              "rwkv_gn_w": (NL, 1024), "rwkv_gn_b": (NL, 1024), "hgrn_lb_logits": (NL, 1024),
              "hgrn_gn_w": (NL, 1024), "ssm_conv_w": (NL, 3072, 4), "ssm_conv_b": (NL, 3072),
              "ssm_dt_bias": (NL, 32), "ssm_a_log": (NL, 32), "ssm_d": (NL, 32), "ssm_gn_w": (NL, 2048),
              "norm_ffn": (NL, 1024), "norm_final": (1024,)}


class Model:
    def __init__(self, NT=1, T=512, layers=(0, 1), stub=("rwkv", "hgrn", "ssm"), dbg=None):
        self.NT, self.T, self.layers, self.stub = NT, T, tuple(layers), set(stub)
        self.dbg = dbg or []
        self.S = NT * T
        self.NF32 = 34
        self.NBF16 = 44
        nc = bass.Bass("TRN2", target_bir_lowering=False)
        self.nc = nc
        self.P = Prog(nc)
        self.build()

    def build(self):
        P, nc, T = self.P, self.nc, self.T
        S = self.S
        self.x_in = P.dram("x", [S, D], F32, kind="ExternalInput")
        self.out = P.dram("out", [S, D], F32, kind="ExternalOutput")
        self.win = {}
        for n in WNAMES:
            sh = WSHAPES[n]
            self.win[n] = P.dram(n, [NL, sh[0], sh[1]], F32, kind="ExternalInput")
        self.vin = {}
        for n, sh in VEC_SHAPES.items():
            self.vin[n] = P.dram(n, list(sh), F32, kind="ExternalInput")
        self.wbf = {}
        for l in self.layers:
            for n in WNAMES:
                sh = WSHAPES[n]
                self.wbf[(n, l)] = P.dram(f"{n}_bf{l}", [sh[0], sh[1]], BF16)
        self.dbg_out = {}

        self.NCH = T // 64
        self.NSLOT = 3
        self.SLOTE = 4096
        self.wslots = [P.sbuf(f"wslot{i}", [128, self.SLOTE], BF16) for i in range(self.NSLOT)]
        self.wi = 0
        self.psb = [P.psum(f"ps{i}", [128, 512], F32) for i in range(8)]
        self.pi = 0
        self.ident = P.sbuf("ident", [128, 128], F32)
        self.identb = P.sbuf("identb", [128, 128], BF16)
        self.ones = P.sbuf("ones", [128, 128], F32)
        self.epsc = P.sbuf("epsc", [128, 1], F32)
        self.xres = self.slabs("xres", 8, F32)
        self.hT = self.slabs("hT", 8, BF16)
        self.f32pool = self.slabs("f32pool", self.NF32, F32)
        self.bf16pool = self.slabs("bf16pool", self.NBF16, BF16)
        self.tokbuf = [P.sbuf(f"tokbuf{i}", [128, D], F32) for i in range(2)]
        self.tki = 0
        self.vecs = {}
        self.setup_consts()
        self.cast_weights()
        for ti in range(self.NT):
            self.load_x(ti)
            for l in self.layers:
                self.layer(l, ti)
            self.final(ti)
        P.emit(final_waits=self.tokbuf + getattr(self, 'dbg_tls', []))

    def slabs(self, name, n, dtype, width=None):
        P = self.P
        w = width or self.T
        t = P.es.enter_context(self.nc.sbuf_tensor(name, [128, n, w], dtype))
        return [Tl(t[:, i, :], Buf(f"{name}{i}")) for i in range(n)]

    def dump(self, name, tl):
        if not self.dbg or name in self.dbg_out:
            return
        P = self.P
        o = P.dram("dbg_" + name, list(tl.shape), F32, kind="ExternalOutput")
        P.dma("pool", o, tl, tl)
        self.dbg_out[name] = o
        self.dbg_tls = getattr(self, "dbg_tls", []) + [tl]

    def ps(self):
        t = self.psb[self.pi % 4]
        self.pi += 1
        return t

    def a32(self, n=None):
        if n is None:
            return self.f32pool.pop()
        return [self.f32pool.pop() for _ in range(n)]

    def a16(self, n=None):
        if n is None:
            return self.bf16pool.pop()
        return [self.bf16pool.pop() for _ in range(n)]

    def f32free(self, *ts):
        for t in ts:
            self.f32pool.extend(t if isinstance(t, list) else [t])

    def f16free(self, *ts):
        for t in ts:
            self.bf16pool.extend(t if isinstance(t, list) else [t])

    def setup_consts(self):
        P = self.P
        nc = self.nc
        P.memset("pool", self.ones, 1.0)
        P.memset("pool", self.epsc, EPS)
        self.gnepsc = P.sbuf("gnepsc", [128, 1], F32)
        P.memset("pool", self.gnepsc, 64e-5)
        P.memset("pool", self.ident, 1.0)
        ia = self.ident.ap
        P.op("pool", lambda e: e.affine_select(out=ia, in_=ia, pattern=[[-1, 128]], compare_op=ALU.is_equal,
                                               fill=0.0, base=0, channel_multiplier=1),
             reads=[self.ident], writes=[self.ident])
        P.copy("dve", self.identb, self.ident)
        self.vstage = P.sbuf("vstage", [128, 512], F32)
        self.vcol = {}
        T = self.T
        self.rmask = P.sbuf("rmask", [128, T], F32)
        P.memset("pool", self.rmask, 1.0)
        P.memset("pool", self.rmask.v(lambda a: a.rearrange("p (c t) -> p c t", t=64)[:, :, 0:1]), 0.0)
        self.neg8 = P.sbuf("neg8", [64, 8, 64], BF16)
        P.memset("pool", self.neg8, 0.0)
        na = self.neg8.ap
        P.op("pool", lambda e: e.affine_select(out=na, in_=na, pattern=[[0, 8], [1, 64]], compare_op=ALU.is_ge,
                                               fill=-30000.0, base=0, channel_multiplier=-1),
             reads=[self.neg8], writes=[self.neg8])
        self.tri_incl = P.sbuf("tri_incl", [64, 64], F32)
        P.memset("pool", self.tri_incl, 1.0)
        ta = self.tri_incl.ap
        P.op("pool", lambda e: e.affine_select(out=ta, in_=ta, pattern=[[1, 64]], compare_op=ALU.is_ge,
                                               fill=0.0, base=0, channel_multiplier=-1),
             reads=[self.tri_incl], writes=[self.tri_incl])
        self.tri_strict = P.sbuf("tri_strict", [64, 64], F32)
        P.memset("pool", self.tri_strict, 1.0)
        tsa = self.tri_strict.ap
        P.op("pool", lambda e: e.affine_select(out=tsa, in_=tsa, pattern=[[1, 64]], compare_op=ALU.is_gt,
                                               fill=0.0, base=0, channel_multiplier=-1),
             reads=[self.tri_strict], writes=[self.tri_strict])
        self.sel = P.sbuf("sel", [32, 2048], F32)
        P.memset("pool", self.sel, 1.0)
        sa = self.sel.ap
        P.op("pool", lambda e: e.affine_select(out=sa, in_=sa, pattern=[[1, 2048]], compare_op=ALU.is_ge,
                                               fill=0.0, base=0, channel_multiplier=-64),
             reads=[self.sel], writes=[self.sel])
        P.op("pool", lambda e: e.affine_select(out=sa, in_=sa, pattern=[[-1, 2048]], compare_op=ALU.is_ge,
                                               fill=0.0, base=63, channel_multiplier=64),
             reads=[self.sel], writes=[self.sel])
        for l in self.layers:
            specs = [("norm_mix", "norm_mix", 8), ("norm_ffn", "norm_ffn", 8),
                     ("ssm_conv_b", "ssm_conv_b", 24), ("ssm_gn_w", "ssm_gn_w", 16),
                     ("hgrn_gn_w", "hgrn_gn_w", 8)]
            self.load_cols(l, f"vcA{l}", specs)
            self.setup_ssm(l)
            self.setup_hgrn(l)
            self.setup_rwkv(l)
        l0 = self.layers[0]
        self.load_cols(None, "vcF", [("norm_final", "norm_final", 8)])

    def load_cols(self, l, name, specs, srcs=None):
        P = self.P
        tot = sum(x[2] for x in specs)
        assert tot <= 128
        P.memset("dve", self.vstage[:, 0:128], 0.0)
        r = 0
        for si, (key, n, nr) in enumerate(specs):
            if srcs is not None:
                src = srcs[si]
            elif l is None:
                src = self.vin[n].v(lambda a: a.rearrange("(r c) -> r c", c=128))
            else:
                src = self.vin[n].v(lambda a: a[l].rearrange("(r c) -> r c", c=128))
            P.dma("sp", self.vstage[r:r + nr, 0:src.shape[1]], src, self.vstage)
            r += nr
        pt = self.ps()
        P.transpose(pt[:, 0:128], self.vstage[:, 0:128], self.ident)
        vc = P.sbuf(name, [128, tot], F32)
        P.copy("dve", vc, pt[:, 0:tot])
        r = 0
        for key, n, nr in specs:
            self.vcol[(key, l)] = vc[:, r:r + nr]
            r += nr

    def cast_weights(self):
        P = self.P
        for l in self.layers:
            for n in WNAMES:
                K = WSHAPES[n][0]
                dst = self.wbf[(n, l)]
                for r0 in range(0, K, 128):
                    nr = min(128, K - r0)
                    P.dma("pool", dst[r0:r0 + nr, :], self.win[n].v(lambda a: a[l, r0:r0 + nr, :]), dst)

    def load_w(self, wt, KC, c0, cw, r0=0, rows=None):
        P = self.P
        slot = self.wslots[self.wi % self.NSLOT]
        self.wi += 1
        assert KC * cw <= self.SLOTE, (KC, cw)
        view = slot.v(lambda a: a[:, :KC * cw].rearrange("p (k c) -> p k c", c=cw))
        if rows is None:
            src = wt.v(lambda a: a[r0:r0 + KC * 128, c0:c0 + cw].rearrange("(k p) n -> p k n", p=128))
            P.dma("sp", view, src, slot)
        else:
            assert KC == 1
            src = wt.v(lambda a: a[r0:r0 + rows, c0:c0 + cw])
            P.dma("sp", view.v(lambda a: a[0:rows, 0, :]), src, slot)
        return view

    def proj(self, wt, rhs, c0, ncols, consume, r0=0, cw=512, rows=None):
        P, T = self.P, self.T
        KC = len(rhs)
        cw = min(cw, (self.SLOTE // KC) // 128 * 128)
        j = 0
        for cb in range(0, ncols, cw):
            w = min(cw, ncols - cb)
            view = self.load_w(wt, KC, c0 + cb, w, r0=r0, rows=rows)
            for b in range(0, w, 128):
                nb = min(128, w - b)
                pt = self.ps()
                for k in range(KC):
                    P.mm(pt[0:nb, 0:T], view[0:rhs[k].shape[0], k, b:b + nb], rhs[k], start=(k == 0), stop=(k == KC - 1))
                consume(pt[0:nb, 0:T], j, nb)
                j += 1

    def load_x(self, ti):
        P, T = self.P, self.T
        for tb in range(T // 128):
            tk = self.tokbuf[self.tki % 2]
            self.tki += 1
            r0 = ti * T + tb * 128
            P.dma("sp", tk, self.x_in[r0:r0 + 128, :], tk)
            for c in range(8):
                pt = self.ps()
                P.transpose(pt[:, 0:128], tk[:, c * 128:(c + 1) * 128], self.ident)
                P.copy("dve" if c % 2 else "act", self.xres[c][:, tb * 128:(tb + 1) * 128], pt[:, 0:128])

    def rmsnorm(self, gain, dst, last=False):
        P, T = self.P, self.T
        pt = self.ps()
        tmpA = self.a32(2)
        for c in range(8):
            sq = tmpA[c % 2]
            P.act(sq, self.xres[c], AF.Square)
            P.mm(pt[:, 0:T], self.ones, sq, start=(c == 0), stop=(c == 7))
        rstd = tmpA[0]
        pt = pt[:, 0:T]
        P.act(rstd, pt, AF.Sqrt, bias=self.epsc, scale=1.0 / D)
        P.recip(rstd, rstd)
        for c in range(8):
            P.stt("dve", dst[c], self.xres[c], gain[:, c:c + 1], rstd, ALU.mult, ALU.mult)
        self.f32free(tmpA)

    def layer(self, l, ti):
        P, T = self.P, self.T
        w_in = self.wbf[("w_in", l)]
        wb = self.wbf[("w_branch", l)]
        self.rmsnorm(self.vcol[("norm_mix", l)], self.hT)
        self.merged = self.a32(8)
        self.gate = self.a32(2)
        branches = [("rwkv", 0, 1024, 0), ("hgrn", OFF_HGRN, 1024, 1024), ("ssm", OFF_SSM, 2048, 2048)]
        for bi, (name, off, width, brow) in enumerate(branches):
            nchunk = width // 128
            self.yT = self.a16(nchunk)
            if name in self.stub:
                def cons(pt, j, nb):
                    P.copy("act", self.yT[j], pt)
                self.proj(w_in, self.hT, off, width, cons)
            else:
                getattr(self, "mixer_" + name)(l, ti)
            for j in range(8):
                gt = self.gate[j % 2]

                def cons_g(pt, jj, nb, gt=gt):
                    P.act(gt, pt, AF.Sigmoid)
                self.proj(w_in, self.hT, OFF_GATE + bi * D + j * 128, 128, cons_g, cw=128)

                def cons_b(pt, jj, nb, gt=gt, j=j, bi=bi):
                    if bi == 0:
                        P.tt("dve", self.merged[j], pt, gt, ALU.mult)
                    else:
                        P.tt("dve", gt, pt, gt, ALU.mult)
                        P.tt("pool", self.merged[j], self.merged[j], gt, ALU.add)
                self.proj(wb, self.yT[:nchunk], j * 128, 128, cons_b, r0=brow, cw=128)
            self.f16free(self.yT)
        self.mergedb = self.a16(8)
        for j in range(8):
            P.copy("act", self.mergedb[j], self.merged[j])
        self.f32free(self.merged)
        def cons_o(pt, j, nb):
            P.tt("dve", self.xres[j], self.xres[j], pt, ALU.add)
        self.proj(self.wbf[("w_out", l)], self.mergedb, 0, D, cons_o)
        self.f16free(self.mergedb)
        self.ffn = self.a16(22)
        self.rmsnorm(self.vcol[("norm_ffn", l)], self.hT)
        wf = self.wbf[("w_ffn_in", l)]
        for j in range(22):
            sg = self.gate[j % 2]

            def cons_gate(pt, jj, nb, sg=sg):
                P.act(sg, pt, AF.Silu)

            def cons_up(pt, jj, nb, sg=sg, j=j):
                P.tt("dve", self.ffn[j], pt, sg, ALU.mult)
            self.proj(wf, self.hT, j * 128, 128, cons_gate, cw=128)
            self.proj(wf, self.hT, FFN_H + j * 128, 128, cons_up, cw=128)
        self.proj(self.wbf[("w_ffn_out", l)], self.ffn, 0, D, cons_o, cw=256)
        self.f16free(self.ffn)
        self.f32free(self.gate)

    def final(self, ti):
        P, T = self.P, self.T
        hf = self.a32(8)
        self.rmsnorm(self.vcol[("norm_final", None)], hf)
        for tb in range(T // 128):
            tk = self.tokbuf[self.tki % 2]
            self.tki += 1
            for c in range(8):
                pt = self.ps()
                P.transpose(pt[:, 0:128], hf[c][:, tb * 128:(tb + 1) * 128], self.ident)
                P.copy("dve" if c % 2 else "act", tk[:, c * 128:(c + 1) * 128], pt[:, 0:128])
            r0 = ti * T + tb * 128
            P.dma("sp", self.out[r0:r0 + 128, :], tk, tk)
        self.f32free(hf)


def make_in_map(inputs, b, S):
    m = {"x": np.ascontiguousarray(inputs["x"][b, :S])}
    for n in WNAMES:
        m[n] = np.ascontiguousarray(inputs[n])
    for n in VEC_SHAPES:
        m[n] = np.ascontiguousarray(inputs[n])
    return m


def bc(t, shape):
    return t.v(lambda a: a.broadcast_to(list(shape)))


def v3(t, inner=64):
    return t.v(lambda a: a.rearrange("p (c t) -> p c t", t=inner))


def setup_ssm(self, l):
    P = self.P
    NCH = self.NCH
    if not hasattr(self, "ssm"):
        self.ssm = {}
        T = self.T
        self.ext = [P.sbuf(f"ext{i}", [128, T + 3], F32) for i in range(2)]
        self.exti = 0
        self.rbd = [P.sbuf(f"rbd{i}", [32, 512], F32) for i in range(2)]
        self.e1 = [P.sbuf(f"e1_{i}", [64, 512], F32) for i in range(2)]
        self.cbs = [P.sbuf(f"cbs{i}", [64, 64], F32) for i in range(2)]
        self.Gb = P.sbuf("Gb", [64, NCH, 512], BF16)
        self.xtok = P.sbuf("xtok", [64, NCH, 512], BF16)
        self.xw = P.sbuf("xw", [64, NCH, 512], BF16)
        self.Btok = P.sbuf("Btok", [64, NCH, 128], BF16)
        self.Sb = [P.sbuf(f"Sb{i}", [128, 512], BF16) for i in range(2)]
        self.tokA = P.sbuf("tokA", [64, NCH * 32], F32)
        self.tokW = P.sbuf("tokW", [64, NCH * 32], F32)
        self.elast = P.sbuf("elast", [32, NCH], F32)
        self.rhs_e = P.sbuf("rhs_e", [32, NCH, 32], F32)
        self.elast_bc = P.sbuf("elast_bc", [128, NCH, 32], F32)
    d = {}
    P.dma("sp", self.vstage[0:24, 0:512],
          self.vin["ssm_conv_w"].v(lambda a: a[l].rearrange("(r c) j -> r (c j)", c=128)), self.vstage)
    cw = P.sbuf(f"convw{l}", [128, 4, 24], F32)
    for j in range(4):
        pt = self.ps()
        P.transpose(pt[:, 0:24], self.vstage.v(lambda a: a[0:24, j:512:4]), self.ident[0:24, 0:24])
        P.copy("dve", cw[:, j, :], pt[:, 0:24])
    d["cw"] = cw
    hp = P.sbuf(f"ssmh{l}", [32, 4], F32)
    for i, n in enumerate(["ssm_dt_bias", "ssm_a_log", "ssm_d"]):
        P.dma("sp", hp[:, i:i + 1], self.vin[n].v(lambda a: a[l].rearrange("(h o) -> h o", o=1)), hp)
    P.act(hp[:, 3:4], hp[:, 1:2], AF.Exp)
    P.ts("dve", hp[:, 3:4], hp[:, 3:4], -1.0, ALU.mult)
    d["hp"] = hp
    d2 = P.sbuf(f"ssmd2{l}", [32, 2], F32)
    P.copy("dve", d2[:, 0:1], hp[:, 2:3])
    P.copy("dve", d2[:, 1:2], hp[:, 2:3])
    pt = self.ps()
    for hpi in range(16):
        P.mm(pt[:, 2 * hpi:2 * hpi + 2], self.sel[:, 128 * hpi:128 * hpi + 128], d2)
    dcol = P.sbuf(f"dcol{l}", [128, 16], F32)
    P.copy("dve", dcol, pt.v(lambda a: a[:, 0:32].rearrange("p (h two) -> p h two", two=2)[:, :, 0]))
    d["dcol"] = dcol
    S = P.es.enter_context(self.nc.sbuf_tensor(f"ssmS{l}", [128, 4, 512], F32))
    d["S"] = [Tl(S[:, g, :], Buf(f"ssmS{l}_{g}")) for g in range(4)]
    for g in range(4):
        P.memset("pool", d["S"][g], 0.0)
    carry = P.sbuf(f"carry{l}", [128, 24, 3], F32)
    P.memset("pool", carry, 0.0)
    d["carry"] = carry
    self.ssm[l] = d


def mixer_ssm(self, l, ti):
    P, T, NCH = self.P, self.T, self.NCH
    d = self.ssm[l]
    w_in = self.wbf[("w_in", l)]
    c_z = OFF_SSM
    c_xbc = OFF_SSM + 2048
    c_dt = OFF_SSM + 2048 + 3072
    hpv, cw, cbv, dcol, carry = d["hp"], d["cw"], self.vcol[("ssm_conv_b", l)], d["dcol"], d["carry"]
    gnw = self.vcol[("ssm_gn_w", l)]
    dts = self.a32(5)
    raw, dtT, lndt, cum, wv = [t[0:32, :] for t in dts]

    def cons_dt(pt, j, nb):
        P.act(raw, pt, AF.Exp, bias=hpv[:, 0:1])
    self.proj(w_in, self.hT, c_dt, 32, cons_dt, cw=128)
    P.act(dtT, raw, AF.Ln, bias=self.ones[0:32, 0:1])
    P.act(lndt, dtT, AF.Ln)
    P.ts("dve", raw, dtT, hpv[:, 3:4], ALU.mult)
    ca, ra, rm = cum.ap, raw.ap, self.rmask[0:32, :].ap
    P.op("dve", lambda e: e.tensor_tensor_scan(out=ca, data0=rm, data1=ra, initial=0.0, op0=ALU.mult, op1=ALU.add),
         reads=[self.rmask, raw], writes=[cum])
    cs = lndt
    P.tt("dve", cs, cum, lndt, ALU.subtract)
    lastb = bc(v3(cum)[:, :, 63:64], [32, NCH, 64])
    P.tt("dve", v3(wv), lastb, v3(cs), ALU.subtract)
    P.act(wv, wv, AF.Exp)
    P.act(self.elast, v3(cum)[:, :, 63], AF.Exp)
    ptT = self.ps()
    for c in range(NCH):
        P.transpose(ptT[0:64, c * 32:(c + 1) * 32], cs[:, c * 64:(c + 1) * 64], self.ident[0:32, 0:32])
    P.copy("dve", self.tokA, ptT[0:64, 0:NCH * 32])
    ptT = self.ps()
    for c in range(NCH):
        P.transpose(ptT[0:64, c * 32:(c + 1) * 32], wv[:, c * 64:(c + 1) * 64], self.ident[0:32, 0:32])
    P.copy("dve", self.tokW, ptT[0:64, 0:NCH * 32])
    cs_tok = v3(self.tokA, 32)
    w_tok = v3(self.tokW, 32)
    P.tt("dve", self.rhs_e, bc(self.elast.v(lambda a: a.unsqueeze(2)), [32, NCH, 32]),
         bc(self.ident[0:32, 0:32].v(lambda a: a.unsqueeze(1)), [32, NCH, 32]), ALU.mult)
    pte = self.ps()
    P.mm(pte[:, 0:NCH * 32], self.ones[0:32, :], self.rhs_e.v(lambda a: a.rearrange("p c h -> p (c h)")))
    P.copy("dve", self.elast_bc.v(lambda a: a.rearrange("p c h -> p (c h)")), pte[:, 0:NCH * 32])

    yT = self.yT
    for g in range(4):
        xc = self.a32(4)
        xcb = self.a16(4)
        Bb, Cb = self.a16(2)

        def conv(pt, ci, dst32, dst16):
            ext = self.ext[self.exti % 2]
            self.exti += 1
            P.copy("pool", ext[:, 0:3], carry[:, ci, :])
            P.copy("act", ext[:, 3:3 + T], pt)
            P.copy("pool", carry[:, ci, :], ext[:, T:T + 3])
            acc = self.a32()
            P.ts("dve", acc, ext[:, 0:T], cw[:, 0, ci:ci + 1], ALU.mult, cbv[:, ci:ci + 1], ALU.add)
            for j in range(1, 4):
                P.stt("dve", acc, ext[:, j:j + T], cw[:, j, ci:ci + 1], acc, ALU.mult, ALU.add)
            if dst32 is not None:
                P.act(dst32, acc, AF.Silu)
                P.copy("pool", dst16, dst32)
            else:
                P.act(dst16, acc, AF.Silu)
            self.f32free(acc)

        self.proj(w_in, self.hT, c_xbc + 512 * g, 512, lambda pt, j, nb: conv(pt, 4 * g + j, xc[j], xcb[j]))
        self.proj(w_in, self.hT, c_xbc + 2048 + 128 * g, 128, lambda pt, j, nb: conv(pt, 16 + g, None, Bb), cw=128)
        self.proj(w_in, self.hT, c_xbc + 2560 + 128 * g, 128, lambda pt, j, nb: conv(pt, 20 + g, None, Cb), cw=128)
        for c in range(NCH):
            cs_ = slice(c * 64, (c + 1) * 64)
            rbd = self.rbd[c % 2]
            P.tt("dve", v3(rbd), bc(cum[:, cs_].v(lambda a: a.unsqueeze(1)), [32, 8, 64]),
                 bc(self.ident[0:32, 8 * g:8 * g + 8].v(lambda a: a.unsqueeze(2)), [32, 8, 64]), ALU.mult)
            pe = self.ps()
            P.mm(pe[0:64, :], self.ones[0:32, 0:64], rbd, start=True, stop=False)
            P.mm(pe[0:64, :], self.identb[0:64, 0:64], self.neg8.v(lambda a: a.rearrange("p h t -> p (h t)")),
                 start=False, stop=True)
            e1 = self.e1[c % 2]
            P.tt("dve", v3(e1), v3(pe[0:64, :]),
                 bc(cs_tok[:, c, 8 * g:8 * g + 8].v(lambda a: a.unsqueeze(2)), [64, 8, 64]), ALU.subtract)
            P.act(e1, e1, AF.Exp)
            pcb = self.ps()
            P.mm(pcb[0:64, 0:64], Bb[:, cs_], Cb[:, cs_])
            cbs = self.cbs[c % 2]
            P.copy("act", cbs, pcb[0:64, 0:64])
            P.tt("pool", v3(self.Gb[:, c, :]), v3(e1), bc(cbs.v(lambda a: a.unsqueeze(1)), [64, 8, 64]), ALU.mult)
            ptx = self.ps().v(lambda a: a.bitcast(BF16))
            for j in range(4):
                P.transpose(ptx[0:64, j * 128:(j + 1) * 128], xcb[j][:, cs_], self.identb)
            P.transpose(ptx[0:64, 512:640], Bb[:, cs_], self.identb)
            P.copy("act", self.xtok[:, c, :], ptx[0:64, 0:512])
            P.copy("act", self.Btok[:, c, :], ptx[0:64, 512:640])
            P.tt("pool", v3(self.xw[:, c, :]), v3(self.xtok[:, c, :]),
                 bc(w_tok[:, c, 8 * g:8 * g + 8].v(lambda a: a.unsqueeze(2)), [64, 8, 64]), ALU.mult)
        S = d["S"][g]
        inter = self.psb[4:8]
        for c in range(NCH):
            cs_ = slice(c * 64, (c + 1) * 64)
            Sb = self.Sb[c % 2]
            P.copy("act", Sb, S)
            for hp in range(4):
                P.mm(inter[hp][:, cs_], Sb[:, hp * 128:(hp + 1) * 128], Cb[:, cs_])
            pd = self.ps()
            P.mm(pd[:, 0:512], self.Btok[:, c, :], self.xw[:, c, :])
            P.tt("dve", v3(S), v3(S), bc(self.elast_bc[:, c, 8 * g:8 * g + 8].v(lambda a: a.unsqueeze(2)), [128, 8, 64]),
                 ALU.mult)
            P.tt("dve", S, S, pd[:, 0:512], ALU.add)
        ys = []
        pn = self.psb[4]
        for hp in range(4):
            hh = 4 * g + hp
            pe2 = self.ps()
            P.mm(pe2[:, 0:T], self.sel[:, 128 * hh:128 * hh + 128], cum)
            ecb = self.a32()
            P.act(ecb, pe2[:, 0:T], AF.Exp)
            tmp = self.a32()
            P.tt("dve", tmp, inter[hp][:, 0:T], ecb, ALU.mult)
            self.f32free(ecb)
            pin = self.ps()
            for c in range(NCH):
                for q in range(2):
                    hs = slice((2 * hp + q) * 64, (2 * hp + q) * 64 + 64)
                    P.mm(pin[64 * q:64 * q + 64, c * 64:(c + 1) * 64], self.xtok[:, c, hs], self.Gb[:, c, hs])
            P.stt("dve", tmp, xc[hp], dcol[:, hh:hh + 1], tmp, ALU.mult, ALU.add)
            P.tt("dve", tmp, tmp, pin[:, 0:T], ALU.add)
            zs = self.a32()
            self.proj(w_in, self.hT, c_z + 128 * hh, 128, lambda pt, j, nb: P.act(zs, pt, AF.Silu), cw=128)
            P.tt("pool", tmp, tmp, zs, ALU.mult)
            P.act(zs, tmp, AF.Square)
            P.mm(pn[:, 0:T], self.ones, zs, start=(hp == 0), stop=(hp == 3))
            self.f32free(zs)
            ys.append(tmp)
        rstd = self.a32()
        P.act(rstd, pn[:, 0:T], AF.Sqrt, bias=self.epsc, scale=1.0 / 512)
        P.recip(rstd, rstd)
        for hp in range(4):
            hh = 4 * g + hp
            P.stt("dve", yT[hh], ys[hp], gnw[:, hh:hh + 1], rstd, ALU.mult, ALU.mult)
        self.f32free(rstd, ys, xc)
        self.f16free(xcb, [Bb, Cb])
    self.f32free(dts)


Model.setup_ssm = setup_ssm
Model.mixer_ssm = mixer_ssm


def setup_hgrn(self, l):
    P = self.P
    NCH = self.NCH
    if not hasattr(self, "hg"):
        self.hg = {}
        self.load_cols(None, "vcLB", [("lb0", None, 8), ("lb1", None, 8)], srcs=[
            self.vin["hgrn_lb_logits"].v(lambda a: a[0].rearrange("(r c) -> r c", c=128)),
            self.vin["hgrn_lb_logits"].v(lambda a: a[1].rearrange("(r c) -> r c", c=128))])
        lb = P.sbuf("hg_lb", [128, 2, 8], F32)
        P.memset("dve", lb, 0.0)
        P.tt("dve", lb[:, 1, :], self.vcol[("lb1", None)], self.vcol[("lb0", None)], ALU.subtract)
        P.act(lb[:, 1, :], lb[:, 1, :], AF.Sigmoid)
        oml = P.sbuf("hg_oml", [128, 2, 8], F32)
        P.ts("dve", oml, lb, -1.0, ALU.mult, 1.0, ALU.add)
        self.hg_lb, self.hg_oml = lb, oml
        self.hg_vtok = P.sbuf("hg_vtok", [64, NCH, 128], BF16)
        self.hg_ktok = P.sbuf("hg_ktok", [64, NCH, 128], BF16)
        self.hg_scT = [P.sbuf(f"hg_scT{i}", [64, 64], BF16) for i in range(2)]
        self.hg_Sb = [P.sbuf(f"hg_Sb{i}", [128, 128], BF16) for i in range(2)]
        self.hg_cols = P.sbuf("hg_cols", [128, 5, NCH], F32)
    S = P.es.enter_context(self.nc.sbuf_tensor(f"hgS{l}", [128, 8, 128], F32))
    d = {"S": [Tl(S[:, h, :], Buf(f"hgS{l}_{h}")) for h in range(8)]}
    for h in range(8):
        P.memset("pool", d["S"][h], 0.0)
    self.hg[l] = d


def mixer_hgrn(self, l, ti):
    P, T, NCH = self.P, self.T, self.NCH
    d = self.hg[l]
    w_in = self.wbf[("w_in", l)]
    li = l
    gnw = self.vcol[("hgrn_gn_w", l)]
    for h in range(8):
        S = d["S"][h]
        f, cum, eq, ek, qs = self.a32(5)
        qb, kb, ib = self.a16(3)
        self.proj(w_in, self.hT, OFF_HGRN + 1024 + 128 * h, 128, lambda pt, j, nb: P.act(f, pt, AF.Sigmoid), cw=128)
        P.ts("dve", f, f, self.hg_oml[:, li, h:h + 1], ALU.mult, self.hg_lb[:, li, h:h + 1], ALU.add)
        if h == 0 and ti == 0:
            self.dump(f"hg_f{l}", f)
        P.act(eq, f, AF.Ln)
        if h == 0 and ti == 0:
            self.dump(f"hg_lnf{l}", eq)
        ca, la, rm = cum.ap, eq.ap, self.rmask.ap
        P.op("dve", lambda e, ca=ca, la=la, rm=rm: e.tensor_tensor_scan(out=ca, data0=rm, data1=la, initial=0.0,
                                                                     op0=ALU.mult, op1=ALU.add),
             reads=[self.rmask, eq], writes=[cum])
        if h == 0 and ti == 0:
            self.dump(f"hg_cumraw{l}", cum)
        P.ts("dve", f, f, -1.0, ALU.mult, 1.0, ALU.add)
        cols = self.hg_cols
        c3 = v3(cum)
        P.act(cols[:, 0, :], c3[:, :, 32], AF.Exp)
        P.act(cols[:, 1, :], c3[:, :, 63], AF.Exp)
        P.tt("dve", cols[:, 3, :], c3[:, :, 63], c3[:, :, 32], ALU.subtract)
        P.act(cols[:, 2, :], cols[:, 3, :], AF.Exp)
        P.copy("dve", cols[:, 4, :], c3[:, :, 32])
        P.tt("dve", c3, c3, bc(cols[:, 4, :].v(lambda a: a.unsqueeze(2)), [128, NCH, 64]), ALU.subtract)
        P.act(eq, cum, AF.Exp)
        P.act(ek, cum, AF.Exp, scale=-1.0)
        self.proj(w_in, self.hT, OFF_HGRN + 128 * h, 128, lambda pt, j, nb: P.act(qs, pt, AF.Silu), cw=128)
        P.tt("dve", qb, qs, eq, ALU.mult)
        P.tt("pool", kb, f, ek, ALU.mult)
        self.proj(w_in, self.hT, OFF_HGRN + 2048 + 128 * h, 128, lambda pt, j, nb: P.copy("act", ib, pt), cw=128)
        ptv = self.ps().v(lambda a: a.bitcast(BF16))
        for c in range(NCH):
            P.transpose(ptv[0:64, c * 128:(c + 1) * 128], ib[:, c * 64:(c + 1) * 64], self.identb)
        P.copy("act", self.hg_vtok.v(lambda a: a.rearrange("p c v -> p (c v)")), ptv[0:64, 0:NCH * 128])
        ptk = self.ps().v(lambda a: a.bitcast(BF16))
        for c in range(NCH):
            P.transpose(ptk[0:64, c * 128:(c + 1) * 128], kb[:, c * 64:(c + 1) * 64], self.identb)
        P.copy("dve", self.hg_ktok.v(lambda a: a.rearrange("p c v -> p (c v)")), ptk[0:64, 0:NCH * 128])
        po = self.psb[5]
        for c in range(NCH):
            cs_ = slice(c * 64, (c + 1) * 64)
            psc = self.ps()
            P.mm(psc[0:64, 0:64], kb[:, cs_], qb[:, cs_])
            scT = self.hg_scT[c % 2]
            P.tt("dve", scT, psc[0:64, 0:64], self.tri_incl, ALU.mult)
            Sb = self.hg_Sb[c % 2]
            P.ts("dve", Sb, S, cols[:, 0, c:c + 1], ALU.mult)
            P.mm(po[:, cs_], self.hg_vtok[:, c, :], scT, start=True, stop=False)
            P.mm(po[:, cs_], Sb, qb[:, cs_], start=False, stop=True)
            pd = self.ps()
            P.mm(pd[:, 0:128], self.hg_ktok[:, c, :], self.hg_vtok[:, c, :])
            P.ts("dve", S, S, cols[:, 1, c:c + 1], ALU.mult)
            P.stt("dve", S, pd[:, 0:128], cols[:, 2, c:c + 1], S, ALU.mult, ALU.add)
        o32 = qs
        P.copy("act", o32, po[:, 0:T])
        if h == 0 and ti == 0:
            self.dump(f"hg_o{l}", o32)
        P.act(eq, o32, AF.Square)
        pn = self.ps()
        P.mm(pn[:, 0:T], self.ones, eq)
        rstd = ek
        P.act(rstd, pn[:, 0:T], AF.Sqrt, bias=self.epsc, scale=1.0 / 128)
        P.recip(rstd, rstd)
        P.stt("dve", o32, o32, gnw[:, h:h + 1], rstd, ALU.mult, ALU.mult)
        self.proj(w_in, self.hT, OFF_HGRN + 3072 + 128 * h, 128, lambda pt, j, nb: P.act(f, pt, AF.Sigmoid), cw=128)
        P.tt("dve", self.yT[h], o32, f, ALU.mult)
        if h == 0 and ti == 0:
            self.dump(f"hg_y{l}", self.yT[h])
            self.dump(f"hg_cum{l}", cum)
            self.dump(f"hg_qb{l}", qb)
            self.dump(f"hg_kb{l}", kb)
            self.dump(f"hg_ib{l}", ib)
        self.f32free(f, cum, eq, ek, qs)
        self.f16free(qb, kb, ib)


Model.setup_hgrn = setup_hgrn
Model.mixer_hgrn = mixer_hgrn


C0 = float(np.exp(-0.5))


def setup_rwkv(self, l):
    P = self.P
    NCH, T = self.NCH, self.T
    if not hasattr(self, "rw"):
        self.rw = {}
        self.bones = P.sbuf("bones", [128, 128], F32)
        P.memset("pool", self.bones, 0.0)
        P.memset("pool", self.bones[0:64, 0:64], 1.0)
        P.memset("pool", self.bones[64:128, 64:128], 1.0)
        self.tri_ls = P.sbuf("tri_ls", [64, 64], F32)
        P.memset("pool", self.tri_ls, 1.0)
        ta = self.tri_ls.ap
        P.op("pool", lambda e: e.affine_select(out=ta, in_=ta, pattern=[[-1, 64]], compare_op=ALU.is_gt,
                                               fill=0.0, base=0, channel_multiplier=1),
             reads=[self.tri_ls], writes=[self.tri_ls])
        self.rext = [P.sbuf(f"rext{i}", [128, T + 1], F32) for i in range(2)]
        self.rexti = 0
        self.rw_Vtok = P.sbuf("rw_Vtok", [64, NCH, 128], BF16)
        self.rw_btok = P.sbuf("rw_btok", [64, NCH, 128], BF16)
        self.rw_ktok = P.sbuf("rw_ktok", [64, NCH, 128], BF16)
        self.rw_cols = P.sbuf("rw_cols", [128, 5, NCH], F32)
        self.rw_Sb = [P.sbuf(f"rw_Sb{i}", [128, 64], BF16) for i in range(2)]
        self.rw_P = [P.sbuf(f"rw_P{i}", [64, 64], F32) for i in range(4)]
        self.rw_Q = [P.sbuf(f"rw_Q{i}", [64, 64], F32) for i in range(4)]
        self.rw_MT = [P.sbuf(f"rw_MT{i}", [64, 64], F32) for i in range(2)]
        self.rw_MTb = [P.sbuf(f"rw_MTb{i}", [64, 64], BF16) for i in range(2)]
        self.rw_A = [P.sbuf(f"rw_A{i}", [64, 3, 64], BF16) for i in range(2)]
        self.rw_Xb = [P.sbuf(f"rw_Xb{i}", [64, 64], BF16) for i in range(2)]
        self.rw_Ub = [P.sbuf(f"rw_Ub{i}", [64, 64], BF16) for i in range(2)]
        self.rw_lr = [P.sbuf(f"rw_lr{i}", [128, T], BF16) for i in range(4)]
    d = {}
    specs = [("mu", None, 26), ("mu26", None, 1), ("muxa", None, 1)]
    srcs = [self.vin["rwkv_mu"].v(lambda a: a[l, 0:3328].rearrange("(r c) -> r c", c=128)),
            self.vin["rwkv_mu"].v(lambda a: a[l, 3328:3360].rearrange("(r c) -> r c", c=32)),
            self.vin["rwkv_mu"].v(lambda a: a[l, 3136:3200].rearrange("(r c) -> r c", c=64))]
    for n in ["w0", "a0", "k_k", "k_a", "gn_w", "gn_b"]:
        specs.append((n, None, 8))
        srcs.append(self.vin["rwkv_" + n].v(lambda a: a[l].rearrange("(r c) -> r c", c=128)))
    specs.append(("r_k", None, 8))
    srcs.append(self.vin["rwkv_r_k"].v(lambda a: a[l].rearrange("h (two c) -> (h two) c", two=1).rearrange("(r x) c -> r (x c)", x=2)))
    self.load_cols(("rw", l), f"vcR{l}", specs, srcs=srcs)
    for key, _, _ in specs:
        d[key] = self.vcol[(key, ("rw", l))]
    omk = P.sbuf(f"rw_omk{l}", [128, 8], F32)
    P.ts("dve", omk, d["k_a"], -1.0, ALU.mult, 1.0, ALU.add)
    d["omk"] = omk
    S = P.es.enter_context(self.nc.sbuf_tensor(f"rwS{l}", [128, 8, 64], F32))
    d["S"] = [Tl(S[:, p, :], Buf(f"rwS{l}_{p}")) for p in range(8)]
    for p in range(8):
        P.memset("pool", d["S"][p], 0.0)
    carry = P.sbuf(f"rw_carry{l}", [128, 28], F32)
    P.memset("pool", carry, 0.0)
    d["carry"] = carry
    self.rw[l] = d


def mixer_rwkv(self, l, ti):
    P, T, NCH = self.P, self.T, self.NCH
    d = self.rw[l]
    w_in = self.wbf[("w_in", l)]
    carry = d["carry"]

    def lerp(pt, np_, mucol, cidx, dst):
        ext = self.rext[self.rexti % 2]
        self.rexti += 1
        P.copy("pool", ext[0:np_, 0:1], carry[0:np_, cidx:cidx + 1])
        P.copy("act", ext[0:np_, 1:T + 1], pt)
        P.copy("pool", carry[0:np_, cidx:cidx + 1], ext[0:np_, T:T + 1])
        dd = self.a32()
        P.tt("dve", dd[0:np_, :], ext[0:np_, 0:T], ext[0:np_, 1:T + 1], ALU.subtract)
        P.stt("dve", dst, dd[0:np_, :], mucol, ext[0:np_, 1:T + 1], ALU.mult, ALU.add)
        self.f32free(dd)

    tmp = self.a32()
    txw, xab, sg0, sg1 = self.rw_lr
    self.proj(w_in, self.hT, 3072, 64, lambda pt, j, nb: lerp(pt, 64, d["mu"][0:64, 24:25], 24, tmp[0:64, :]), cw=128)
    P.act(txw[0:64, :], tmp[0:64, :], AF.Tanh)
    self.proj(w_in, self.hT, 3136, 64, lambda pt, j, nb: lerp(pt, 64, d["muxa"][0:64, 0:1], 27, tmp[0:64, :]), cw=128)
    P.copy("act", xab[0:64, :], tmp[0:64, :])
    self.proj(w_in, self.hT, 3200, 128, lambda pt, j, nb: lerp(pt, 128, d["mu"][:, 25:26], 25, tmp), cw=128)
    P.act(sg0, tmp, AF.Sigmoid)
    self.proj(w_in, self.hT, 3328, 32, lambda pt, j, nb: lerp(pt, 32, d["mu26"][0:32, 0:1], 26, tmp[0:32, :]), cw=128)
    P.act(sg1[0:32, :], tmp[0:32, :], AF.Sigmoid)
    self.f32free(tmp)
    cols = self.rw_cols
    for p in range(8):
        r, k, v, sg, Sc, a, kkn, g = self.a32(8)
        ab, rb, bb, kb, vb = self.a16(5)
        self.proj(w_in, self.hT, 128 * p, 128, lambda pt, j, nb: lerp(pt, 128, d["mu"][:, p:p + 1], p, r), cw=128)
        self.proj(w_in, self.hT, 1024 + 128 * p, 128, lambda pt, j, nb: lerp(pt, 128, d["mu"][:, 8 + p:9 + p], 8 + p, k), cw=128)
        self.proj(w_in, self.hT, 2048 + 128 * p, 128, lambda pt, j, nb: lerp(pt, 128, d["mu"][:, 16 + p:17 + p], 16 + p, v), cw=128)
        self.proj(self.wbf[("rwkv_w_up", l)], [txw[0:64, :]], 128 * p, 128,
                  lambda pt, j, nb: P.act(sg, pt, AF.Sigmoid, bias=d["w0"][:, p:p + 1]), cw=128, rows=64)
        self.proj(self.wbf[("rwkv_a_up", l)], [xab[0:64, :]], 128 * p, 128,
                  lambda pt, j, nb: P.act(a, pt, AF.Sigmoid, bias=d["a0"][:, p:p + 1]), cw=128, rows=64)
        vw0 = self.load_w(self.wbf[("rwkv_g_up", l)], 1, 128 * p, 128, r0=0, rows=128)
        vw1 = self.load_w(self.wbf[("rwkv_g_up", l)], 1, 128 * p, 128, r0=128, rows=32)
        pg = self.ps()
        P.mm(pg[:, 0:T], vw0[:, 0, :], sg0, start=True, stop=False)
        P.mm(pg[:, 0:T], vw1[0:32, 0, :], sg1[0:32, :], start=False, stop=True)
        P.copy("act", g, pg[:, 0:T])
        S_ = Sc
        sa, ga, rm = S_.ap, sg.ap, self.rmask.ap
        P.op("dve", lambda e, sa=sa, ga=ga, rm=rm: e.tensor_tensor_scan(out=sa, data0=rm, data1=ga, initial=0.0,
                                                                     op0=ALU.mult, op1=ALU.add),
             reads=[self.rmask, sg], writes=[S_])
        s3 = v3(S_)
        P.act(cols[:, 0, :], s3[:, :, 32], AF.Exp, scale=-C0)
        P.act(cols[:, 1, :], s3[:, :, 63], AF.Exp, scale=-C0)
        P.tt("dve", cols[:, 3, :], s3[:, :, 63], s3[:, :, 32], ALU.subtract)
        P.act(cols[:, 2, :], cols[:, 3, :], AF.Exp, scale=-C0)
        P.copy("dve", cols[:, 4, :], s3[:, :, 32])
        P.tt("dve", s3, s3, bc(cols[:, 4, :].v(lambda a_: a_.unsqueeze(2)), [128, NCH, 64]), ALU.subtract)
        e1, e2, t1 = self.a32(3)
        P.tt("dve", t1, Sc, sg, ALU.subtract)
        P.ts("dve", kkn, k, d["k_k"][:, p:p + 1], ALU.mult)
        P.act(e1, kkn, AF.Square)
        pn = self.ps()
        P.mm(pn[:, 0:T], self.bones, e1)
        P.act(e1, pn[:, 0:T], AF.Sqrt)
        P.ts("dve", e1, e1, 1e-12, ALU.max)
        P.recip(e1, e1)
        P.tt("dve", kkn, kkn, e1, ALU.mult)
        P.act(e2, t1, AF.Exp, scale=-C0)
        P.stt("dve", ab, kkn, -1.0, e2, ALU.mult, ALU.mult)
        P.act(e1, Sc, AF.Exp, scale=-C0)
        P.tt("pool", rb, r, e1, ALU.mult)
        P.act(e2, Sc, AF.Exp, scale=C0)
        P.tt("dve", t1, kkn, a, ALU.mult)
        P.tt("pool", bb, t1, e2, ALU.mult)
        P.ts("dve", t1, a, d["k_a"][:, p:p + 1], ALU.mult, d["omk"][:, p:p + 1], ALU.add)
        P.tt("dve", k, k, t1, ALU.mult)
        P.tt("pool", kb, k, e2, ALU.mult)
        P.copy("act", vb, v)
        P.stt("dve", t1, r, d["r_k"][:, p:p + 1], k, ALU.mult, ALU.mult)
        pbn = self.ps()
        P.mm(pbn[:, 0:T], self.bones, t1)
        bonus = r
        P.tt("dve", bonus, pbn[:, 0:T], v, ALU.mult)
        self.f32free(e1, e2, t1)
        for src, dstt in ((vb, self.rw_Vtok), (bb, self.rw_btok), (kb, self.rw_ktok)):
            ptt = self.ps().v(lambda a_: a_.bitcast(BF16))
            for c in range(NCH):
                P.transpose(ptt[0:64, c * 128:(c + 1) * 128], src[:, c * 64:(c + 1) * 64], self.identb)
            P.copy("act", dstt.v(lambda a_: a_.rearrange("p c v -> p (c v)")), ptt[0:64, 0:NCH * 128])
        S = d["S"][p]
        pO = self.psb[4]
        for c in range(NCH):
            cs_ = slice(c * 64, (c + 1) * 64)
            Sb = self.rw_Sb[c % 2]
            P.ts("dve", Sb, S, cols[:, 0, c:c + 1], ALU.mult)
            pS = self.psb[5]
            for q in range(2):
                hs = slice(64 * q, 64 * q + 64)
                pA, pB, pC, pD = self.psb[0], self.psb[1], self.psb[2], self.psb[3]
                P.mm(pA[0:64, 0:64], bb[hs, cs_], ab[hs, cs_])
                P.mm(pA[0:64, 64:128], ab[hs, cs_], bb[hs, cs_])
                P.mm(pA[0:64, 128:192], kb[hs, cs_], ab[hs, cs_])
                P.mm(pA[0:64, 192:256], bb[hs, cs_], rb[hs, cs_])
                P.mm(pA[0:64, 256:320], kb[hs, cs_], rb[hs, cs_])
                Pm, Qm = self.rw_P[0], self.rw_Q[0]
                A3 = self.rw_A[q]
                P.tt("dve", Pm, pA[0:64, 0:64], self.tri_strict, ALU.mult)
                P.tt("dve", Qm, pA[0:64, 64:128], self.tri_ls, ALU.mult)
                P.tt("dve", A3[:, 0, :], pA[0:64, 128:192], self.tri_strict, ALU.mult)
                P.tt("dve", A3[:, 1, :], pA[0:64, 192:256], self.tri_incl, ALU.mult)
                P.tt("dve", A3[:, 2, :], pA[0:64, 256:320], self.tri_incl, ALU.mult)
                MT = self.rw_MT[0]
                P.tt("pool", MT, Pm, self.ident[0:64, 0:64], ALU.add)
                for it in range(1, 6):
                    Pn, Qn = self.rw_P[it % 2 + 2 * (it % 2 == 0)], self.rw_Q[it % 2 + 2 * (it % 2 == 0)]
                    Pn, Qn = self.rw_P[it % 2], self.rw_Q[it % 2]
                    if it < 5:
                        P.mm(pB[0:64, 0:64], Qm, Pm)
                        P.copy("act", Pn, pB[0:64, 0:64])
                    P.mm(pC[0:64, 0:64], Pm, Qm)
                    P.copy("dve", Qn, pC[0:64, 0:64])
                    P.mm(pD[0:64, 0:64], Qn, MT)
                    MTn = self.rw_MT[it % 2]
                    P.tt("dve", MTn, MT, pD[0:64, 0:64], ALU.add)
                    Pm, Qm, MT = Pn, Qn, MTn
                MTb = self.rw_MTb[q]
                P.copy("act", MTb, MT)
                Xb, Ub = self.rw_Xb[q], self.rw_Ub[q]
                Vt = self.rw_Vtok[:, c, hs]
                P.mm(pB[0:64, 64:128], ab[hs, cs_], Sb[hs, :], start=True, stop=False)
                P.mm(pB[0:64, 64:128], A3[:, 0, :], Vt, start=False, stop=True)
                P.copy("act", Xb, pB[0:64, 64:128])
                P.mm(pC[0:64, 64:128], MTb, Xb)
                P.copy("act", Ub, pC[0:64, 64:128])
                P.mm(pO[hs, cs_], Sb[hs, :], rb[hs, cs_], start=True, stop=False)
                P.mm(pO[hs, cs_], Ub, A3[:, 1, :], start=False, stop=False)
                P.mm(pO[hs, cs_], Vt, A3[:, 2, :], start=False, stop=True)
                P.mm(pS[hs, 0:64], self.rw_btok[:, c, hs], Ub, start=True, stop=False)
                P.mm(pS[hs, 0:64], self.rw_ktok[:, c, hs], Vt, start=False, stop=True)
            P.ts("dve", S, S, cols[:, 1, c:c + 1], ALU.mult)
            P.stt("dve", S, pS[:, 0:64], cols[:, 2, c:c + 1], S, ALU.mult, ALU.add)
        o = k
        P.copy("act", o, pO[:, 0:T])
        pm = self.ps()
        P.mm(pm[:, 0:T], self.bones, o)
        P.stt("dve", o, pm[:, 0:T], -1.0 / 64, o, ALU.mult, ALU.add)
        P.act(sg, o, AF.Square)
        pv = self.ps()
        P.mm(pv[:, 0:T], self.bones, sg)
        P.act(sg, pv[:, 0:T], AF.Sqrt, bias=self.gnepsc, scale=1.0 / 64)
        P.recip(sg, sg)
        P.stt("dve", o, o, d["gn_w"][:, p:p + 1], sg, ALU.mult, ALU.mult)
        P.stt("dve", o, o, d["gn_b"][:, p:p + 1], bonus, ALU.add, ALU.add)
        P.tt("dve", self.yT[p], o, g, ALU.mult)
        self.f32free(r, k, v, sg, Sc, a, kkn, g)
        self.f16free(ab, rb, bb, kb, vb)


Model.setup_rwkv = setup_rwkv
Model.mixer_rwkv = mixer_rwkv


def kernel(**inputs):
    inputs = {k_: np.asarray(v_) for k_, v_ in inputs.items()}
    S = 4096
    m = Model(NT=S // 256, T=256, layers=(0, 1), stub=())
    maps = [make_in_map(inputs, b, S) for b in range(8)]
    res = run_bass_kernel_spmd(m.nc, maps, core_ids=list(range(8)))
    out = np.stack([np.asarray(r["out"]) for r in res.results]).astype(np.float32)
    return out
```

```python
import numpy as np
from contextlib import ExitStack
import concourse.bass as bass
import concourse.mybir as mybir

F32 = mybir.dt.float32
BF16 = mybir.dt.bfloat16
AF = mybir.ActivationFunctionType
ALU = mybir.AluOpType
AX = mybir.AxisListType


class Buf:
    __slots__ = ("name", "w", "r", "dsem", "dcount")

    def __init__(self, name):
        self.name = name
        self.w = None
        self.r = []
        self.dsem = None
        self.dcount = 0


class Tl:
    __slots__ = ("ap", "buf")

    def __init__(self, ap, buf):
        self.ap = ap
        self.buf = buf

    def __getitem__(self, k):
        return Tl(self.ap[k], self.buf)

    def v(self, fn):
        return Tl(fn(self.ap), self.buf)

    @property
    def shape(self):
        return self.ap.shape


class Prog:
    ENGS = ("pe", "act", "dve", "pool", "sp")

    def __init__(self, nc, same_eng_sync=True):
        self.nc = nc
        self.es = ExitStack()
        self.streams = {e: [] for e in self.ENGS}
        self.count = {e: 0 for e in self.ENGS}
        self.seen = {e: {} for e in self.ENGS}
        self.sem = {}
        for e in self.ENGS:
            self.sem[e] = self.es.enter_context(nc.semaphore("sem_" + e))
        self.same_eng_sync = same_eng_sync
        self.nbuf = 0
        self.n_wait = 0
        self.n_ins = 0

    def sbuf(self, name, shape, dtype):
        t = self.es.enter_context(self.nc.sbuf_tensor(name, list(shape), dtype))
        return Tl(t[:], Buf(name))

    def psum(self, name, shape, dtype=F32):
        t = self.es.enter_context(self.nc.psum_tensor(name, list(shape), dtype))
        return Tl(t[:], Buf(name))

    def dram(self, name, shape, dtype, kind="Internal"):
        t = self.nc.dram_tensor(name, list(shape), dtype, kind=kind)
        return Tl(t.ap(), Buf(name))

    def newbuf(self, name="b"):
        self.nbuf += 1
        return Buf(f"{name}{self.nbuf}")

    def _dsem(self, buf):
        if buf.dsem is None:
            buf.dsem = self.es.enter_context(self.nc.semaphore("d_" + buf.name))
        return buf.dsem

    def _need(self, eng, reads, writes):
        need = {}

        def add(tok):
            if tok is None:
                return
            kind = tok[0]
            if kind == "E":
                _, e2, seq = tok
                if e2 == eng and (eng == "pe" or not self.same_eng_sync):
                    return
                key = ("E", e2)
                sem, val = self.sem[e2], seq
            else:
                _, b, n = tok
                key = ("D", id(b))
                sem, val = b.dsem, 16 * n
            if key not in need or need[key][1] < val:
                need[key] = (sem, val)

        for b in reads:
            add(b.w)
        for b in writes:
            add(b.w)
            for t in b.r:
                add(t)
        out = []
        seen = self.seen[eng]
        for key, (sem, val) in need.items():
            if seen.get(key, 0) >= val:
                continue
            seen[key] = val
            out.append((sem, val))
        return out

    def op(self, eng, fn, reads=(), writes=()):
        reads = [t.buf for t in reads if t is not None and isinstance(t, Tl)]
        writes = [t.buf for t in writes if t is not None and isinstance(t, Tl)]
        for sem, val in self._need(eng, reads, writes):
            self.streams[eng].append(("w", sem, val))
            self.n_wait += 1
        self.count[eng] += 1
        seq = self.count[eng]
        self.streams[eng].append(("i", fn, self.sem[eng], 1))
        self.n_ins += 1
        tok = ("E", eng, seq)
        for b in reads:
            b.r.append(tok)
        for b in writes:
            b.w = tok
            b.r = []
        return tok

    def dma(self, q, out, in_, sem_tl):
        reads = [in_.buf]
        writes = [out.buf]
        sb = sem_tl.buf
        sem = self._dsem(sb)
        for s, val in self._need(q, reads, writes):
            self.streams[q].append(("w", s, val))
            self.n_wait += 1
        sb.dcount += 1
        oap, iap = out.ap, in_.ap
        self.streams[q].append(("i", lambda e: e.dma_start(out=oap, in_=iap), sem, 16))
        self.n_ins += 1
        tok = ("D", sb, sb.dcount)
        in_.buf.r.append(tok)
        out.buf.w = tok
        out.buf.r = []
        return tok

    def wait_all_dma(self, q, tls):
        for t in tls:
            b = t.buf
            if b.dsem is not None and b.dcount > 0:
                self.streams[q].append(("w", b.dsem, 16 * b.dcount))

    def mm(self, out, lhsT, rhs, start=True, stop=True):
        o, l, r = out.ap, lhsT.ap, rhs.ap
        return self.op("pe", lambda e: e.matmul(o, lhsT=l, rhs=r, start=start, stop=stop),
                       reads=[lhsT, rhs], writes=[out])

    def transpose(self, out, in_, ident):
        o, i, d = out.ap, in_.ap, ident.ap
        return self.op("pe", lambda e: e.transpose(o, i, d), reads=[in_, ident], writes=[out])

    def act(self, out, in_, func, bias=0.0, scale=1.0, accum=None, eng="act"):
        o, i = out.ap, in_.ap
        b = bias.ap if isinstance(bias, Tl) else bias
        s = scale.ap if isinstance(scale, Tl) else scale
        reads = [in_] + [x for x in (bias, scale) if isinstance(x, Tl)]
        writes = [out]
        if accum is not None:
            a = accum.ap
            writes.append(accum)
            return self.op(eng, lambda e: e.activation(out=o, in_=i, func=func, bias=b, scale=s, accum_out=a),
                           reads=reads, writes=writes)
        return self.op(eng, lambda e: e.activation(out=o, in_=i, func=func, bias=b, scale=s),
                       reads=reads, writes=writes)

    def tt(self, eng, out, a, b, op):
        o, x, y = out.ap, a.ap, b.ap
        return self.op(eng, lambda e: e.tensor_tensor(out=o, in0=x, in1=y, op=op), reads=[a, b], writes=[out])

    def ts(self, eng, out, a, s1, op0, s2=None, op1=None):
        o, x = out.ap, a.ap
        v1 = s1.ap if isinstance(s1, Tl) else s1
        v2 = s2.ap if isinstance(s2, Tl) else s2
        reads = [a] + [x_ for x_ in (s1, s2) if isinstance(x_, Tl)]
        if op1 is None:
            return self.op(eng, lambda e: e.tensor_scalar(out=o, in0=x, scalar1=v1, scalar2=None, op0=op0),
                           reads=reads, writes=[out])
        return self.op(eng, lambda e: e.tensor_scalar(out=o, in0=x, scalar1=v1, scalar2=v2, op0=op0, op1=op1),
                       reads=reads, writes=[out])

    def stt(self, eng, out, a, s, b, op0, op1):
        o, x, y = out.ap, a.ap, b.ap
        sv = s.ap if isinstance(s, Tl) else s
        reads = [a, b] + ([s] if isinstance(s, Tl) else [])
        return self.op(eng, lambda e: e.scalar_tensor_tensor(out=o, in0=x, scalar=sv, in1=y, op0=op0, op1=op1),
                       reads=reads, writes=[out])

    def copy(self, eng, out, in_):
        o, i = out.ap, in_.ap
        if eng == "act":
            return self.op(eng, lambda e: e.copy(out=o, in_=i), reads=[in_], writes=[out])
        return self.op(eng, lambda e: e.tensor_copy(out=o, in_=i), reads=[in_], writes=[out])

    def memset(self, eng, out, val):
        o = out.ap
        return self.op(eng, lambda e: e.memset(o, val), reads=[], writes=[out])

    def recip(self, out, in_):
        o, i = out.ap, in_.ap
        return self.op("dve", lambda e: e.reciprocal(out=o, in_=i), reads=[in_], writes=[out])

    def emit(self, final_waits=()):
        nc = self.nc
        streams = self.streams
        for e in self.ENGS:
            if e != "sp" and self.count[e] > 0:
                streams["sp"].append(("w", self.sem[e], self.count[e]))
        self.wait_all_dma("sp", final_waits)

        def run(eng_handle, lst):
            for it in lst:
                if it[0] == "w":
                    eng_handle.wait_ge(it[1], it[2])
                else:
                    _, fn, sem, inc = it
                    fn(eng_handle).then_inc(sem, inc)

        with nc.Block() as block:
            @block.tensor
            def _(e):
                run(e, streams["pe"])

            @block.scalar
            def _(e):
                run(e, streams["act"])

            @block.vector
            def _(e):
                run(e, streams["dve"])

            @block.gpsimd
            def _(e):
                run(e, streams["pool"])

            @block.sync
            def _(e):
                run(e, streams["sp"])
        self.es.close()


from concourse.bass_utils import run_bass_kernel_spmd

D = 1024
KC_D = 8
NL = 2
RW_COLS = 3360
OFF_HGRN = 3360
OFF_SSM = 7456
OFF_GATE = 12608
IN_COLS = 15680
FFN_H = 2816
EPS = 1e-5

WNAMES = ["w_in", "w_branch", "w_out", "w_ffn_in", "w_ffn_out", "rwkv_w_up", "rwkv_a_up", "rwkv_g_up"]
WSHAPES = {"w_in": (1024, IN_COLS), "w_branch": (4096, 1024), "w_out": (1024, 1024),
           "w_ffn_in": (1024, 2 * FFN_H), "w_ffn_out": (FFN_H, 1024),
           "rwkv_w_up": (64, 1024), "rwkv_a_up": (64, 1024), "rwkv_g_up": (160, 1024)}
VEC_SHAPES = {"norm_mix": (NL, 1024), "rwkv_mu": (NL, 3360), "rwkv_w0": (NL, 1024), "rwkv_a0": (NL, 1024),
              "rwkv_k_k": (NL, 1024), "rwkv_k_a": (NL, 1024), "rwkv_r_k": (NL, 16, 64),
              "rwkv_gn_w": (NL, 1024), "rwkv_gn_b": (NL, 1024), "hgrn_lb_logits": (NL, 1024),
              "hgrn_gn_w": (NL, 1024), "ssm_conv_w": (NL, 3072, 4), "ssm_conv_b": (NL, 3072),
              "ssm_dt_bias": (NL, 32), "ssm_a_log": (NL, 32), "ssm_d": (NL, 32), "ssm_gn_w": (NL, 2048),
              "norm_ffn": (NL, 1024), "norm_final": (1024,)}


class Model:
    def __init__(self, NT=1, T=512, layers=(0, 1), stub=("rwkv", "hgrn", "ssm"), dbg=None):
        self.NT, self.T, self.layers, self.stub = NT, T, tuple(layers), set(stub)
        self.dbg = dbg or []
        self.S = NT * T
        self.NF32 = 25
        self.NBF16 = 23
        nc = bass.Bass("TRN2", target_bir_lowering=False)
        self.nc = nc
        self.P = Prog(nc)
        self.build()

    def build(self):
        P, nc, T = self.P, self.nc, self.T
        S = self.S
        self.x_in = P.dram("x", [S, D], F32, kind="ExternalInput")
        self.out = P.dram("out", [S, D], F32, kind="ExternalOutput")
        self.win = {}
        for n in WNAMES:
            sh = WSHAPES[n]
            self.win[n] = P.dram(n, [NL, sh[0], sh[1]], F32, kind="ExternalInput")
        self.vin = {}
        for n, sh in VEC_SHAPES.items():
            self.vin[n] = P.dram(n, list(sh), F32, kind="ExternalInput")
        self.wbf = {}
        for l in self.layers:
            for n in WNAMES:
                sh = WSHAPES[n]
                self.wbf[(n, l)] = P.dram(f"{n}_bf{l}", [sh[0], sh[1]], BF16)
        self.dbg_out = {}

        self.NCH = T // 64
        self.NSLOT = 3
        self.SLOTE = 4096
        self.wslots = [P.sbuf(f"wslot{i}", [128, self.SLOTE], BF16) for i in range(self.NSLOT)]
        self.wi = 0
        self.psb = [P.psum(f"ps{i}", [128, 512], F32) for i in range(8)]
        self.pi = 0
        self.ident = P.sbuf("ident", [128, 128], F32)
        self.identb = P.sbuf("identb", [128, 128], BF16)
        self.ones = P.sbuf("ones", [128, 128], F32)
        self.epsc = P.sbuf("epsc", [128, 1], F32)
        self.xres = self.slabs("xres", 8, F32)
        self.hT = self.slabs("hT", 8, BF16)
        self.f32pool = self.slabs("f32pool", self.NF32, F32)
        self.bf16pool = self.slabs("bf16pool", self.NBF16, BF16)
        self.tokbuf = [P.sbuf(f"tokbuf{i}", [128, D], F32) for i in range(2)]
        self.tki = 0
        self.vecs = {}
        self.setup_consts()
        self.cast_weights()
        for ti in range(self.NT):
            self.load_x(ti)
            for l in self.layers:
                self.layer(l, ti)
            self.final(ti)
        P.emit(final_waits=self.tokbuf + getattr(self, 'dbg_tls', []))

    def slabs(self, name, n, dtype, width=None):
        P = self.P
        w = width or self.T
        t = P.es.enter_context(self.nc.sbuf_tensor(name, [128, n, w], dtype))
        return [Tl(t[:, i, :], Buf(f"{name}{i}")) for i in range(n)]

    def dump(self, name, tl):
        if not self.dbg or name in self.dbg_out:
            return
        P = self.P
        o = P.dram("dbg_" + name, list(tl.shape), F32, kind="ExternalOutput")
        P.dma("pool", o, tl, tl)
        self.dbg_out[name] = o
        self.dbg_tls = getattr(self, "dbg_tls", []) + [tl]

    def ps(self):
        t = self.psb[self.pi % 4]
        self.pi += 1
        return t

    def a32(self, n=None):
        r = self.f32pool.pop() if n is None else [self.f32pool.pop() for _ in range(n)]
        self.min32 = min(getattr(self, "min32", 999), len(self.f32pool))
        return r

    def a16(self, n=None):
        r = self.bf16pool.pop() if n is None else [self.bf16pool.pop() for _ in range(n)]
        self.min16 = min(getattr(self, "min16", 999), len(self.bf16pool))
        return r

    def f32free(self, *ts):
        for t in ts:
            self.f32pool.extend(t if isinstance(t, list) else [t])

    def f16free(self, *ts):
        for t in ts:
            self.bf16pool.extend(t if isinstance(t, list) else [t])

    def setup_consts(self):
        P = self.P
        nc = self.nc
        P.memset("pool", self.ones, 1.0)
        P.memset("pool", self.epsc, EPS)
        self.gnepsc = P.sbuf("gnepsc", [128, 1], F32)
        P.memset("pool", self.gnepsc, 64e-5)
        P.memset("pool", self.ident, 1.0)
        ia = self.ident.ap
        P.op("pool", lambda e: e.affine_select(out=ia, in_=ia, pattern=[[-1, 128]], compare_op=ALU.is_equal,
                                               fill=0.0, base=0, channel_multiplier=1),
             reads=[self.ident], writes=[self.ident])
        P.copy("dve", self.identb, self.ident)
        self.vstage = P.sbuf("vstage", [128, 512], F32)
        self.vcol = {}
        T = self.T
        self.rmask = P.sbuf("rmask", [128, T], F32)
        P.memset("pool", self.rmask, 1.0)
        P.memset("pool", self.rmask.v(lambda a: a.rearrange("p (c t) -> p c t", t=64)[:, :, 0:1]), 0.0)
        self.neg8 = P.sbuf("neg8", [64, 8, 64], BF16)
        P.memset("pool", self.neg8, 0.0)
        na = self.neg8.ap
        P.op("pool", lambda e: e.affine_select(out=na, in_=na, pattern=[[0, 8], [1, 64]], compare_op=ALU.is_ge,
                                               fill=-30000.0, base=0, channel_multiplier=-1),
             reads=[self.neg8], writes=[self.neg8])
        self.tri_incl = P.sbuf("tri_incl", [64, 64], F32)
        P.memset("pool", self.tri_incl, 1.0)
        ta = self.tri_incl.ap
        P.op("pool", lambda e: e.affine_select(out=ta, in_=ta, pattern=[[1, 64]], compare_op=ALU.is_ge,
                                               fill=0.0, base=0, channel_multiplier=-1),
             reads=[self.tri_incl], writes=[self.tri_incl])
        self.tri_strict = P.sbuf("tri_strict", [64, 64], F32)
        P.memset("pool", self.tri_strict, 1.0)
        tsa = self.tri_strict.ap
        P.op("pool", lambda e: e.affine_select(out=tsa, in_=tsa, pattern=[[1, 64]], compare_op=ALU.is_gt,
                                               fill=0.0, base=0, channel_multiplier=-1),
             reads=[self.tri_strict], writes=[self.tri_strict])
        self.sel = P.sbuf("sel", [32, 2048], F32)
        P.memset("pool", self.sel, 1.0)
        sa = self.sel.ap
        P.op("pool", lambda e: e.affine_select(out=sa, in_=sa, pattern=[[1, 2048]], compare_op=ALU.is_ge,
                                               fill=0.0, base=0, channel_multiplier=-64),
             reads=[self.sel], writes=[self.sel])
        P.op("pool", lambda e: e.affine_select(out=sa, in_=sa, pattern=[[-1, 2048]], compare_op=ALU.is_ge,
                                               fill=0.0, base=63, channel_multiplier=64),
             reads=[self.sel], writes=[self.sel])
        for l in self.layers:
            specs = [("norm_mix", "norm_mix", 8), ("norm_ffn", "norm_ffn", 8),
                     ("ssm_conv_b", "ssm_conv_b", 24), ("ssm_gn_w", "ssm_gn_w", 16),
                     ("hgrn_gn_w", "hgrn_gn_w", 8)]
            self.load_cols(l, f"vcA{l}", specs)
            self.setup_ssm(l)
            self.setup_hgrn(l)
            self.setup_rwkv(l)
        l0 = self.layers[0]
        self.load_cols(None, "vcF", [("norm_final", "norm_final", 8)])

    def load_cols(self, l, name, specs, srcs=None):
        P = self.P
        tot = sum(x[2] for x in specs)
        assert tot <= 128
        P.memset("dve", self.vstage[:, 0:128], 0.0)
        r = 0
        for si, (key, n, nr) in enumerate(specs):
            if srcs is not None:
                src = srcs[si]
            elif l is None:
                src = self.vin[n].v(lambda a: a.rearrange("(r c) -> r c", c=128))
            else:
                src = self.vin[n].v(lambda a: a[l].rearrange("(r c) -> r c", c=128))
            P.dma("sp", self.vstage[r:r + nr, 0:src.shape[1]], src, self.vstage)
            r += nr
        pt = self.ps()
        P.transpose(pt[:, 0:128], self.vstage[:, 0:128], self.ident)
        vc = P.sbuf(name, [128, tot], F32)
        P.copy("dve", vc, pt[:, 0:tot])
        r = 0
        for key, n, nr in specs:
            self.vcol[(key, l)] = vc[:, r:r + nr]
            r += nr

    def cast_weights(self):
        P = self.P
        for l in self.layers:
            for n in WNAMES:
                K = WSHAPES[n][0]
                dst = self.wbf[(n, l)]
                for r0 in range(0, K, 128):
                    nr = min(128, K - r0)
                    P.dma("pool", dst[r0:r0 + nr, :], self.win[n].v(lambda a: a[l, r0:r0 + nr, :]), dst)

    def load_w(self, wt, KC, c0, cw, r0=0, rows=None):
        P = self.P
        slot = self.wslots[self.wi % self.NSLOT]
        self.wi += 1
        assert KC * cw <= self.SLOTE, (KC, cw)
        view = slot.v(lambda a: a[:, :KC * cw].rearrange("p (k c) -> p k c", c=cw))
        if rows is None:
            src = wt.v(lambda a: a[r0:r0 + KC * 128, c0:c0 + cw].rearrange("(k p) n -> p k n", p=128))
            P.dma("sp", view, src, slot)
        else:
            assert KC == 1
            src = wt.v(lambda a: a[r0:r0 + rows, c0:c0 + cw])
            P.dma("sp", view.v(lambda a: a[0:rows, 0, :]), src, slot)
        return view

    def proj(self, wt, rhs, c0, ncols, consume, r0=0, cw=512, rows=None):
        P, T = self.P, self.T
        KC = len(rhs)
        cw = min(cw, (self.SLOTE // KC) // 128 * 128)
        j = 0
        for cb in range(0, ncols, cw):
            w = min(cw, ncols - cb)
            view = self.load_w(wt, KC, c0 + cb, w, r0=r0, rows=rows)
            for b in range(0, w, 128):
                nb = min(128, w - b)
                pt = self.ps()
                for k in range(KC):
                    P.mm(pt[0:nb, 0:T], view[0:rhs[k].shape[0], k, b:b + nb], rhs[k], start=(k == 0), stop=(k == KC - 1))
                consume(pt[0:nb, 0:T], j, nb)
                j += 1

    def load_x(self, ti):
        P, T = self.P, self.T
        for tb in range(T // 128):
            tk = self.tokbuf[self.tki % 2]
            self.tki += 1
            r0 = ti * T + tb * 128
            P.dma("sp", tk, self.x_in[r0:r0 + 128, :], tk)
            for c in range(8):
                pt = self.ps()
                P.transpose(pt[:, 0:128], tk[:, c * 128:(c + 1) * 128], self.ident)
                P.copy("dve" if c % 2 else "act", self.xres[c][:, tb * 128:(tb + 1) * 128], pt[:, 0:128])

    def rmsnorm(self, gain, dst, last=False):
        P, T = self.P, self.T
        pt = self.ps()
        tmpA = self.a32(2)
        for c in range(8):
            sq = tmpA[c % 2]
            P.act(sq, self.xres[c], AF.Square)
            P.mm(pt[:, 0:T], self.ones, sq, start=(c == 0), stop=(c == 7))
        rstd = tmpA[0]
        pt = pt[:, 0:T]
        P.act(rstd, pt, AF.Sqrt, bias=self.epsc, scale=1.0 / D)
        P.recip(rstd, rstd)
        for c in range(8):
            P.stt("dve", dst[c], self.xres[c], gain[:, c:c + 1], rstd, ALU.mult, ALU.mult)
        self.f32free(tmpA)

    def layer(self, l, ti):
        P, T = self.P, self.T
        w_in = self.wbf[("w_in", l)]
        wb = self.wbf[("w_branch", l)]
        self.rmsnorm(self.vcol[("norm_mix", l)], self.hT)
        self.merged = self.a32(8)
        self.gate = self.a32(2)
        branches = [("rwkv", 0, 1024, 0), ("hgrn", OFF_HGRN, 1024, 1024), ("ssm", OFF_SSM, 2048, 2048)]
        for bi, (name, off, width, brow) in enumerate(branches):
            nchunk = width // 128
            self.yT = self.a16(nchunk)
            if name in self.stub:
                def cons(pt, j, nb):
                    P.copy("act", self.yT[j], pt)
                self.proj(w_in, self.hT, off, width, cons)
            else:
                getattr(self, "mixer_" + name)(l, ti)
            for j in range(8):
                gt = self.gate[j % 2]

                def cons_g(pt, jj, nb, gt=gt):
                    P.act(gt, pt, AF.Sigmoid)
                self.proj(w_in, self.hT, OFF_GATE + bi * D + j * 128, 128, cons_g, cw=128)

                def cons_b(pt, jj, nb, gt=gt, j=j, bi=bi):
                    if bi == 0:
                        P.tt("dve", self.merged[j], pt, gt, ALU.mult)
                    else:
                        P.tt("dve", gt, pt, gt, ALU.mult)
                        P.tt("pool", self.merged[j], self.merged[j], gt, ALU.add)
                self.proj(wb, self.yT[:nchunk], j * 128, 128, cons_b, r0=brow, cw=128)
            self.f16free(self.yT)
        self.mergedb = self.a16(8)
        for j in range(8):
            P.copy("act", self.mergedb[j], self.merged[j])
        self.f32free(self.merged)
        def cons_o(pt, j, nb):
            P.tt("dve", self.xres[j], self.xres[j], pt, ALU.add)
        self.proj(self.wbf[("w_out", l)], self.mergedb, 0, D, cons_o)
        self.f16free(self.mergedb)
        self.ffn = self.a16(22)
        self.rmsnorm(self.vcol[("norm_ffn", l)], self.hT)
        wf = self.wbf[("w_ffn_in", l)]
        for j in range(22):
            sg = self.gate[j % 2]

            def cons_gate(pt, jj, nb, sg=sg):
                P.act(sg, pt, AF.Silu)

            def cons_up(pt, jj, nb, sg=sg, j=j):
                P.tt("dve", self.ffn[j], pt, sg, ALU.mult)
            self.proj(wf, self.hT, j * 128, 128, cons_gate, cw=128)
            self.proj(wf, self.hT, FFN_H + j * 128, 128, cons_up, cw=128)
        self.proj(self.wbf[("w_ffn_out", l)], self.ffn, 0, D, cons_o, cw=256)
        self.f16free(self.ffn)
        self.f32free(self.gate)

    def final(self, ti):
        P, T = self.P, self.T
        hf = self.a32(8)
        self.rmsnorm(self.vcol[("norm_final", None)], hf)
        for tb in range(T // 128):
            tk = self.tokbuf[self.tki % 2]
            self.tki += 1
            for c in range(8):
                pt = self.ps()
                P.transpose(pt[:, 0:128], hf[c][:, tb * 128:(tb + 1) * 128], self.ident)
                P.copy("dve" if c % 2 else "act", tk[:, c * 128:(c + 1) * 128], pt[:, 0:128])
            r0 = ti * T + tb * 128
            P.dma("sp", self.out[r0:r0 + 128, :], tk, tk)
        self.f32free(hf)


def make_in_map(inputs, b, S):
    m = {"x": np.ascontiguousarray(inputs["x"][b, :S])}
    for n in WNAMES:
        m[n] = np.ascontiguousarray(inputs[n])
    for n in VEC_SHAPES:
        m[n] = np.ascontiguousarray(inputs[n])
    return m


def bc(t, shape):
    return t.v(lambda a: a.broadcast_to(list(shape)))


def v3(t, inner=64):
    return t.v(lambda a: a.rearrange("p (c t) -> p c t", t=inner))


def setup_ssm(self, l):
    P = self.P
    NCH = self.NCH
    if not hasattr(self, "ssm"):
        self.ssm = {}
        T = self.T
        self.ext = [P.sbuf(f"ext{i}", [128, T + 3], F32) for i in range(2)]
        self.exti = 0
        self.rbd = [P.sbuf(f"rbd{i}", [32, 512], F32) for i in range(2)]
        self.e1 = [P.sbuf(f"e1_{i}", [64, 512], F32) for i in range(2)]
        self.cbs = [P.sbuf(f"cbs{i}", [64, 64], F32) for i in range(2)]
        self.Gb = P.sbuf("Gb", [64, NCH, 512], BF16)
        self.xtok = P.sbuf("xtok", [64, NCH, 512], BF16)
        self.xw = P.sbuf("xw", [64, NCH, 512], BF16)
        self.Btok = P.sbuf("Btok", [64, NCH, 128], BF16)
        self.Sb = [P.sbuf(f"Sb{i}", [128, 512], BF16) for i in range(2)]
        self.tokA = P.sbuf("tokA", [64, NCH * 32], F32)
        self.tokW = P.sbuf("tokW", [64, NCH * 32], F32)
        self.elast = P.sbuf("elast", [32, NCH], F32)
        self.rhs_e = P.sbuf("rhs_e", [32, NCH, 32], F32)
        self.elast_bc = P.sbuf("elast_bc", [128, NCH, 32], F32)
    d = {}
    P.dma("sp", self.vstage[0:24, 0:512],
          self.vin["ssm_conv_w"].v(lambda a: a[l].rearrange("(r c) j -> r (c j)", c=128)), self.vstage)
    cw = P.sbuf(f"convw{l}", [128, 4, 24], F32)
    for j in range(4):
        pt = self.ps()
        P.transpose(pt[:, 0:24], self.vstage.v(lambda a: a[0:24, j:512:4]), self.ident[0:24, 0:24])
        P.copy("dve", cw[:, j, :], pt[:, 0:24])
    d["cw"] = cw
    hp = P.sbuf(f"ssmh{l}", [32, 4], F32)
    for i, n in enumerate(["ssm_dt_bias", "ssm_a_log", "ssm_d"]):
        P.dma("sp", hp[:, i:i + 1], self.vin[n].v(lambda a: a[l].rearrange("(h o) -> h o", o=1)), hp)
    P.act(hp[:, 3:4], hp[:, 1:2], AF.Exp)
    P.ts("dve", hp[:, 3:4], hp[:, 3:4], -1.0, ALU.mult)
    d["hp"] = hp
    d2 = P.sbuf(f"ssmd2{l}", [32, 2], F32)
    P.copy("dve", d2[:, 0:1], hp[:, 2:3])
    P.copy("dve", d2[:, 1:2], hp[:, 2:3])
    pt = self.ps()
    for hpi in range(16):
        P.mm(pt[:, 2 * hpi:2 * hpi + 2], self.sel[:, 128 * hpi:128 * hpi + 128], d2)
    dcol = P.sbuf(f"dcol{l}", [128, 16], F32)
    P.copy("dve", dcol, pt.v(lambda a: a[:, 0:32].rearrange("p (h two) -> p h two", two=2)[:, :, 0]))
    d["dcol"] = dcol
    S = P.es.enter_context(self.nc.sbuf_tensor(f"ssmS{l}", [128, 4, 512], F32))
    d["S"] = [Tl(S[:, g, :], Buf(f"ssmS{l}_{g}")) for g in range(4)]
    for g in range(4):
        P.memset("pool", d["S"][g], 0.0)
    carry = P.sbuf(f"carry{l}", [128, 24, 3], F32)
    P.memset("pool", carry, 0.0)
    d["carry"] = carry
    self.ssm[l] = d


def mixer_ssm(self, l, ti):
    P, T, NCH = self.P, self.T, self.NCH
    d = self.ssm[l]
    w_in = self.wbf[("w_in", l)]
    c_z = OFF_SSM
    c_xbc = OFF_SSM + 2048
    c_dt = OFF_SSM + 2048 + 3072
    hpv, cw, cbv, dcol, carry = d["hp"], d["cw"], self.vcol[("ssm_conv_b", l)], d["dcol"], d["carry"]
    gnw = self.vcol[("ssm_gn_w", l)]
    dts = self.a32(5)
    raw, dtT, lndt, cum, wv = [t[0:32, :] for t in dts]

    def cons_dt(pt, j, nb):
        P.act(raw, pt, AF.Exp, bias=hpv[:, 0:1])
    self.proj(w_in, self.hT, c_dt, 32, cons_dt, cw=128)
    P.act(dtT, raw, AF.Ln, bias=self.ones[0:32, 0:1])
    P.act(lndt, dtT, AF.Ln)
    P.ts("dve", raw, dtT, hpv[:, 3:4], ALU.mult)
    ca, ra, rm = cum.ap, raw.ap, self.rmask[0:32, :].ap
    P.op("dve", lambda e: e.tensor_tensor_scan(out=ca, data0=rm, data1=ra, initial=0.0, op0=ALU.mult, op1=ALU.add),
         reads=[self.rmask, raw], writes=[cum])
    cs = lndt
    P.tt("dve", cs, cum, lndt, ALU.subtract)
    lastb = bc(v3(cum)[:, :, 63:64], [32, NCH, 64])
    P.tt("dve", v3(wv), lastb, v3(cs), ALU.subtract)
    P.act(wv, wv, AF.Exp)
    P.act(self.elast, v3(cum)[:, :, 63], AF.Exp)
    ptT = self.ps()
    for c in range(NCH):
        P.transpose(ptT[0:64, c * 32:(c + 1) * 32], cs[:, c * 64:(c + 1) * 64], self.ident[0:32, 0:32])
    P.copy("dve", self.tokA, ptT[0:64, 0:NCH * 32])
    ptT = self.ps()
    for c in range(NCH):
        P.transpose(ptT[0:64, c * 32:(c + 1) * 32], wv[:, c * 64:(c + 1) * 64], self.ident[0:32, 0:32])
    P.copy("dve", self.tokW, ptT[0:64, 0:NCH * 32])
    cs_tok = v3(self.tokA, 32)
    w_tok = v3(self.tokW, 32)
    P.tt("dve", self.rhs_e, bc(self.elast.v(lambda a: a.unsqueeze(2)), [32, NCH, 32]),
         bc(self.ident[0:32, 0:32].v(lambda a: a.unsqueeze(1)), [32, NCH, 32]), ALU.mult)
    pte = self.ps()
    P.mm(pte[:, 0:NCH * 32], self.ones[0:32, :], self.rhs_e.v(lambda a: a.rearrange("p c h -> p (c h)")))
    P.copy("dve", self.elast_bc.v(lambda a: a.rearrange("p c h -> p (c h)")), pte[:, 0:NCH * 32])

    yT = self.yT
    for g in range(4):
        xc = self.a32(4)
        xcb = self.a16(4)
        Bb, Cb = self.a16(2)

        def conv(pt, ci, dst32, dst16):
            ext = self.ext[self.exti % 2]
            self.exti += 1
            P.copy("pool", ext[:, 0:3], carry[:, ci, :])
            P.copy("act", ext[:, 3:3 + T], pt)
            P.copy("pool", carry[:, ci, :], ext[:, T:T + 3])
            acc = self.a32()
            P.ts("dve", acc, ext[:, 0:T], cw[:, 0, ci:ci + 1], ALU.mult, cbv[:, ci:ci + 1], ALU.add)
            for j in range(1, 4):
                P.stt("dve", acc, ext[:, j:j + T], cw[:, j, ci:ci + 1], acc, ALU.mult, ALU.add)
            if dst32 is not None:
                P.act(dst32, acc, AF.Silu)
                P.copy("pool", dst16, dst32)
            else:
                P.act(dst16, acc, AF.Silu)
            self.f32free(acc)

        self.proj(w_in, self.hT, c_xbc + 512 * g, 512, lambda pt, j, nb: conv(pt, 4 * g + j, xc[j], xcb[j]))
        self.proj(w_in, self.hT, c_xbc + 2048 + 128 * g, 128, lambda pt, j, nb: conv(pt, 16 + g, None, Bb), cw=128)
        self.proj(w_in, self.hT, c_xbc + 2560 + 128 * g, 128, lambda pt, j, nb: conv(pt, 20 + g, None, Cb), cw=128)
        for c in range(NCH):
            cs_ = slice(c * 64, (c + 1) * 64)
            rbd = self.rbd[c % 2]
            P.tt("dve", v3(rbd), bc(cum[:, cs_].v(lambda a: a.unsqueeze(1)), [32, 8, 64]),
                 bc(self.ident[0:32, 8 * g:8 * g + 8].v(lambda a: a.unsqueeze(2)), [32, 8, 64]), ALU.mult)
            pe = self.ps()
            P.mm(pe[0:64, :], self.ones[0:32, 0:64], rbd, start=True, stop=False)
            P.mm(pe[0:64, :], self.identb[0:64, 0:64], self.neg8.v(lambda a: a.rearrange("p h t -> p (h t)")),
                 start=False, stop=True)
            e1 = self.e1[c % 2]
            P.tt("dve", v3(e1), v3(pe[0:64, :]),
                 bc(cs_tok[:, c, 8 * g:8 * g + 8].v(lambda a: a.unsqueeze(2)), [64, 8, 64]), ALU.subtract)
            P.act(e1, e1, AF.Exp)
            pcb = self.ps()
            P.mm(pcb[0:64, 0:64], Bb[:, cs_], Cb[:, cs_])
            cbs = self.cbs[c % 2]
            P.copy("act", cbs, pcb[0:64, 0:64])
            P.tt("pool", v3(self.Gb[:, c, :]), v3(e1), bc(cbs.v(lambda a: a.unsqueeze(1)), [64, 8, 64]), ALU.mult)
            ptx = self.ps().v(lambda a: a.bitcast(BF16))
            for j in range(4):
                P.transpose(ptx[0:64, j * 128:(j + 1) * 128], xcb[j][:, cs_], self.identb)
            P.transpose(ptx[0:64, 512:640], Bb[:, cs_], self.identb)
            P.copy("act", self.xtok[:, c, :], ptx[0:64, 0:512])
            P.copy("act", self.Btok[:, c, :], ptx[0:64, 512:640])
            P.tt("pool", v3(self.xw[:, c, :]), v3(self.xtok[:, c, :]),
                 bc(w_tok[:, c, 8 * g:8 * g + 8].v(lambda a: a.unsqueeze(2)), [64, 8, 64]), ALU.mult)
        S = d["S"][g]
        inter = self.psb[4:8]
        for c in range(NCH):
            cs_ = slice(c * 64, (c + 1) * 64)
            Sb = self.Sb[c % 2]
            P.copy("act", Sb, S)
            for hp in range(4):
                P.mm(inter[hp][:, cs_], Sb[:, hp * 128:(hp + 1) * 128], Cb[:, cs_])
            pd = self.ps()
            P.mm(pd[:, 0:512], self.Btok[:, c, :], self.xw[:, c, :])
            P.tt("dve", v3(S), v3(S), bc(self.elast_bc[:, c, 8 * g:8 * g + 8].v(lambda a: a.unsqueeze(2)), [128, 8, 64]),
                 ALU.mult)
            P.tt("dve", S, S, pd[:, 0:512], ALU.add)
        ys = []
        pn = self.psb[4]
        for hp in range(4):
            hh = 4 * g + hp
            pe2 = self.ps()
            P.mm(pe2[:, 0:T], self.sel[:, 128 * hh:128 * hh + 128], cum)
            ecb = self.a32()
            P.act(ecb, pe2[:, 0:T], AF.Exp)
            tmp = self.a32()
            P.tt("dve", tmp, inter[hp][:, 0:T], ecb, ALU.mult)
            self.f32free(ecb)
            pin = self.ps()
            for c in range(NCH):
                for q in range(2):
                    hs = slice((2 * hp + q) * 64, (2 * hp + q) * 64 + 64)
                    P.mm(pin[64 * q:64 * q + 64, c * 64:(c + 1) * 64], self.xtok[:, c, hs], self.Gb[:, c, hs])
            P.stt("dve", tmp, xc[hp], dcol[:, hh:hh + 1], tmp, ALU.mult, ALU.add)
            P.tt("dve", tmp, tmp, pin[:, 0:T], ALU.add)
            zs = self.a32()
            self.proj(w_in, self.hT, c_z + 128 * hh, 128, lambda pt, j, nb: P.act(zs, pt, AF.Silu), cw=128)
            P.tt("pool", tmp, tmp, zs, ALU.mult)
            P.act(zs, tmp, AF.Square)
            P.mm(pn[:, 0:T], self.ones, zs, start=(hp == 0), stop=(hp == 3))
            self.f32free(zs)
            ys.append(tmp)
        rstd = self.a32()
        P.act(rstd, pn[:, 0:T], AF.Sqrt, bias=self.epsc, scale=1.0 / 512)
        P.recip(rstd, rstd)
        for hp in range(4):
            hh = 4 * g + hp
            P.stt("dve", yT[hh], ys[hp], gnw[:, hh:hh + 1], rstd, ALU.mult, ALU.mult)
        self.f32free(rstd, ys, xc)
        self.f16free(xcb, [Bb, Cb])
    self.f32free(dts)


Model.setup_ssm = setup_ssm
Model.mixer_ssm = mixer_ssm


def setup_hgrn(self, l):
    P = self.P
    NCH = self.NCH
    if not hasattr(self, "hg"):
        self.hg = {}
        self.load_cols(None, "vcLB", [("lb0", None, 8), ("lb1", None, 8)], srcs=[
            self.vin["hgrn_lb_logits"].v(lambda a: a[0].rearrange("(r c) -> r c", c=128)),
            self.vin["hgrn_lb_logits"].v(lambda a: a[1].rearrange("(r c) -> r c", c=128))])
        lb = P.sbuf("hg_lb", [128, 2, 8], F32)
        P.memset("dve", lb, 0.0)
        P.tt("dve", lb[:, 1, :], self.vcol[("lb1", None)], self.vcol[("lb0", None)], ALU.subtract)
        P.act(lb[:, 1, :], lb[:, 1, :], AF.Sigmoid)
        oml = P.sbuf("hg_oml", [128, 2, 8], F32)
        P.ts("dve", oml, lb, -1.0, ALU.mult, 1.0, ALU.add)
        self.hg_lb, self.hg_oml = lb, oml
        self.hg_vtok = P.sbuf("hg_vtok", [64, NCH, 128], BF16)
        self.hg_ktok = P.sbuf("hg_ktok", [64, NCH, 128], BF16)
        self.hg_scT = [P.sbuf(f"hg_scT{i}", [64, 64], BF16) for i in range(2)]
        self.hg_Sb = [P.sbuf(f"hg_Sb{i}", [128, 128], BF16) for i in range(2)]
        self.hg_cols = P.sbuf("hg_cols", [128, 5, NCH], F32)
    S = P.es.enter_context(self.nc.sbuf_tensor(f"hgS{l}", [128, 8, 128], F32))
    d = {"S": [Tl(S[:, h, :], Buf(f"hgS{l}_{h}")) for h in range(8)]}
    for h in range(8):
        P.memset("pool", d["S"][h], 0.0)
    self.hg[l] = d


def mixer_hgrn(self, l, ti):
    P, T, NCH = self.P, self.T, self.NCH
    d = self.hg[l]
    w_in = self.wbf[("w_in", l)]
    li = l
    gnw = self.vcol[("hgrn_gn_w", l)]
    for h in range(8):
        S = d["S"][h]
        f, cum, eq, ek, qs = self.a32(5)
        qb, kb, ib = self.a16(3)
        self.proj(w_in, self.hT, OFF_HGRN + 1024 + 128 * h, 128, lambda pt, j, nb: P.act(f, pt, AF.Sigmoid), cw=128)
        P.ts("dve", f, f, self.hg_oml[:, li, h:h + 1], ALU.mult, self.hg_lb[:, li, h:h + 1], ALU.add)
        if h == 0 and ti == 0:
            self.dump(f"hg_f{l}", f)
        P.act(eq, f, AF.Ln)
        if h == 0 and ti == 0:
            self.dump(f"hg_lnf{l}", eq)
        ca, la, rm = cum.ap, eq.ap, self.rmask.ap
        P.op("dve", lambda e, ca=ca, la=la, rm=rm: e.tensor_tensor_scan(out=ca, data0=rm, data1=la, initial=0.0,
                                                                     op0=ALU.mult, op1=ALU.add),
             reads=[self.rmask, eq], writes=[cum])
        if h == 0 and ti == 0:
            self.dump(f"hg_cumraw{l}", cum)
        P.ts("dve", f, f, -1.0, ALU.mult, 1.0, ALU.add)
        cols = self.hg_cols
        c3 = v3(cum)
        P.act(cols[:, 0, :], c3[:, :, 32], AF.Exp)
        P.act(cols[:, 1, :], c3[:, :, 63], AF.Exp)
        P.tt("dve", cols[:, 3, :], c3[:, :, 63], c3[:, :, 32], ALU.subtract)
        P.act(cols[:, 2, :], cols[:, 3, :], AF.Exp)
        P.copy("dve", cols[:, 4, :], c3[:, :, 32])
        P.tt("dve", c3, c3, bc(cols[:, 4, :].v(lambda a: a.unsqueeze(2)), [128, NCH, 64]), ALU.subtract)
        P.act(eq, cum, AF.Exp)
        P.act(ek, cum, AF.Exp, scale=-1.0)
        self.proj(w_in, self.hT, OFF_HGRN + 128 * h, 128, lambda pt, j, nb: P.act(qs, pt, AF.Silu), cw=128)
        P.tt("dve", qb, qs, eq, ALU.mult)
        P.tt("pool", kb, f, ek, ALU.mult)
        self.proj(w_in, self.hT, OFF_HGRN + 2048 + 128 * h, 128, lambda pt, j, nb: P.copy("act", ib, pt), cw=128)
        ptv = self.ps().v(lambda a: a.bitcast(BF16))
        for c in range(NCH):
            P.transpose(ptv[0:64, c * 128:(c + 1) * 128], ib[:, c * 64:(c + 1) * 64], self.identb)
        P.copy("act", self.hg_vtok.v(lambda a: a.rearrange("p c v -> p (c v)")), ptv[0:64, 0:NCH * 128])
        ptk = self.ps().v(lambda a: a.bitcast(BF16))
        for c in range(NCH):
            P.transpose(ptk[0:64, c * 128:(c + 1) * 128], kb[:, c * 64:(c + 1) * 64], self.identb)
        P.copy("dve", self.hg_ktok.v(lambda a: a.rearrange("p c v -> p (c v)")), ptk[0:64, 0:NCH * 128])
        po = self.psb[5]
        for c in range(NCH):
            cs_ = slice(c * 64, (c + 1) * 64)
            psc = self.ps()
            P.mm(psc[0:64, 0:64], kb[:, cs_], qb[:, cs_])
            scT = self.hg_scT[c % 2]
            P.tt("dve", scT, psc[0:64, 0:64], self.tri_incl, ALU.mult)
            Sb = self.hg_Sb[c % 2]
            P.ts("dve", Sb, S, cols[:, 0, c:c + 1], ALU.mult)
            P.mm(po[:, cs_], self.hg_vtok[:, c, :], scT, start=True, stop=False)
            P.mm(po[:, cs_], Sb, qb[:, cs_], start=False, stop=True)
            pd = self.ps()
            P.mm(pd[:, 0:128], self.hg_ktok[:, c, :], self.hg_vtok[:, c, :])
            P.ts("dve", S, S, cols[:, 1, c:c + 1], ALU.mult)
            P.stt("dve", S, pd[:, 0:128], cols[:, 2, c:c + 1], S, ALU.mult, ALU.add)
        o32 = qs
        P.copy("act", o32, po[:, 0:T])
        if h == 0 and ti == 0:
            self.dump(f"hg_o{l}", o32)
        P.act(eq, o32, AF.Square)
        pn = self.ps()
        P.mm(pn[:, 0:T], self.ones, eq)
        rstd = ek
        P.act(rstd, pn[:, 0:T], AF.Sqrt, bias=self.epsc, scale=1.0 / 128)
        P.recip(rstd, rstd)
        P.stt("dve", o32, o32, gnw[:, h:h + 1], rstd, ALU.mult, ALU.mult)
        self.proj(w_in, self.hT, OFF_HGRN + 3072 + 128 * h, 128, lambda pt, j, nb: P.act(f, pt, AF.Sigmoid), cw=128)
        P.tt("dve", self.yT[h], o32, f, ALU.mult)
        if h == 0 and ti == 0:
            self.dump(f"hg_y{l}", self.yT[h])
            self.dump(f"hg_cum{l}", cum)
            self.dump(f"hg_qb{l}", qb)
            self.dump(f"hg_kb{l}", kb)
            self.dump(f"hg_ib{l}", ib)
        self.f32free(f, cum, eq, ek, qs)
        self.f16free(qb, kb, ib)


Model.setup_hgrn = setup_hgrn
Model.mixer_hgrn = mixer_hgrn


C0 = float(np.exp(-0.5))


def setup_rwkv(self, l):
    P = self.P
    NCH, T = self.NCH, self.T
    if not hasattr(self, "rw"):
        self.rw = {}
        self.bones = P.sbuf("bones", [128, 128], F32)
        P.memset("pool", self.bones, 0.0)
        P.memset("pool", self.bones[0:64, 0:64], 1.0)
        P.memset("pool", self.bones[64:128, 64:128], 1.0)
        self.tri_ls = P.sbuf("tri_ls", [64, 64], F32)
        P.memset("pool", self.tri_ls, 1.0)
        ta = self.tri_ls.ap
        P.op("pool", lambda e: e.affine_select(out=ta, in_=ta, pattern=[[-1, 64]], compare_op=ALU.is_gt,
                                               fill=0.0, base=0, channel_multiplier=1),
             reads=[self.tri_ls], writes=[self.tri_ls])
        self.rext = [P.sbuf(f"rext{i}", [128, T + 1], F32) for i in range(2)]
        self.rexti = 0
        self.rw_Vtok = P.sbuf("rw_Vtok", [64, NCH, 128], BF16)
        self.rw_btok = P.sbuf("rw_btok", [64, NCH, 128], BF16)
        self.rw_ktok = P.sbuf("rw_ktok", [64, NCH, 128], BF16)
        self.rw_cols = P.sbuf("rw_cols", [128, 5, NCH], F32)
        self.rw_Sb = [P.sbuf(f"rw_Sb{i}", [128, 64], BF16) for i in range(2)]
        self.rw_lr = [P.sbuf(f"rw_lr{i}", [128, T], BF16) for i in range(4)]
        U = 2 * NCH
        self.rw_PU = [P.sbuf(f"rw_PU{i}", [64, 512], F32) for i in range(2)]
        self.rw_QU = [P.sbuf(f"rw_QU{i}", [64, 512], F32) for i in range(2)]
        self.rw_MTU = [P.sbuf(f"rw_MTU{i}", [64, 512], F32) for i in range(2)]
        self.rw_MTbU = P.sbuf("rw_MTbU", [64, U * 64], BF16)
        self.rw_A3 = P.sbuf("rw_A3", [64, 3, U * 64], BF16)
        self.rw_Xb2 = [P.sbuf(f"rw_Xb2{i}", [64, 128], BF16) for i in range(2)]
        self.rw_Ub2 = [P.sbuf(f"rw_Ub2{i}", [64, 128], BF16) for i in range(2)]
    d = {}
    specs = [("mu", None, 26), ("mu26", None, 1), ("muxa", None, 1)]
    srcs = [self.vin["rwkv_mu"].v(lambda a: a[l, 0:3328].rearrange("(r c) -> r c", c=128)),
            self.vin["rwkv_mu"].v(lambda a: a[l, 3328:3360].rearrange("(r c) -> r c", c=32)),
            self.vin["rwkv_mu"].v(lambda a: a[l, 3136:3200].rearrange("(r c) -> r c", c=64))]
    for n in ["w0", "a0", "k_k", "k_a", "gn_w", "gn_b"]:
        specs.append((n, None, 8))
        srcs.append(self.vin["rwkv_" + n].v(lambda a: a[l].rearrange("(r c) -> r c", c=128)))
    specs.append(("r_k", None, 8))
    srcs.append(self.vin["rwkv_r_k"].v(lambda a: a[l].rearrange("h (two c) -> (h two) c", two=1).rearrange("(r x) c -> r (x c)", x=2)))
    self.load_cols(("rw", l), f"vcR{l}", specs, srcs=srcs)
    for key, _, _ in specs:
        d[key] = self.vcol[(key, ("rw", l))]
    omk = P.sbuf(f"rw_omk{l}", [128, 8], F32)
    P.ts("dve", omk, d["k_a"], -1.0, ALU.mult, 1.0, ALU.add)
    d["omk"] = omk
    S = P.es.enter_context(self.nc.sbuf_tensor(f"rwS{l}", [128, 8, 64], F32))
    d["S"] = [Tl(S[:, p, :], Buf(f"rwS{l}_{p}")) for p in range(8)]
    for p in range(8):
        P.memset("pool", d["S"][p], 0.0)
    carry = P.sbuf(f"rw_carry{l}", [128, 28], F32)
    P.memset("pool", carry, 0.0)
    d["carry"] = carry
    self.rw[l] = d


def mixer_rwkv(self, l, ti):
    P, T, NCH = self.P, self.T, self.NCH
    d = self.rw[l]
    w_in = self.wbf[("w_in", l)]
    carry = d["carry"]

    def lerp(pt, np_, mucol, cidx, dst):
        ext = self.rext[self.rexti % 2]
        self.rexti += 1
        P.copy("pool", ext[0:np_, 0:1], carry[0:np_, cidx:cidx + 1])
        P.copy("act", ext[0:np_, 1:T + 1], pt)
        P.copy("pool", carry[0:np_, cidx:cidx + 1], ext[0:np_, T:T + 1])
        dd = self.a32()
        P.tt("dve", dd[0:np_, :], ext[0:np_, 0:T], ext[0:np_, 1:T + 1], ALU.subtract)
        P.stt("dve", dst, dd[0:np_, :], mucol, ext[0:np_, 1:T + 1], ALU.mult, ALU.add)
        self.f32free(dd)

    tmp = self.a32()
    txw, xab, sg0, sg1 = self.rw_lr
    self.proj(w_in, self.hT, 3072, 64, lambda pt, j, nb: lerp(pt, 64, d["mu"][0:64, 24:25], 24, tmp[0:64, :]), cw=128)
    P.act(txw[0:64, :], tmp[0:64, :], AF.Tanh)
    self.proj(w_in, self.hT, 3136, 64, lambda pt, j, nb: lerp(pt, 64, d["muxa"][0:64, 0:1], 27, tmp[0:64, :]), cw=128)
    P.copy("act", xab[0:64, :], tmp[0:64, :])
    self.proj(w_in, self.hT, 3200, 128, lambda pt, j, nb: lerp(pt, 128, d["mu"][:, 25:26], 25, tmp), cw=128)
    P.act(sg0, tmp, AF.Sigmoid)
    self.proj(w_in, self.hT, 3328, 32, lambda pt, j, nb: lerp(pt, 32, d["mu26"][0:32, 0:1], 26, tmp[0:32, :]), cw=128)
    P.act(sg1[0:32, :], tmp[0:32, :], AF.Sigmoid)
    self.f32free(tmp)
    cols = self.rw_cols
    for p in range(8):
        r, k, v, sg, Sc, a, kkn, g = self.a32(8)
        ab, rb, bb, kb, vb = self.a16(5)
        self.proj(w_in, self.hT, 128 * p, 128, lambda pt, j, nb: lerp(pt, 128, d["mu"][:, p:p + 1], p, r), cw=128)
        self.proj(w_in, self.hT, 1024 + 128 * p, 128, lambda pt, j, nb: lerp(pt, 128, d["mu"][:, 8 + p:9 + p], 8 + p, k), cw=128)
        self.proj(w_in, self.hT, 2048 + 128 * p, 128, lambda pt, j, nb: lerp(pt, 128, d["mu"][:, 16 + p:17 + p], 16 + p, v), cw=128)
        self.proj(self.wbf[("rwkv_w_up", l)], [txw[0:64, :]], 128 * p, 128,
                  lambda pt, j, nb: P.act(sg, pt, AF.Sigmoid, bias=d["w0"][:, p:p + 1]), cw=128, rows=64)
        self.proj(self.wbf[("rwkv_a_up", l)], [xab[0:64, :]], 128 * p, 128,
                  lambda pt, j, nb: P.act(a, pt, AF.Sigmoid, bias=d["a0"][:, p:p + 1]), cw=128, rows=64)
        vw0 = self.load_w(self.wbf[("rwkv_g_up", l)], 1, 128 * p, 128, r0=0, rows=128)
        vw1 = self.load_w(self.wbf[("rwkv_g_up", l)], 1, 128 * p, 128, r0=128, rows=32)
        pg = self.ps()
        P.mm(pg[:, 0:T], vw0[:, 0, :], sg0, start=True, stop=False)
        P.mm(pg[:, 0:T], vw1[0:32, 0, :], sg1[0:32, :], start=False, stop=True)
        P.copy("act", g, pg[:, 0:T])
        S_ = Sc
        sa, ga, rm = S_.ap, sg.ap, self.rmask.ap
        P.op("dve", lambda e, sa=sa, ga=ga, rm=rm: e.tensor_tensor_scan(out=sa, data0=rm, data1=ga, initial=0.0,
                                                                     op0=ALU.mult, op1=ALU.add),
             reads=[self.rmask, sg], writes=[S_])
        s3 = v3(S_)
        P.act(cols[:, 0, :], s3[:, :, 32], AF.Exp, scale=-C0)
        P.act(cols[:, 1, :], s3[:, :, 63], AF.Exp, scale=-C0)
        P.tt("dve", cols[:, 3, :], s3[:, :, 63], s3[:, :, 32], ALU.subtract)
        P.act(cols[:, 2, :], cols[:, 3, :], AF.Exp, scale=-C0)
        P.copy("dve", cols[:, 4, :], s3[:, :, 32])
        P.tt("dve", s3, s3, bc(cols[:, 4, :].v(lambda a_: a_.unsqueeze(2)), [128, NCH, 64]), ALU.subtract)
        e1, e2, t1 = self.a32(3)
        P.tt("dve", t1, Sc, sg, ALU.subtract)
        P.ts("dve", kkn, k, d["k_k"][:, p:p + 1], ALU.mult)
        P.act(e1, kkn, AF.Square)
        pn = self.ps()
        P.mm(pn[:, 0:T], self.bones, e1)
        P.act(e1, pn[:, 0:T], AF.Sqrt)
        P.ts("dve", e1, e1, 1e-12, ALU.max)
        P.recip(e1, e1)
        P.tt("dve", kkn, kkn, e1, ALU.mult)
        P.act(e2, t1, AF.Exp, scale=-C0)
        P.stt("dve", ab, kkn, -1.0, e2, ALU.mult, ALU.mult)
        P.act(e1, Sc, AF.Exp, scale=-C0)
        P.tt("pool", rb, r, e1, ALU.mult)
        P.act(e2, Sc, AF.Exp, scale=C0)
        P.tt("dve", t1, kkn, a, ALU.mult)
        P.tt("pool", bb, t1, e2, ALU.mult)
        P.ts("dve", t1, a, d["k_a"][:, p:p + 1], ALU.mult, d["omk"][:, p:p + 1], ALU.add)
        P.tt("dve", k, k, t1, ALU.mult)
        P.tt("pool", kb, k, e2, ALU.mult)
        P.copy("act", vb, v)
        P.stt("dve", t1, r, d["r_k"][:, p:p + 1], k, ALU.mult, ALU.mult)
        pbn = self.ps()
        P.mm(pbn[:, 0:T], self.bones, t1)
        bonus = r
        P.tt("dve", bonus, pbn[:, 0:T], v, ALU.mult)
        self.f32free(e1, e2, t1)
        for src, dstt in ((vb, self.rw_Vtok), (bb, self.rw_btok), (kb, self.rw_ktok)):
            ptt = self.ps().v(lambda a_: a_.bitcast(BF16))
            for c in range(NCH):
                P.transpose(ptt[0:64, c * 128:(c + 1) * 128], src[:, c * 64:(c + 1) * 64], self.identb)
            P.copy("act", dstt.v(lambda a_: a_.rearrange("p c v -> p (c v)")), ptt[0:64, 0:NCH * 128])
        S = d["S"][p]
        U = 2 * NCH
        A3, MTb = self.rw_A3, self.rw_MTbU
        def uidx(c, q):
            ug_, cc = divmod(c, 4)
            return ug_ * 8 + q * 4 + cc
        m_st = bc(self.tri_strict.v(lambda a_: a_.unsqueeze(1)), [64, 4, 64])
        m_ls = bc(self.tri_ls.v(lambda a_: a_.unsqueeze(1)), [64, 4, 64])
        m_in = bc(self.tri_incl.v(lambda a_: a_.unsqueeze(1)), [64, 4, 64])
        idb = bc(self.ident[0:64, 0:64].v(lambda a_: a_.unsqueeze(1)), [64, 8, 64])
        bk = self.psb
        for ug in range(U // 8):
            gsl = slice(ug * 512, (ug + 1) * 512)
            Pm, Qm, MT = self.rw_PU[0], self.rw_QU[0], self.rw_MTU[0]

            def batch(bank_pair, lhs, rhs):
                for q in range(2):
                    hs = slice(64 * q, 64 * q + 64)
                    for cc in range(4):
                        c = ug * 4 + cc
                        cs_ = slice(c * 64, (c + 1) * 64)
                        P.mm(bank_pair[q][0:64, cc * 64:(cc + 1) * 64], lhs[hs, cs_], rhs[hs, cs_])
            batch((bk[0], bk[1]), bb, ab)
            batch((bk[2], bk[3]), ab, bb)
            for q in range(2):
                qs_ = slice(q * 256, (q + 1) * 256)
                P.tt("dve", v3(Pm[:, qs_]), v3(bk[0 + q][0:64, 0:256]), m_st, ALU.mult)
                P.tt("dve", v3(Qm[:, qs_]), v3(bk[2 + q][0:64, 0:256]), m_ls, ALU.mult)
            P.tt("pool", v3(MT), v3(Pm), idb, ALU.add)
            batch((bk[0], bk[1]), kb, ab)
            batch((bk[2], bk[3]), bb, rb)
            batch((bk[6], bk[7]), kb, rb)
            for q in range(2):
                qs_ = slice(ug * 512 + q * 256, ug * 512 + (q + 1) * 256)
                P.tt("dve", v3(A3[:, 0, qs_]), v3(bk[0 + q][0:64, 0:256]), m_st, ALU.mult)
                P.tt("dve", v3(A3[:, 1, qs_]), v3(bk[2 + q][0:64, 0:256]), m_in, ALU.mult)
                P.tt("dve", v3(A3[:, 2, qs_]), v3(bk[6 + q][0:64, 0:256]), m_in, ALU.mult)
            b0, b1, b2 = bk[0], bk[1], bk[2]
            for it in range(1, 6):
                Pn, Qn, MTn = self.rw_PU[it % 2], self.rw_QU[it % 2], self.rw_MTU[it % 2]
                for ui in range(8):
                    us = slice(ui * 64, ui * 64 + 64)
                    if it < 5:
                        P.mm(b0[0:64, us], Qm[:, us], Pm[:, us])
                    P.mm(b1[0:64, us], Pm[:, us], Qm[:, us])
                if it < 5:
                    P.copy("act", Pn, b0[0:64, :])
                P.copy("dve", Qn, b1[0:64, :])
                for ui in range(8):
                    us = slice(ui * 64, ui * 64 + 64)
                    P.mm(b2[0:64, us], Qn[:, us], MT[:, us])
                P.tt("dve", MTn, MT, b2[0:64, :], ALU.add)
                Pm, Qm, MT = Pn, Qn, MTn
            P.copy("act", MTb[:, gsl], MT)
        pO = (bk[4], bk[6])
        for c in range(NCH):
            cs_ = slice(c * 64, (c + 1) * 64)
            Sb = self.rw_Sb[c % 2]
            P.ts("dve", Sb, S, cols[:, 0, c:c + 1], ALU.mult)
            pS = bk[5]
            pX = (bk[(c % 2) * 2], bk[(c % 2) * 2 + 1])
            pU = bk[7]
            Xb, Ub = self.rw_Xb2[c % 2], self.rw_Ub2[c % 2]
            for q in range(2):
                hs = slice(64 * q, 64 * q + 64)
                us = slice(uidx(c, q) * 64, uidx(c, q) * 64 + 64)
                Vt = self.rw_Vtok[:, c, hs]
                P.mm(pX[q][0:64, 0:64], ab[hs, cs_], Sb[hs, :], start=True, stop=False)
                P.mm(pX[q][0:64, 0:64], A3[:, 0, us], Vt, start=False, stop=True)
                P.copy("act", Xb[:, hs], pX[q][0:64, 0:64])
            for q in range(2):
                hs = slice(64 * q, 64 * q + 64)
                us = slice(uidx(c, q) * 64, uidx(c, q) * 64 + 64)
                P.mm(pU[0:64, hs], MTb[:, us], Xb[:, hs])
            P.copy("act", Ub, pU[0:64, 0:128])
            for q in range(2):
                hs = slice(64 * q, 64 * q + 64)
                Vt = self.rw_Vtok[:, c, hs]
                P.mm(pS[hs, 0:64], self.rw_btok[:, c, hs], Ub[:, hs], start=True, stop=False)
                P.mm(pS[hs, 0:64], self.rw_ktok[:, c, hs], Vt, start=False, stop=True)
            for q in range(2):
                hs = slice(64 * q, 64 * q + 64)
                us = slice(uidx(c, q) * 64, uidx(c, q) * 64 + 64)
                Vt = self.rw_Vtok[:, c, hs]
                P.mm(pO[q][hs, cs_], Sb[hs, :], rb[hs, cs_], start=True, stop=False)
                P.mm(pO[q][hs, cs_], Ub[:, hs], A3[:, 1, us], start=False, stop=False)
                P.mm(pO[q][hs, cs_], Vt, A3[:, 2, us], start=False, stop=True)
            P.ts("dve", S, S, cols[:, 1, c:c + 1], ALU.mult)
            P.stt("dve", S, pS[:, 0:64], cols[:, 2, c:c + 1], S, ALU.mult, ALU.add)
        o = k
        P.copy("act", o[0:64, :], pO[0][0:64, 0:T])
        P.copy("act", o[64:128, :], pO[1][64:128, 0:T])
        pm = self.ps()
        P.mm(pm[:, 0:T], self.bones, o)
        P.stt("dve", o, pm[:, 0:T], -1.0 / 64, o, ALU.mult, ALU.add)
        P.act(sg, o, AF.Square)
        pv = self.ps()
        P.mm(pv[:, 0:T], self.bones, sg)
        P.act(sg, pv[:, 0:T], AF.Sqrt, bias=self.gnepsc, scale=1.0 / 64)
        P.recip(sg, sg)
        P.stt("dve", o, o, d["gn_w"][:, p:p + 1], sg, ALU.mult, ALU.mult)
        P.stt("dve", o, o, d["gn_b"][:, p:p + 1], bonus, ALU.add, ALU.add)
        P.tt("dve", self.yT[p], o, g, ALU.mult)
        self.f32free(r, k, v, sg, Sc, a, kkn, g)
        self.f16free(ab, rb, bb, kb, vb)


Model.setup_rwkv = setup_rwkv
Model.mixer_rwkv = mixer_rwkv


def kernel(**inputs):
    inputs = {k_: np.asarray(v_) for k_, v_ in inputs.items()}
    S = 4096
    m = Model(NT=S // 256, T=256, layers=(0, 1), stub=())
    maps = [make_in_map(inputs, b, S) for b in range(8)]
    res = run_bass_kernel_spmd(m.nc, maps, core_ids=list(range(8)))
    out = np.stack([np.asarray(r["out"]) for r in res.results]).astype(np.float32)
    return out
```

```python
import numpy as np
from contextlib import ExitStack
import concourse.bass as bass
import concourse.mybir as mybir

F32 = mybir.dt.float32
BF16 = mybir.dt.bfloat16
AF = mybir.ActivationFunctionType
ALU = mybir.AluOpType
AX = mybir.AxisListType


class Buf:
    __slots__ = ("name", "w", "r", "dsem", "dcount")

    def __init__(self, name):
        self.name = name
        self.w = None
        self.r = []
        self.dsem = None
        self.dcount = 0


class Tl:
    __slots__ = ("ap", "buf")

    def __init__(self, ap, buf):
        self.ap = ap
        self.buf = buf

    def __getitem__(self, k):
        return Tl(self.ap[k], self.buf)

    def v(self, fn):
        return Tl(fn(self.ap), self.buf)

    @property
    def shape(self):
        return self.ap.shape


class Prog:
    ENGS = ("pe", "act", "dve", "pool", "sp")

    def __init__(self, nc, same_eng_sync=True):
        self.nc = nc
        self.es = ExitStack()
        self.streams = {e: [] for e in self.ENGS}
        self.count = {e: 0 for e in self.ENGS}
        self.seen = {e: {} for e in self.ENGS}
        self.sem = {}
        for e in self.ENGS:
            self.sem[e] = self.es.enter_context(nc.semaphore("sem_" + e))
        self.same_eng_sync = same_eng_sync
        self.nbuf = 0
        self.n_wait = 0
        self.n_ins = 0

    def sbuf(self, name, shape, dtype):
        t = self.es.enter_context(self.nc.sbuf_tensor(name, list(shape), dtype))
        return Tl(t[:], Buf(name))

    def psum(self, name, shape, dtype=F32):
        t = self.es.enter_context(self.nc.psum_tensor(name, list(shape), dtype))
        return Tl(t[:], Buf(name))

    def dram(self, name, shape, dtype, kind="Internal"):
        t = self.nc.dram_tensor(name, list(shape), dtype, kind=kind)
        return Tl(t.ap(), Buf(name))

    def newbuf(self, name="b"):
        self.nbuf += 1
        return Buf(f"{name}{self.nbuf}")

    def _dsem(self, buf):
        if buf.dsem is None:
            buf.dsem = self.es.enter_context(self.nc.semaphore("d_" + buf.name))
        return buf.dsem

    def _need(self, eng, reads, writes):
        need = {}

        def add(tok):
            if tok is None:
                return
            kind = tok[0]
            if kind == "E":
                _, e2, seq = tok
                if e2 == eng and (eng == "pe" or not self.same_eng_sync):
                    return
                key = ("E", e2)
                sem, val = self.sem[e2], seq
            else:
                _, b, n = tok
                key = ("D", id(b))
                sem, val = b.dsem, 16 * n
            if key not in need or need[key][1] < val:
                need[key] = (sem, val)

        for b in reads:
            add(b.w)
        for b in writes:
            add(b.w)
            for t in b.r:
                add(t)
        out = []
        seen = self.seen[eng]
        for key, (sem, val) in need.items():
            if seen.get(key, 0) >= val:
                continue
            seen[key] = val
            out.append((sem, val))
        return out

    def op(self, eng, fn, reads=(), writes=()):
        reads = [t.buf for t in reads if t is not None and isinstance(t, Tl)]
        writes = [t.buf for t in writes if t is not None and isinstance(t, Tl)]
        for sem, val in self._need(eng, reads, writes):
            self.streams[eng].append(("w", sem, val))
            self.n_wait += 1
        self.count[eng] += 1
        seq = self.count[eng]
        self.streams[eng].append(("i", fn, self.sem[eng], 1))
        self.n_ins += 1
        tok = ("E", eng, seq)
        for b in reads:
            b.r.append(tok)
        for b in writes:
            b.w = tok
            b.r = []
        return tok

    def dma(self, q, out, in_, sem_tl):
        reads = [in_.buf]
        writes = [out.buf]
        sb = sem_tl.buf
        sem = self._dsem(sb)
        for s, val in self._need(q, reads, writes):
            self.streams[q].append(("w", s, val))
            self.n_wait += 1
        sb.dcount += 1
        oap, iap = out.ap, in_.ap
        self.streams[q].append(("i", lambda e: e.dma_start(out=oap, in_=iap), sem, 16))
        self.n_ins += 1
        tok = ("D", sb, sb.dcount)
        in_.buf.r.append(tok)
        out.buf.w = tok
        out.buf.r = []
        return tok

    def wait_all_dma(self, q, tls):
        for t in tls:
            b = t.buf
            if b.dsem is not None and b.dcount > 0:
                self.streams[q].append(("w", b.dsem, 16 * b.dcount))

    def mm(self, out, lhsT, rhs, start=True, stop=True):
        o, l, r = out.ap, lhsT.ap, rhs.ap
        return self.op("pe", lambda e: e.matmul(o, lhsT=l, rhs=r, start=start, stop=stop),
                       reads=[lhsT, rhs], writes=[out])

    def transpose(self, out, in_, ident):
        o, i, d = out.ap, in_.ap, ident.ap
        return self.op("pe", lambda e: e.transpose(o, i, d), reads=[in_, ident], writes=[out])

    def act(self, out, in_, func, bias=0.0, scale=1.0, accum=None, eng="act"):
        o, i = out.ap, in_.ap
        b = bias.ap if isinstance(bias, Tl) else bias
        s = scale.ap if isinstance(scale, Tl) else scale
        reads = [in_] + [x for x in (bias, scale) if isinstance(x, Tl)]
        writes = [out]
        if accum is not None:
            a = accum.ap
            writes.append(accum)
            return self.op(eng, lambda e: e.activation(out=o, in_=i, func=func, bias=b, scale=s, accum_out=a),
                           reads=reads, writes=writes)
        return self.op(eng, lambda e: e.activation(out=o, in_=i, func=func, bias=b, scale=s),
                       reads=reads, writes=writes)

    def tt(self, eng, out, a, b, op):
        o, x, y = out.ap, a.ap, b.ap
        return self.op(eng, lambda e: e.tensor_tensor(out=o, in0=x, in1=y, op=op), reads=[a, b], writes=[out])

    def ts(self, eng, out, a, s1, op0, s2=None, op1=None):
        o, x = out.ap, a.ap
        v1 = s1.ap if isinstance(s1, Tl) else s1
        v2 = s2.ap if isinstance(s2, Tl) else s2
        reads = [a] + [x_ for x_ in (s1, s2) if isinstance(x_, Tl)]
        if op1 is None:
            return self.op(eng, lambda e: e.tensor_scalar(out=o, in0=x, scalar1=v1, scalar2=None, op0=op0),
                           reads=reads, writes=[out])
        return self.op(eng, lambda e: e.tensor_scalar(out=o, in0=x, scalar1=v1, scalar2=v2, op0=op0, op1=op1),
                       reads=reads, writes=[out])

    def stt(self, eng, out, a, s, b, op0, op1):
        o, x, y = out.ap, a.ap, b.ap
        sv = s.ap if isinstance(s, Tl) else s
        reads = [a, b] + ([s] if isinstance(s, Tl) else [])
        return self.op(eng, lambda e: e.scalar_tensor_tensor(out=o, in0=x, scalar=sv, in1=y, op0=op0, op1=op1),
                       reads=reads, writes=[out])

    def copy(self, eng, out, in_):
        o, i = out.ap, in_.ap
        if eng == "act":
            return self.op(eng, lambda e: e.copy(out=o, in_=i), reads=[in_], writes=[out])
        return self.op(eng, lambda e: e.tensor_copy(out=o, in_=i), reads=[in_], writes=[out])

    def memset(self, eng, out, val):
        o = out.ap
        return self.op(eng, lambda e: e.memset(o, val), reads=[], writes=[out])

    def recip(self, out, in_):
        o, i = out.ap, in_.ap
        return self.op("dve", lambda e: e.reciprocal(out=o, in_=i), reads=[in_], writes=[out])

    def emit(self, final_waits=()):
        nc = self.nc
        streams = self.streams
        for e in self.ENGS:
            if e != "sp" and self.count[e] > 0:
                streams["sp"].append(("w", self.sem[e], self.count[e]))
        self.wait_all_dma("sp", final_waits)

        def run(eng_handle, lst):
            for it in lst:
                if it[0] == "w":
                    eng_handle.wait_ge(it[1], it[2])
                else:
                    _, fn, sem, inc = it
                    fn(eng_handle).then_inc(sem, inc)

        with nc.Block() as block:
            @block.tensor
            def _(e):
                run(e, streams["pe"])

            @block.scalar
            def _(e):
                run(e, streams["act"])

            @block.vector
            def _(e):
                run(e, streams["dve"])

            @block.gpsimd
            def _(e):
                run(e, streams["pool"])

            @block.sync
            def _(e):
                run(e, streams["sp"])
        self.es.close()


from concourse.bass_utils import run_bass_kernel_spmd

D = 1024
KC_D = 8
NL = 2
RW_COLS = 3360
OFF_HGRN = 3360
OFF_SSM = 7456
OFF_GATE = 12608
IN_COLS = 15680
FFN_H = 2816
EPS = 1e-5
SAME_ENG_SYNC = True
CHAIN_DT = BF16

WNAMES = ["w_in", "w_branch", "w_out", "w_ffn_in", "w_ffn_out", "rwkv_w_up", "rwkv_a_up", "rwkv_g_up"]
WSHAPES = {"w_in": (1024, IN_COLS), "w_branch": (4096, 1024), "w_out": (1024, 1024),
           "w_ffn_in": (1024, 2 * FFN_H), "w_ffn_out": (FFN_H, 1024),
           "rwkv_w_up": (64, 1024), "rwkv_a_up": (64, 1024), "rwkv_g_up": (160, 1024)}
VEC_SHAPES = {"norm_mix": (NL, 1024), "rwkv_mu": (NL, 3360), "rwkv_w0": (NL, 1024), "rwkv_a0": (NL, 1024),
              "rwkv_k_k": (NL, 1024), "rwkv_k_a": (NL, 1024), "rwkv_r_k": (NL, 16, 64),
              "rwkv_gn_w": (NL, 1024), "rwkv_gn_b": (NL, 1024), "hgrn_lb_logits": (NL, 1024),
              "hgrn_gn_w": (NL, 1024), "ssm_conv_w": (NL, 3072, 4), "ssm_conv_b": (NL, 3072),
              "ssm_dt_bias": (NL, 32), "ssm_a_log": (NL, 32), "ssm_d": (NL, 32), "ssm_gn_w": (NL, 2048),
              "norm_ffn": (NL, 1024), "norm_final": (1024,)}


class Model:
    def __init__(self, NT=1, T=512, layers=(0, 1), stub=("rwkv", "hgrn", "ssm"), dbg=None):
        self.NT, self.T, self.layers, self.stub = NT, T, tuple(layers), set(stub)
        self.dbg = dbg or []
        self.S = NT * T
        self.NF32 = 27
        self.NBF16 = 23
        nc = bass.Bass("TRN2", target_bir_lowering=False)
        self.nc = nc
        self.P = Prog(nc, same_eng_sync=SAME_ENG_SYNC)
        self.build()

    def build(self):
        P, nc, T = self.P, self.nc, self.T
        S = self.S
        self.x_in = P.dram("x", [S, D], F32, kind="ExternalInput")
        self.out = P.dram("out", [S, D], F32, kind="ExternalOutput")
        self.win = {}
        for n in WNAMES:
            sh = WSHAPES[n]
            self.win[n] = P.dram(n, [NL, sh[0], sh[1]], F32, kind="ExternalInput")
        self.vin = {}
        for n, sh in VEC_SHAPES.items():
            self.vin[n] = P.dram(n, list(sh), F32, kind="ExternalInput")
        self.wbf = {}
        for l in self.layers:
            for n in WNAMES:
                sh = WSHAPES[n]
                self.wbf[(n, l)] = P.dram(f"{n}_bf{l}", [sh[0], sh[1]], BF16)
        self.dbg_out = {}

        self.NCH = T // 64
        self.NSLOT = 3
        self.SLOTE = 4096
        self.wslots = [P.sbuf(f"wslot{i}", [128, self.SLOTE], BF16) for i in range(self.NSLOT)]
        self.wi = 0
        self.psb = [P.psum(f"ps{i}", [128, 512], F32) for i in range(8)]
        self.pi = 0
        self.ident = P.sbuf("ident", [128, 128], F32)
        self.identb = P.sbuf("identb", [128, 128], BF16)
        self.ones = P.sbuf("ones", [128, 128], F32)
        self.epsc = P.sbuf("epsc", [128, 1], F32)
        self.xres = self.slabs("xres", 8, F32)
        self.hT = self.slabs("hT", 8, BF16)
        self.f32pool = self.slabs("f32pool", self.NF32, F32)
        self.bf16pool = self.slabs("bf16pool", self.NBF16, BF16)
        self.tokbuf = [P.sbuf(f"tokbuf{i}", [128, D], F32) for i in range(2)]
        self.tki = 0
        self.vecs = {}
        self.setup_consts()
        self.cast_weights()
        for ti in range(self.NT):
            self.load_x(ti)
            for l in self.layers:
                self.layer(l, ti)
            self.final(ti)
        P.emit(final_waits=self.tokbuf + getattr(self, 'dbg_tls', []))

    def slabs(self, name, n, dtype, width=None):
        P = self.P
        w = width or self.T
        t = P.es.enter_context(self.nc.sbuf_tensor(name, [128, n, w], dtype))
        return [Tl(t[:, i, :], Buf(f"{name}{i}")) for i in range(n)]

    def dump(self, name, tl):
        if not self.dbg or name in self.dbg_out:
            return
        P = self.P
        o = P.dram("dbg_" + name, list(tl.shape), F32, kind="ExternalOutput")
        P.dma("pool", o, tl, tl)
        self.dbg_out[name] = o
        self.dbg_tls = getattr(self, "dbg_tls", []) + [tl]

    def ps(self):
        t = self.psb[self.pi % 4]
        self.pi += 1
        return t

    def a32(self, n=None):
        r = self.f32pool.pop() if n is None else [self.f32pool.pop() for _ in range(n)]
        self.min32 = min(getattr(self, "min32", 999), len(self.f32pool))
        return r

    def a16(self, n=None):
        r = self.bf16pool.pop() if n is None else [self.bf16pool.pop() for _ in range(n)]
        self.min16 = min(getattr(self, "min16", 999), len(self.bf16pool))
        return r

    def f32free(self, *ts):
        for t in ts:
            self.f32pool.extend(t if isinstance(t, list) else [t])

    def f16free(self, *ts):
        for t in ts:
            self.bf16pool.extend(t if isinstance(t, list) else [t])

    def setup_consts(self):
        P = self.P
        nc = self.nc
        P.memset("pool", self.ones, 1.0)
        P.memset("pool", self.epsc, EPS)
        self.gnepsc = P.sbuf("gnepsc", [128, 1], F32)
        P.memset("pool", self.gnepsc, 64e-5)
        P.memset("pool", self.ident, 1.0)
        ia = self.ident.ap
        P.op("pool", lambda e: e.affine_select(out=ia, in_=ia, pattern=[[-1, 128]], compare_op=ALU.is_equal,
                                               fill=0.0, base=0, channel_multiplier=1),
             reads=[self.ident], writes=[self.ident])
        P.copy("dve", self.identb, self.ident)
        self.vstage = P.sbuf("vstage", [128, 512], F32)
        self.vcol = {}
        T = self.T
        self.rmask = P.sbuf("rmask", [128, T], F32)
        P.memset("pool", self.rmask, 1.0)
        P.memset("pool", self.rmask.v(lambda a: a.rearrange("p (c t) -> p c t", t=64)[:, :, 0:1]), 0.0)
        self.neg8 = P.sbuf("neg8", [64, 8, 64], BF16)
        P.memset("pool", self.neg8, 0.0)
        na = self.neg8.ap
        P.op("pool", lambda e: e.affine_select(out=na, in_=na, pattern=[[0, 8], [1, 64]], compare_op=ALU.is_ge,
                                               fill=-30000.0, base=0, channel_multiplier=-1),
             reads=[self.neg8], writes=[self.neg8])
        self.tri_incl = P.sbuf("tri_incl", [64, 64], F32)
        P.memset("pool", self.tri_incl, 1.0)
        ta = self.tri_incl.ap
        P.op("pool", lambda e: e.affine_select(out=ta, in_=ta, pattern=[[1, 64]], compare_op=ALU.is_ge,
                                               fill=0.0, base=0, channel_multiplier=-1),
             reads=[self.tri_incl], writes=[self.tri_incl])
        self.tri_strict = P.sbuf("tri_strict", [64, 64], F32)
        P.memset("pool", self.tri_strict, 1.0)
        tsa = self.tri_strict.ap
        P.op("pool", lambda e: e.affine_select(out=tsa, in_=tsa, pattern=[[1, 64]], compare_op=ALU.is_gt,
                                               fill=0.0, base=0, channel_multiplier=-1),
             reads=[self.tri_strict], writes=[self.tri_strict])
        self.sel = P.sbuf("sel", [32, 2048], F32)
        P.memset("pool", self.sel, 1.0)
        sa = self.sel.ap
        P.op("pool", lambda e: e.affine_select(out=sa, in_=sa, pattern=[[1, 2048]], compare_op=ALU.is_ge,
                                               fill=0.0, base=0, channel_multiplier=-64),
             reads=[self.sel], writes=[self.sel])
        P.op("pool", lambda e: e.affine_select(out=sa, in_=sa, pattern=[[-1, 2048]], compare_op=ALU.is_ge,
                                               fill=0.0, base=63, channel_multiplier=64),
             reads=[self.sel], writes=[self.sel])
        for l in self.layers:
            specs = [("norm_mix", "norm_mix", 8), ("norm_ffn", "norm_ffn", 8),
                     ("ssm_conv_b", "ssm_conv_b", 24), ("ssm_gn_w", "ssm_gn_w", 16),
                     ("hgrn_gn_w", "hgrn_gn_w", 8)]
            self.load_cols(l, f"vcA{l}", specs)
            self.setup_ssm(l)
            self.setup_hgrn(l)
            self.setup_rwkv(l)
        l0 = self.layers[0]
        self.load_cols(None, "vcF", [("norm_final", "norm_final", 8)])

    def load_cols(self, l, name, specs, srcs=None):
        P = self.P
        tot = sum(x[2] for x in specs)
        assert tot <= 128
        P.memset("dve", self.vstage[:, 0:128], 0.0)
        r = 0
        for si, (key, n, nr) in enumerate(specs):
            if srcs is not None:
                src = srcs[si]
            elif l is None:
                src = self.vin[n].v(lambda a: a.rearrange("(r c) -> r c", c=128))
            else:
                src = self.vin[n].v(lambda a: a[l].rearrange("(r c) -> r c", c=128))
            P.dma("sp", self.vstage[r:r + nr, 0:src.shape[1]], src, self.vstage)
            r += nr
        pt = self.ps()
        P.transpose(pt[:, 0:128], self.vstage[:, 0:128], self.ident)
        vc = P.sbuf(name, [128, tot], F32)
        P.copy("dve", vc, pt[:, 0:tot])
        r = 0
        for key, n, nr in specs:
            self.vcol[(key, l)] = vc[:, r:r + nr]
            r += nr

    def cast_weights(self):
        P = self.P
        for l in self.layers:
            for n in WNAMES:
                K = WSHAPES[n][0]
                dst = self.wbf[(n, l)]
                for r0 in range(0, K, 128):
                    nr = min(128, K - r0)
                    P.dma("pool", dst[r0:r0 + nr, :], self.win[n].v(lambda a: a[l, r0:r0 + nr, :]), dst)

    def load_w(self, wt, KC, c0, cw, r0=0, rows=None):
        P = self.P
        slot = self.wslots[self.wi % self.NSLOT]
        self.wi += 1
        assert KC * cw <= self.SLOTE, (KC, cw)
        view = slot.v(lambda a: a[:, :KC * cw].rearrange("p (k c) -> p k c", c=cw))
        if rows is None:
            src = wt.v(lambda a: a[r0:r0 + KC * 128, c0:c0 + cw].rearrange("(k p) n -> p k n", p=128))
            P.dma("sp", view, src, slot)
        else:
            assert KC == 1
            src = wt.v(lambda a: a[r0:r0 + rows, c0:c0 + cw])
            P.dma("sp", view.v(lambda a: a[0:rows, 0, :]), src, slot)
        return view

    def proj(self, wt, rhs, c0, ncols, consume, r0=0, cw=512, rows=None):
        P, T = self.P, self.T
        KC = len(rhs)
        cw = min(cw, (self.SLOTE // KC) // 128 * 128)
        j = 0
        for cb in range(0, ncols, cw):
            w = min(cw, ncols - cb)
            view = self.load_w(wt, KC, c0 + cb, w, r0=r0, rows=rows)
            for b in range(0, w, 128):
                nb = min(128, w - b)
                pt = self.ps()
                for k in range(KC):
                    P.mm(pt[0:nb, 0:T], view[0:rhs[k].shape[0], k, b:b + nb], rhs[k], start=(k == 0), stop=(k == KC - 1))
                consume(pt[0:nb, 0:T], j, nb)
                j += 1

    def load_x(self, ti):
        P, T = self.P, self.T
        for tb in range(T // 128):
            tk = self.tokbuf[self.tki % 2]
            self.tki += 1
            r0 = ti * T + tb * 128
            P.dma("sp", tk, self.x_in[r0:r0 + 128, :], tk)
            for c in range(8):
                pt = self.ps()
                P.transpose(pt[:, 0:128], tk[:, c * 128:(c + 1) * 128], self.ident)
                P.copy("dve" if c % 2 else "act", self.xres[c][:, tb * 128:(tb + 1) * 128], pt[:, 0:128])

    def rmsnorm(self, gain, dst, last=False):
        P, T = self.P, self.T
        pt = self.ps()
        tmpA = self.a32(2)
        for c in range(8):
            sq = tmpA[c % 2]
            P.act(sq, self.xres[c], AF.Square)
            P.mm(pt[:, 0:T], self.ones, sq, start=(c == 0), stop=(c == 7))
        rstd = tmpA[0]
        pt = pt[:, 0:T]
        P.act(rstd, pt, AF.Sqrt, bias=self.epsc, scale=1.0 / D)
        P.recip(rstd, rstd)
        for c in range(8):
            P.stt("dve", dst[c], self.xres[c], gain[:, c:c + 1], rstd, ALU.mult, ALU.mult)
        self.f32free(tmpA)

    def layer(self, l, ti):
        P, T = self.P, self.T
        w_in = self.wbf[("w_in", l)]
        wb = self.wbf[("w_branch", l)]
        self.rmsnorm(self.vcol[("norm_mix", l)], self.hT)
        self.merged = self.a32(8)
        self.gate = self.a32(4)
        branches = [("rwkv", 0, 1024, 0), ("hgrn", OFF_HGRN, 1024, 1024), ("ssm", OFF_SSM, 2048, 2048)]
        for bi, (name, off, width, brow) in enumerate(branches):
            nchunk = width // 128
            self.yT = self.a16(nchunk)
            if name in self.stub:
                def cons(pt, j, nb):
                    P.copy("act", self.yT[j], pt)
                self.proj(w_in, self.hT, off, width, cons)
            else:
                getattr(self, "mixer_" + name)(l, ti)
            for jj in range(0, 8, 4):
                gts = self.gate

                def cons_g(pt, j, nb, gts=gts):
                    P.act(gts[j], pt, AF.Sigmoid)
                self.proj(w_in, self.hT, OFF_GATE + bi * D + jj * 128, 512, cons_g, cw=512)

                def cons_b(pt, j, nb, gts=gts, jj=jj, bi=bi):
                    if bi == 0:
                        P.tt("dve", self.merged[jj + j], pt, gts[j], ALU.mult)
                    else:
                        P.tt("dve", gts[j], pt, gts[j], ALU.mult)
                        P.tt("pool", self.merged[jj + j], self.merged[jj + j], gts[j], ALU.add)
                self.proj(wb, self.yT[:nchunk], jj * 128, 512, cons_b, r0=brow, cw=512)
            self.f16free(self.yT)
        self.mergedb = self.a16(8)
        for j in range(8):
            P.copy("act", self.mergedb[j], self.merged[j])
        self.f32free(self.merged)
        def cons_o(pt, j, nb):
            P.tt("dve", self.xres[j], self.xres[j], pt, ALU.add)
        self.proj(self.wbf[("w_out", l)], self.mergedb, 0, D, cons_o)
        self.f16free(self.mergedb)
        self.ffn = self.a16(22)
        self.rmsnorm(self.vcol[("norm_ffn", l)], self.hT)
        wf = self.wbf[("w_ffn_in", l)]
        for jj in range(0, 22, 4):
            nbk = min(4, 22 - jj)
            sgs = self.gate

            def cons_gate(pt, j, nb, sgs=sgs):
                P.act(sgs[j], pt, AF.Silu)

            def cons_up(pt, j, nb, sgs=sgs, jj=jj):
                P.tt("dve", self.ffn[jj + j], pt, sgs[j], ALU.mult)
            self.proj(wf, self.hT, jj * 128, nbk * 128, cons_gate, cw=512)
            self.proj(wf, self.hT, FFN_H + jj * 128, nbk * 128, cons_up, cw=512)
        self.proj(self.wbf[("w_ffn_out", l)], self.ffn, 0, D, cons_o, cw=256)
        self.f16free(self.ffn)
        self.f32free(self.gate)

    def final(self, ti):
        P, T = self.P, self.T
        hf = self.a32(8)
        self.rmsnorm(self.vcol[("norm_final", None)], hf)
        for tb in range(T // 128):
            tk = self.tokbuf[self.tki % 2]
            self.tki += 1
            for c in range(8):
                pt = self.ps()
                P.transpose(pt[:, 0:128], hf[c][:, tb * 128:(tb + 1) * 128], self.ident)
                P.copy("dve" if c % 2 else "act", tk[:, c * 128:(c + 1) * 128], pt[:, 0:128])
            r0 = ti * T + tb * 128
            P.dma("sp", self.out[r0:r0 + 128, :], tk, tk)
        self.f32free(hf)


def make_in_map(inputs, b, S):
    m = {"x": np.ascontiguousarray(inputs["x"][b, :S])}
    for n in WNAMES:
        m[n] = np.ascontiguousarray(inputs[n])
    for n in VEC_SHAPES:
        m[n] = np.ascontiguousarray(inputs[n])
    return m


def bc(t, shape):
    return t.v(lambda a: a.broadcast_to(list(shape)))


def v3(t, inner=64):
    return t.v(lambda a: a.rearrange("p (c t) -> p c t", t=inner))


def setup_ssm(self, l):
    P = self.P
    NCH = self.NCH
    if not hasattr(self, "ssm"):
        self.ssm = {}
        T = self.T
        self.ext = [P.sbuf(f"ext{i}", [128, T + 3], F32) for i in range(2)]
        self.exti = 0
        self.rbd = [P.sbuf(f"rbd{i}", [32, 512], F32) for i in range(2)]
        self.e1 = [P.sbuf(f"e1_{i}", [64, 512], F32) for i in range(2)]
        self.cbs = [P.sbuf(f"cbs{i}", [64, 64], F32) for i in range(2)]
        self.Gb = P.sbuf("Gb", [64, NCH, 512], BF16)
        self.xtok = P.sbuf("xtok", [64, NCH, 512], BF16)
        self.xw = P.sbuf("xw", [64, NCH, 512], BF16)
        self.Btok = P.sbuf("Btok", [64, NCH, 128], BF16)
        self.Sb = [P.sbuf(f"Sb{i}", [128, 512], BF16) for i in range(2)]
        self.tokA = P.sbuf("tokA", [64, NCH * 32], F32)
        self.tokW = P.sbuf("tokW", [64, NCH * 32], F32)
        self.elast = P.sbuf("elast", [32, NCH], F32)
        self.rhs_e = P.sbuf("rhs_e", [32, NCH, 32], F32)
        self.elast_bc = P.sbuf("elast_bc", [128, NCH, 32], F32)
    d = {}
    P.dma("sp", self.vstage[0:24, 0:512],
          self.vin["ssm_conv_w"].v(lambda a: a[l].rearrange("(r c) j -> r (c j)", c=128)), self.vstage)
    cw = P.sbuf(f"convw{l}", [128, 4, 24], F32)
    for j in range(4):
        pt = self.ps()
        P.transpose(pt[:, 0:24], self.vstage.v(lambda a: a[0:24, j:512:4]), self.ident[0:24, 0:24])
        P.copy("dve", cw[:, j, :], pt[:, 0:24])
    d["cw"] = cw
    hp = P.sbuf(f"ssmh{l}", [32, 4], F32)
    for i, n in enumerate(["ssm_dt_bias", "ssm_a_log", "ssm_d"]):
        P.dma("sp", hp[:, i:i + 1], self.vin[n].v(lambda a: a[l].rearrange("(h o) -> h o", o=1)), hp)
    P.act(hp[:, 3:4], hp[:, 1:2], AF.Exp)
    P.ts("dve", hp[:, 3:4], hp[:, 3:4], -1.0, ALU.mult)
    d["hp"] = hp
    d2 = P.sbuf(f"ssmd2{l}", [32, 2], F32)
    P.copy("dve", d2[:, 0:1], hp[:, 2:3])
    P.copy("dve", d2[:, 1:2], hp[:, 2:3])
    pt = self.ps()
    for hpi in range(16):
        P.mm(pt[:, 2 * hpi:2 * hpi + 2], self.sel[:, 128 * hpi:128 * hpi + 128], d2)
    dcol = P.sbuf(f"dcol{l}", [128, 16], F32)
    P.copy("dve", dcol, pt.v(lambda a: a[:, 0:32].rearrange("p (h two) -> p h two", two=2)[:, :, 0]))
    d["dcol"] = dcol
    S = P.es.enter_context(self.nc.sbuf_tensor(f"ssmS{l}", [128, 4, 512], F32))
    d["S"] = [Tl(S[:, g, :], Buf(f"ssmS{l}_{g}")) for g in range(4)]
    for g in range(4):
        P.memset("pool", d["S"][g], 0.0)
    carry = P.sbuf(f"carry{l}", [128, 24, 3], F32)
    P.memset("pool", carry, 0.0)
    d["carry"] = carry
    self.ssm[l] = d


def mixer_ssm(self, l, ti):
    P, T, NCH = self.P, self.T, self.NCH
    d = self.ssm[l]
    w_in = self.wbf[("w_in", l)]
    c_z = OFF_SSM
    c_xbc = OFF_SSM + 2048
    c_dt = OFF_SSM + 2048 + 3072
    hpv, cw, cbv, dcol, carry = d["hp"], d["cw"], self.vcol[("ssm_conv_b", l)], d["dcol"], d["carry"]
    gnw = self.vcol[("ssm_gn_w", l)]
    dts = self.a32(5)
    raw, dtT, lndt, cum, wv = [t[0:32, :] for t in dts]

    def cons_dt(pt, j, nb):
        P.act(raw, pt, AF.Exp, bias=hpv[:, 0:1])
    self.proj(w_in, self.hT, c_dt, 32, cons_dt, cw=128)
    P.act(dtT, raw, AF.Ln, bias=self.ones[0:32, 0:1])
    P.act(lndt, dtT, AF.Ln)
    P.ts("dve", raw, dtT, hpv[:, 3:4], ALU.mult)
    ca, ra, rm = cum.ap, raw.ap, self.rmask[0:32, :].ap
    P.op("dve", lambda e: e.tensor_tensor_scan(out=ca, data0=rm, data1=ra, initial=0.0, op0=ALU.mult, op1=ALU.add),
         reads=[self.rmask, raw], writes=[cum])
    cs = lndt
    P.tt("dve", cs, cum, lndt, ALU.subtract)
    lastb = bc(v3(cum)[:, :, 63:64], [32, NCH, 64])
    P.tt("dve", v3(wv), lastb, v3(cs), ALU.subtract)
    P.act(wv, wv, AF.Exp)
    P.act(self.elast, v3(cum)[:, :, 63], AF.Exp)
    ptT = self.ps()
    for c in range(NCH):
        P.transpose(ptT[0:64, c * 32:(c + 1) * 32], cs[:, c * 64:(c + 1) * 64], self.ident[0:32, 0:32])
    P.copy("dve", self.tokA, ptT[0:64, 0:NCH * 32])
    ptT = self.ps()
    for c in range(NCH):
        P.transpose(ptT[0:64, c * 32:(c + 1) * 32], wv[:, c * 64:(c + 1) * 64], self.ident[0:32, 0:32])
    P.copy("dve", self.tokW, ptT[0:64, 0:NCH * 32])
    cs_tok = v3(self.tokA, 32)
    w_tok = v3(self.tokW, 32)
    P.tt("dve", self.rhs_e, bc(self.elast.v(lambda a: a.unsqueeze(2)), [32, NCH, 32]),
         bc(self.ident[0:32, 0:32].v(lambda a: a.unsqueeze(1)), [32, NCH, 32]), ALU.mult)
    pte = self.ps()
    P.mm(pte[:, 0:NCH * 32], self.ones[0:32, :], self.rhs_e.v(lambda a: a.rearrange("p c h -> p (c h)")))
    P.copy("dve", self.elast_bc.v(lambda a: a.rearrange("p c h -> p (c h)")), pte[:, 0:NCH * 32])

    yT = self.yT
    for g in range(4):
        xc = self.a32(4)
        xcb = self.a16(4)
        Bb, Cb = self.a16(2)

        def conv(pt, ci, dst32, dst16):
            ext = self.ext[self.exti % 2]
            self.exti += 1
            P.copy("pool", ext[:, 0:3], carry[:, ci, :])
            P.copy("act", ext[:, 3:3 + T], pt)
            P.copy("pool", carry[:, ci, :], ext[:, T:T + 3])
            acc = self.a32()
            P.ts("dve", acc, ext[:, 0:T], cw[:, 0, ci:ci + 1], ALU.mult, cbv[:, ci:ci + 1], ALU.add)
            for j in range(1, 4):
                P.stt("dve", acc, ext[:, j:j + T], cw[:, j, ci:ci + 1], acc, ALU.mult, ALU.add)
            if dst32 is not None:
                P.act(dst32, acc, AF.Silu)
                P.copy("pool", dst16, dst32)
            else:
                P.act(dst16, acc, AF.Silu)
            self.f32free(acc)

        self.proj(w_in, self.hT, c_xbc + 512 * g, 512, lambda pt, j, nb: conv(pt, 4 * g + j, xc[j], xcb[j]))
        self.proj(w_in, self.hT, c_xbc + 2048 + 128 * g, 128, lambda pt, j, nb: conv(pt, 16 + g, None, Bb), cw=128)
        self.proj(w_in, self.hT, c_xbc + 2560 + 128 * g, 128, lambda pt, j, nb: conv(pt, 20 + g, None, Cb), cw=128)
        for c in range(NCH):
            cs_ = slice(c * 64, (c + 1) * 64)
            rbd = self.rbd[c % 2]
            P.tt("dve", v3(rbd), bc(cum[:, cs_].v(lambda a: a.unsqueeze(1)), [32, 8, 64]),
                 bc(self.ident[0:32, 8 * g:8 * g + 8].v(lambda a: a.unsqueeze(2)), [32, 8, 64]), ALU.mult)
            pe = self.ps()
            P.mm(pe[0:64, :], self.ones[0:32, 0:64], rbd, start=True, stop=False)
            P.mm(pe[0:64, :], self.identb[0:64, 0:64], self.neg8.v(lambda a: a.rearrange("p h t -> p (h t)")),
                 start=False, stop=True)
            e1 = self.e1[c % 2]
            P.tt("dve", v3(e1), v3(pe[0:64, :]),
                 bc(cs_tok[:, c, 8 * g:8 * g + 8].v(lambda a: a.unsqueeze(2)), [64, 8, 64]), ALU.subtract)
            P.act(e1, e1, AF.Exp)
            pcb = self.ps()
            P.mm(pcb[0:64, 0:64], Bb[:, cs_], Cb[:, cs_])
            cbs = self.cbs[c % 2]
            P.copy("act", cbs, pcb[0:64, 0:64])
            P.tt("pool", v3(self.Gb[:, c, :]), v3(e1), bc(cbs.v(lambda a: a.unsqueeze(1)), [64, 8, 64]), ALU.mult)
            ptx = self.ps().v(lambda a: a.bitcast(BF16))
            for j in range(4):
                P.transpose(ptx[0:64, j * 128:(j + 1) * 128], xcb[j][:, cs_], self.identb)
            P.transpose(ptx[0:64, 512:640], Bb[:, cs_], self.identb)
            P.copy("act", self.xtok[:, c, :], ptx[0:64, 0:512])
            P.copy("act", self.Btok[:, c, :], ptx[0:64, 512:640])
            P.tt("pool", v3(self.xw[:, c, :]), v3(self.xtok[:, c, :]),
                 bc(w_tok[:, c, 8 * g:8 * g + 8].v(lambda a: a.unsqueeze(2)), [64, 8, 64]), ALU.mult)
        S = d["S"][g]
        inter = self.psb[4:8]
        for c in range(NCH):
            cs_ = slice(c * 64, (c + 1) * 64)
            Sb = self.Sb[c % 2]
            P.copy("act", Sb, S)
            for hp in range(4):
                P.mm(inter[hp][:, cs_], Sb[:, hp * 128:(hp + 1) * 128], Cb[:, cs_])
            pd = self.ps()
            P.mm(pd[:, 0:512], self.Btok[:, c, :], self.xw[:, c, :])
            P.tt("dve", v3(S), v3(S), bc(self.elast_bc[:, c, 8 * g:8 * g + 8].v(lambda a: a.unsqueeze(2)), [128, 8, 64]),
                 ALU.mult)
            P.tt("dve", S, S, pd[:, 0:512], ALU.add)
        ys = []
        pn = self.psb[4]
        for hp in range(4):
            hh = 4 * g + hp
            pe2 = self.ps()
            P.mm(pe2[:, 0:T], self.sel[:, 128 * hh:128 * hh + 128], cum)
            ecb = self.a32()
            P.act(ecb, pe2[:, 0:T], AF.Exp)
            tmp = self.a32()
            P.tt("dve", tmp, inter[hp][:, 0:T], ecb, ALU.mult)
            self.f32free(ecb)
            pin = self.ps()
            for c in range(NCH):
                for q in range(2):
                    hs = slice((2 * hp + q) * 64, (2 * hp + q) * 64 + 64)
                    P.mm(pin[64 * q:64 * q + 64, c * 64:(c + 1) * 64], self.xtok[:, c, hs], self.Gb[:, c, hs])
            P.stt("dve", tmp, xc[hp], dcol[:, hh:hh + 1], tmp, ALU.mult, ALU.add)
            P.tt("dve", tmp, tmp, pin[:, 0:T], ALU.add)
            zs = self.a32()
            self.proj(w_in, self.hT, c_z + 128 * hh, 128, lambda pt, j, nb: P.act(zs, pt, AF.Silu), cw=128)
            P.tt("pool", tmp, tmp, zs, ALU.mult)
            P.act(zs, tmp, AF.Square)
            P.mm(pn[:, 0:T], self.ones, zs, start=(hp == 0), stop=(hp == 3))
            self.f32free(zs)
            ys.append(tmp)
        rstd = self.a32()
        P.act(rstd, pn[:, 0:T], AF.Sqrt, bias=self.epsc, scale=1.0 / 512)
        P.recip(rstd, rstd)
        for hp in range(4):
            hh = 4 * g + hp
            P.stt("dve", yT[hh], ys[hp], gnw[:, hh:hh + 1], rstd, ALU.mult, ALU.mult)
        self.f32free(rstd, ys, xc)
        self.f16free(xcb, [Bb, Cb])
    self.f32free(dts)


Model.setup_ssm = setup_ssm
Model.mixer_ssm = mixer_ssm


def setup_hgrn(self, l):
    P = self.P
    NCH = self.NCH
    if not hasattr(self, "hg"):
        self.hg = {}
        self.load_cols(None, "vcLB", [("lb0", None, 8), ("lb1", None, 8)], srcs=[
            self.vin["hgrn_lb_logits"].v(lambda a: a[0].rearrange("(r c) -> r c", c=128)),
            self.vin["hgrn_lb_logits"].v(lambda a: a[1].rearrange("(r c) -> r c", c=128))])
        lb = P.sbuf("hg_lb", [128, 2, 8], F32)
        P.memset("dve", lb, 0.0)
        P.tt("dve", lb[:, 1, :], self.vcol[("lb1", None)], self.vcol[("lb0", None)], ALU.subtract)
        P.act(lb[:, 1, :], lb[:, 1, :], AF.Sigmoid)
        oml = P.sbuf("hg_oml", [128, 2, 8], F32)
        P.ts("dve", oml, lb, -1.0, ALU.mult, 1.0, ALU.add)
        self.hg_lb, self.hg_oml = lb, oml
        self.hg_vtok = P.sbuf("hg_vtok", [64, NCH, 128], BF16)
        self.hg_ktok = P.sbuf("hg_ktok", [64, NCH, 128], BF16)
        self.hg_scT = [P.sbuf(f"hg_scT{i}", [64, 64], BF16) for i in range(2)]
        self.hg_Sb = [P.sbuf(f"hg_Sb{i}", [128, 128], BF16) for i in range(2)]
        self.hg_cols = P.sbuf("hg_cols", [128, 5, NCH], F32)
    S = P.es.enter_context(self.nc.sbuf_tensor(f"hgS{l}", [128, 8, 128], F32))
    d = {"S": [Tl(S[:, h, :], Buf(f"hgS{l}_{h}")) for h in range(8)]}
    for h in range(8):
        P.memset("pool", d["S"][h], 0.0)
    self.hg[l] = d


def mixer_hgrn(self, l, ti):
    P, T, NCH = self.P, self.T, self.NCH
    d = self.hg[l]
    w_in = self.wbf[("w_in", l)]
    li = l
    gnw = self.vcol[("hgrn_gn_w", l)]
    for h in range(8):
        S = d["S"][h]
        f, cum, eq, ek, qs = self.a32(5)
        qb, kb, ib = self.a16(3)
        self.proj(w_in, self.hT, OFF_HGRN + 1024 + 128 * h, 128, lambda pt, j, nb: P.act(f, pt, AF.Sigmoid), cw=128)
        P.ts("dve", f, f, self.hg_oml[:, li, h:h + 1], ALU.mult, self.hg_lb[:, li, h:h + 1], ALU.add)
        if h == 0 and ti == 0:
            self.dump(f"hg_f{l}", f)
        P.act(eq, f, AF.Ln)
        if h == 0 and ti == 0:
            self.dump(f"hg_lnf{l}", eq)
        ca, la, rm = cum.ap, eq.ap, self.rmask.ap
        P.op("dve", lambda e, ca=ca, la=la, rm=rm: e.tensor_tensor_scan(out=ca, data0=rm, data1=la, initial=0.0,
                                                                     op0=ALU.mult, op1=ALU.add),
             reads=[self.rmask, eq], writes=[cum])
        if h == 0 and ti == 0:
            self.dump(f"hg_cumraw{l}", cum)
        P.ts("dve", f, f, -1.0, ALU.mult, 1.0, ALU.add)
        cols = self.hg_cols
        c3 = v3(cum)
        P.act(cols[:, 0, :], c3[:, :, 32], AF.Exp)
        P.act(cols[:, 1, :], c3[:, :, 63], AF.Exp)
        P.tt("dve", cols[:, 3, :], c3[:, :, 63], c3[:, :, 32], ALU.subtract)
        P.act(cols[:, 2, :], cols[:, 3, :], AF.Exp)
        P.copy("dve", cols[:, 4, :], c3[:, :, 32])
        P.tt("dve", c3, c3, bc(cols[:, 4, :].v(lambda a: a.unsqueeze(2)), [128, NCH, 64]), ALU.subtract)
        P.act(eq, cum, AF.Exp)
        P.act(ek, cum, AF.Exp, scale=-1.0)
        self.proj(w_in, self.hT, OFF_HGRN + 128 * h, 128, lambda pt, j, nb: P.act(qs, pt, AF.Silu), cw=128)
        P.tt("dve", qb, qs, eq, ALU.mult)
        P.tt("pool", kb, f, ek, ALU.mult)
        self.proj(w_in, self.hT, OFF_HGRN + 2048 + 128 * h, 128, lambda pt, j, nb: P.copy("act", ib, pt), cw=128)
        ptv = self.ps().v(lambda a: a.bitcast(BF16))
        for c in range(NCH):
            P.transpose(ptv[0:64, c * 128:(c + 1) * 128], ib[:, c * 64:(c + 1) * 64], self.identb)
        P.copy("act", self.hg_vtok.v(lambda a: a.rearrange("p c v -> p (c v)")), ptv[0:64, 0:NCH * 128])
        ptk = self.ps().v(lambda a: a.bitcast(BF16))
        for c in range(NCH):
            P.transpose(ptk[0:64, c * 128:(c + 1) * 128], kb[:, c * 64:(c + 1) * 64], self.identb)
        P.copy("dve", self.hg_ktok.v(lambda a: a.rearrange("p c v -> p (c v)")), ptk[0:64, 0:NCH * 128])
        po = self.psb[5]
        for c in range(NCH):
            cs_ = slice(c * 64, (c + 1) * 64)
            psc = self.ps()
            P.mm(psc[0:64, 0:64], kb[:, cs_], qb[:, cs_])
            scT = self.hg_scT[c % 2]
            P.tt("dve", scT, psc[0:64, 0:64], self.tri_incl, ALU.mult)
            Sb = self.hg_Sb[c % 2]
            P.ts("dve", Sb, S, cols[:, 0, c:c + 1], ALU.mult)
            P.mm(po[:, cs_], self.hg_vtok[:, c, :], scT, start=True, stop=False)
            P.mm(po[:, cs_], Sb, qb[:, cs_], start=False, stop=True)
            pd = self.ps()
            P.mm(pd[:, 0:128], self.hg_ktok[:, c, :], self.hg_vtok[:, c, :])
            P.ts("dve", S, S, cols[:, 1, c:c + 1], ALU.mult)
            P.stt("dve", S, pd[:, 0:128], cols[:, 2, c:c + 1], S, ALU.mult, ALU.add)
        o32 = qs
        P.copy("act", o32, po[:, 0:T])
        if h == 0 and ti == 0:
            self.dump(f"hg_o{l}", o32)
        P.act(eq, o32, AF.Square)
        pn = self.ps()
        P.mm(pn[:, 0:T], self.ones, eq)
        rstd = ek
        P.act(rstd, pn[:, 0:T], AF.Sqrt, bias=self.epsc, scale=1.0 / 128)
        P.recip(rstd, rstd)
        P.stt("dve", o32, o32, gnw[:, h:h + 1], rstd, ALU.mult, ALU.mult)
        self.proj(w_in, self.hT, OFF_HGRN + 3072 + 128 * h, 128, lambda pt, j, nb: P.act(f, pt, AF.Sigmoid), cw=128)
        P.tt("dve", self.yT[h], o32, f, ALU.mult)
        if h == 0 and ti == 0:
            self.dump(f"hg_y{l}", self.yT[h])
            self.dump(f"hg_cum{l}", cum)
            self.dump(f"hg_qb{l}", qb)
            self.dump(f"hg_kb{l}", kb)
            self.dump(f"hg_ib{l}", ib)
        self.f32free(f, cum, eq, ek, qs)
        self.f16free(qb, kb, ib)


Model.setup_hgrn = setup_hgrn
Model.mixer_hgrn = mixer_hgrn


C0 = float(np.exp(-0.5))


def setup_rwkv(self, l):
    P = self.P
    NCH, T = self.NCH, self.T
    if not hasattr(self, "rw"):
        self.rw = {}
        self.bones = P.sbuf("bones", [128, 128], F32)
        P.memset("pool", self.bones, 0.0)
        P.memset("pool", self.bones[0:64, 0:64], 1.0)
        P.memset("pool", self.bones[64:128, 64:128], 1.0)
        self.tri_ls = P.sbuf("tri_ls", [64, 64], F32)
        P.memset("pool", self.tri_ls, 1.0)
        ta = self.tri_ls.ap
        P.op("pool", lambda e: e.affine_select(out=ta, in_=ta, pattern=[[-1, 64]], compare_op=ALU.is_gt,
                                               fill=0.0, base=0, channel_multiplier=1),
             reads=[self.tri_ls], writes=[self.tri_ls])
        self.rext = [P.sbuf(f"rext{i}", [128, T + 1], F32) for i in range(2)]
        self.rexti = 0
        self.rw_Vtok = P.sbuf("rw_Vtok", [64, NCH, 128], BF16)
        self.rw_btok = P.sbuf("rw_btok", [64, NCH, 128], BF16)
        self.rw_ktok = P.sbuf("rw_ktok", [64, NCH, 128], BF16)
        self.rw_cols = P.sbuf("rw_cols", [128, 5, NCH], F32)
        self.rw_Sb = [P.sbuf(f"rw_Sb{i}", [128, 64], BF16) for i in range(2)]
        self.rw_lr = [P.sbuf(f"rw_lr{i}", [128, T], BF16) for i in range(4)]
        U = 2 * NCH
        self.rw_PU = [P.sbuf(f"rw_PU{i}", [64, 512], CHAIN_DT) for i in range(2)]
        self.rw_QU = [P.sbuf(f"rw_QU{i}", [64, 512], CHAIN_DT) for i in range(2)]
        self.rw_MTU = [P.sbuf(f"rw_MTU{i}", [64, 512], CHAIN_DT) for i in range(2)]
        self.rw_MTbU = P.sbuf("rw_MTbU", [64, U * 64], BF16)
        self.rw_A3 = P.sbuf("rw_A3", [64, 3, U * 64], BF16)
        self.rw_Xb2 = [P.sbuf(f"rw_Xb2{i}", [64, 128], BF16) for i in range(2)]
        self.rw_Ub2 = [P.sbuf(f"rw_Ub2{i}", [64, 128], BF16) for i in range(2)]
    d = {}
    specs = [("mu", None, 26), ("mu26", None, 1), ("muxa", None, 1)]
    srcs = [self.vin["rwkv_mu"].v(lambda a: a[l, 0:3328].rearrange("(r c) -> r c", c=128)),
            self.vin["rwkv_mu"].v(lambda a: a[l, 3328:3360].rearrange("(r c) -> r c", c=32)),
            self.vin["rwkv_mu"].v(lambda a: a[l, 3136:3200].rearrange("(r c) -> r c", c=64))]
    for n in ["w0", "a0", "k_k", "k_a", "gn_w", "gn_b"]:
        specs.append((n, None, 8))
        srcs.append(self.vin["rwkv_" + n].v(lambda a: a[l].rearrange("(r c) -> r c", c=128)))
    specs.append(("r_k", None, 8))
    srcs.append(self.vin["rwkv_r_k"].v(lambda a: a[l].rearrange("h (two c) -> (h two) c", two=1).rearrange("(r x) c -> r (x c)", x=2)))
    self.load_cols(("rw", l), f"vcR{l}", specs, srcs=srcs)
    for key, _, _ in specs:
        d[key] = self.vcol[(key, ("rw", l))]
    omk = P.sbuf(f"rw_omk{l}", [128, 8], F32)
    P.ts("dve", omk, d["k_a"], -1.0, ALU.mult, 1.0, ALU.add)
    d["omk"] = omk
    S = P.es.enter_context(self.nc.sbuf_tensor(f"rwS{l}", [128, 8, 64], F32))
    d["S"] = [Tl(S[:, p, :], Buf(f"rwS{l}_{p}")) for p in range(8)]
    for p in range(8):
        P.memset("pool", d["S"][p], 0.0)
    carry = P.sbuf(f"rw_carry{l}", [128, 28], F32)
    P.memset("pool", carry, 0.0)
    d["carry"] = carry
    self.rw[l] = d


def mixer_rwkv(self, l, ti):
    P, T, NCH = self.P, self.T, self.NCH
    d = self.rw[l]
    w_in = self.wbf[("w_in", l)]
    carry = d["carry"]

    def lerp(pt, np_, mucol, cidx, dst):
        ext = self.rext[self.rexti % 2]
        self.rexti += 1
        P.copy("pool", ext[0:np_, 0:1], carry[0:np_, cidx:cidx + 1])
        P.copy("act", ext[0:np_, 1:T + 1], pt)
        P.copy("pool", carry[0:np_, cidx:cidx + 1], ext[0:np_, T:T + 1])
        dd = self.a32()
        P.tt("dve", dd[0:np_, :], ext[0:np_, 0:T], ext[0:np_, 1:T + 1], ALU.subtract)
        P.stt("dve", dst, dd[0:np_, :], mucol, ext[0:np_, 1:T + 1], ALU.mult, ALU.add)
        self.f32free(dd)

    tmp = self.a32()
    txw, xab, sg0, sg1 = self.rw_lr
    self.proj(w_in, self.hT, 3072, 64, lambda pt, j, nb: lerp(pt, 64, d["mu"][0:64, 24:25], 24, tmp[0:64, :]), cw=128)
    P.act(txw[0:64, :], tmp[0:64, :], AF.Tanh)
    self.proj(w_in, self.hT, 3136, 64, lambda pt, j, nb: lerp(pt, 64, d["muxa"][0:64, 0:1], 27, tmp[0:64, :]), cw=128)
    P.copy("act", xab[0:64, :], tmp[0:64, :])
    self.proj(w_in, self.hT, 3200, 128, lambda pt, j, nb: lerp(pt, 128, d["mu"][:, 25:26], 25, tmp), cw=128)
    P.act(sg0, tmp, AF.Sigmoid)
    self.proj(w_in, self.hT, 3328, 32, lambda pt, j, nb: lerp(pt, 32, d["mu26"][0:32, 0:1], 26, tmp[0:32, :]), cw=128)
    P.act(sg1[0:32, :], tmp[0:32, :], AF.Sigmoid)
    self.f32free(tmp)
    cols = self.rw_cols
    for p in range(8):
        r, k, v, sg, Sc, a, kkn, g = self.a32(8)
        ab, rb, bb, kb, vb = self.a16(5)
        self.proj(w_in, self.hT, 128 * p, 128, lambda pt, j, nb: lerp(pt, 128, d["mu"][:, p:p + 1], p, r), cw=128)
        self.proj(w_in, self.hT, 1024 + 128 * p, 128, lambda pt, j, nb: lerp(pt, 128, d["mu"][:, 8 + p:9 + p], 8 + p, k), cw=128)
        self.proj(w_in, self.hT, 2048 + 128 * p, 128, lambda pt, j, nb: lerp(pt, 128, d["mu"][:, 16 + p:17 + p], 16 + p, v), cw=128)
        self.proj(self.wbf[("rwkv_w_up", l)], [txw[0:64, :]], 128 * p, 128,
                  lambda pt, j, nb: P.act(sg, pt, AF.Sigmoid, bias=d["w0"][:, p:p + 1]), cw=128, rows=64)
        self.proj(self.wbf[("rwkv_a_up", l)], [xab[0:64, :]], 128 * p, 128,
                  lambda pt, j, nb: P.act(a, pt, AF.Sigmoid, bias=d["a0"][:, p:p + 1]), cw=128, rows=64)
        vw0 = self.load_w(self.wbf[("rwkv_g_up", l)], 1, 128 * p, 128, r0=0, rows=128)
        vw1 = self.load_w(self.wbf[("rwkv_g_up", l)], 1, 128 * p, 128, r0=128, rows=32)
        pg = self.ps()
        P.mm(pg[:, 0:T], vw0[:, 0, :], sg0, start=True, stop=False)
        P.mm(pg[:, 0:T], vw1[0:32, 0, :], sg1[0:32, :], start=False, stop=True)
        P.copy("act", g, pg[:, 0:T])
        S_ = Sc
        sa, ga, rm = S_.ap, sg.ap, self.rmask.ap
        P.op("dve", lambda e, sa=sa, ga=ga, rm=rm: e.tensor_tensor_scan(out=sa, data0=rm, data1=ga, initial=0.0,
                                                                     op0=ALU.mult, op1=ALU.add),
             reads=[self.rmask, sg], writes=[S_])
        s3 = v3(S_)
        P.act(cols[:, 0, :], s3[:, :, 32], AF.Exp, scale=-C0)
        P.act(cols[:, 1, :], s3[:, :, 63], AF.Exp, scale=-C0)
        P.tt("dve", cols[:, 3, :], s3[:, :, 63], s3[:, :, 32], ALU.subtract)
        P.act(cols[:, 2, :], cols[:, 3, :], AF.Exp, scale=-C0)
        P.copy("dve", cols[:, 4, :], s3[:, :, 32])
        P.tt("dve", s3, s3, bc(cols[:, 4, :].v(lambda a_: a_.unsqueeze(2)), [128, NCH, 64]), ALU.subtract)
        e1, e2, t1 = self.a32(3)
        P.tt("dve", t1, Sc, sg, ALU.subtract)
        P.ts("dve", kkn, k, d["k_k"][:, p:p + 1], ALU.mult)
        P.act(e1, kkn, AF.Square)
        pn = self.ps()
        P.mm(pn[:, 0:T], self.bones, e1)
        P.act(e1, pn[:, 0:T], AF.Sqrt)
        P.ts("dve", e1, e1, 1e-12, ALU.max)
        P.recip(e1, e1)
        P.tt("dve", kkn, kkn, e1, ALU.mult)
        P.act(e2, t1, AF.Exp, scale=-C0)
        P.stt("dve", ab, kkn, -1.0, e2, ALU.mult, ALU.mult)
        P.act(e1, Sc, AF.Exp, scale=-C0)
        P.tt("pool", rb, r, e1, ALU.mult)
        P.act(e2, Sc, AF.Exp, scale=C0)
        P.tt("dve", t1, kkn, a, ALU.mult)
        P.tt("pool", bb, t1, e2, ALU.mult)
        P.ts("dve", t1, a, d["k_a"][:, p:p + 1], ALU.mult, d["omk"][:, p:p + 1], ALU.add)
        P.tt("dve", k, k, t1, ALU.mult)
        P.tt("pool", kb, k, e2, ALU.mult)
        P.copy("act", vb, v)
        P.stt("dve", t1, r, d["r_k"][:, p:p + 1], k, ALU.mult, ALU.mult)
        pbn = self.ps()
        P.mm(pbn[:, 0:T], self.bones, t1)
        bonus = r
        P.tt("dve", bonus, pbn[:, 0:T], v, ALU.mult)
        self.f32free(e1, e2, t1)
        for src, dstt in ((vb, self.rw_Vtok), (bb, self.rw_btok), (kb, self.rw_ktok)):
            ptt = self.ps().v(lambda a_: a_.bitcast(BF16))
            for c in range(NCH):
                P.transpose(ptt[0:64, c * 128:(c + 1) * 128], src[:, c * 64:(c + 1) * 64], self.identb)
            P.copy("act", dstt.v(lambda a_: a_.rearrange("p c v -> p (c v)")), ptt[0:64, 0:NCH * 128])
        S = d["S"][p]
        U = 2 * NCH
        A3, MTb = self.rw_A3, self.rw_MTbU
        def uidx(c, q):
            ug_, cc = divmod(c, 4)
            return ug_ * 8 + q * 4 + cc
        m_st = bc(self.tri_strict.v(lambda a_: a_.unsqueeze(1)), [64, 4, 64])
        m_ls = bc(self.tri_ls.v(lambda a_: a_.unsqueeze(1)), [64, 4, 64])
        m_in = bc(self.tri_incl.v(lambda a_: a_.unsqueeze(1)), [64, 4, 64])
        idb = bc(self.ident[0:64, 0:64].v(lambda a_: a_.unsqueeze(1)), [64, 8, 64])
        bk = self.psb
        for ug in range(U // 8):
            gsl = slice(ug * 512, (ug + 1) * 512)
            Pm, Qm, MT = self.rw_PU[0], self.rw_QU[0], self.rw_MTU[0]

            def batch(bank_pair, lhs, rhs):
                for q in range(2):
                    hs = slice(64 * q, 64 * q + 64)
                    for cc in range(4):
                        c = ug * 4 + cc
                        cs_ = slice(c * 64, (c + 1) * 64)
                        P.mm(bank_pair[q][0:64, cc * 64:(cc + 1) * 64], lhs[hs, cs_], rhs[hs, cs_])
            batch((bk[0], bk[1]), bb, ab)
            batch((bk[2], bk[3]), ab, bb)
            for q in range(2):
                qs_ = slice(q * 256, (q + 1) * 256)
                P.tt("dve", v3(Pm[:, qs_]), v3(bk[0 + q][0:64, 0:256]), m_st, ALU.mult)
                P.tt("dve", v3(Qm[:, qs_]), v3(bk[2 + q][0:64, 0:256]), m_ls, ALU.mult)
            P.tt("pool", v3(MT), v3(Pm), idb, ALU.add)
            batch((bk[0], bk[1]), kb, ab)
            batch((bk[2], bk[3]), bb, rb)
            batch((bk[6], bk[7]), kb, rb)
            for q in range(2):
                qs_ = slice(ug * 512 + q * 256, ug * 512 + (q + 1) * 256)
                P.tt("dve", v3(A3[:, 0, qs_]), v3(bk[0 + q][0:64, 0:256]), m_st, ALU.mult)
                P.tt("dve", v3(A3[:, 1, qs_]), v3(bk[2 + q][0:64, 0:256]), m_in, ALU.mult)
                P.tt("dve", v3(A3[:, 2, qs_]), v3(bk[6 + q][0:64, 0:256]), m_in, ALU.mult)
            b0, b1, b2 = bk[0], bk[1], bk[2]
            for it in range(1, 6):
                Pn, Qn, MTn = self.rw_PU[it % 2], self.rw_QU[it % 2], self.rw_MTU[it % 2]
                for ui in range(8):
                    us = slice(ui * 64, ui * 64 + 64)
                    if it < 5:
                        P.mm(b0[0:64, us], Qm[:, us], Pm[:, us])
                    P.mm(b1[0:64, us], Pm[:, us], Qm[:, us])
                if it < 5:
                    P.copy("act", Pn, b0[0:64, :])
                P.copy("dve", Qn, b1[0:64, :])
                for ui in range(8):
                    us = slice(ui * 64, ui * 64 + 64)
                    P.mm(b2[0:64, us], Qn[:, us], MT[:, us])
                P.tt("dve", MTn, MT, b2[0:64, :], ALU.add)
                Pm, Qm, MT = Pn, Qn, MTn
            P.copy("act", MTb[:, gsl], MT)
        pO = (bk[4], bk[6])
        for c in range(NCH):
            cs_ = slice(c * 64, (c + 1) * 64)
            Sb = self.rw_Sb[c % 2]
            P.ts("dve", Sb, S, cols[:, 0, c:c + 1], ALU.mult)
            pS = bk[5]
            pX = (bk[(c % 2) * 2], bk[(c % 2) * 2 + 1])
            pU = bk[7]
            Xb, Ub = self.rw_Xb2[c % 2], self.rw_Ub2[c % 2]
            for q in range(2):
                hs = slice(64 * q, 64 * q + 64)
                us = slice(uidx(c, q) * 64, uidx(c, q) * 64 + 64)
                Vt = self.rw_Vtok[:, c, hs]
                P.mm(pX[q][0:64, 0:64], ab[hs, cs_], Sb[hs, :], start=True, stop=False)
                P.mm(pX[q][0:64, 0:64], A3[:, 0, us], Vt, start=False, stop=True)
                P.copy("act", Xb[:, hs], pX[q][0:64, 0:64])
            for q in range(2):
                hs = slice(64 * q, 64 * q + 64)
                us = slice(uidx(c, q) * 64, uidx(c, q) * 64 + 64)
                P.mm(pU[0:64, hs], MTb[:, us], Xb[:, hs])
            P.copy("act", Ub, pU[0:64, 0:128])
            for q in range(2):
                hs = slice(64 * q, 64 * q + 64)
                Vt = self.rw_Vtok[:, c, hs]
                P.mm(pS[hs, 0:64], self.rw_btok[:, c, hs], Ub[:, hs], start=True, stop=False)
                P.mm(pS[hs, 0:64], self.rw_ktok[:, c, hs], Vt, start=False, stop=True)
            for q in range(2):
                hs = slice(64 * q, 64 * q + 64)
                us = slice(uidx(c, q) * 64, uidx(c, q) * 64 + 64)
                Vt = self.rw_Vtok[:, c, hs]
                P.mm(pO[q][hs, cs_], Sb[hs, :], rb[hs, cs_], start=True, stop=False)
                P.mm(pO[q][hs, cs_], Ub[:, hs], A3[:, 1, us], start=False, stop=False)
                P.mm(pO[q][hs, cs_], Vt, A3[:, 2, us], start=False, stop=True)
            P.ts("dve", S, S, cols[:, 1, c:c + 1], ALU.mult)
            P.stt("dve", S, pS[:, 0:64], cols[:, 2, c:c + 1], S, ALU.mult, ALU.add)
        o = k
        P.copy("act", o[0:64, :], pO[0][0:64, 0:T])
        P.copy("act", o[64:128, :], pO[1][64:128, 0:T])
        pm = self.ps()
        P.mm(pm[:, 0:T], self.bones, o)
        P.stt("dve", o, pm[:, 0:T], -1.0 / 64, o, ALU.mult, ALU.add)
        P.act(sg, o, AF.Square)
        pv = self.ps()
        P.mm(pv[:, 0:T], self.bones, sg)
        P.act(sg, pv[:, 0:T], AF.Sqrt, bias=self.gnepsc, scale=1.0 / 64)
        P.recip(sg, sg)
        P.stt("dve", o, o, d["gn_w"][:, p:p + 1], sg, ALU.mult, ALU.mult)
        P.stt("dve", o, o, d["gn_b"][:, p:p + 1], bonus, ALU.add, ALU.add)
        P.tt("dve", self.yT[p], o, g, ALU.mult)
        self.f32free(r, k, v, sg, Sc, a, kkn, g)
        self.f16free(ab, rb, bb, kb, vb)


Model.setup_rwkv = setup_rwkv
Model.mixer_rwkv = mixer_rwkv


def kernel(**inputs):
    inputs = {k_: np.asarray(v_) for k_, v_ in inputs.items()}
    S = 4096
    m = Model(NT=S // 256, T=256, layers=(0, 1), stub=())
    maps = [make_in_map(inputs, b, S) for b in range(8)]
    res = run_bass_kernel_spmd(m.nc, maps, core_ids=list(range(8)))
    out = np.stack([np.asarray(r["out"]) for r in res.results]).astype(np.float32)
    return out
```

```python
import numpy as np
from contextlib import ExitStack
import concourse.bass as bass
import concourse.mybir as mybir

F32 = mybir.dt.float32
BF16 = mybir.dt.bfloat16
AF = mybir.ActivationFunctionType
ALU = mybir.AluOpType
AX = mybir.AxisListType


class Buf:
    __slots__ = ("name", "w", "r", "dsem", "dcount")

    def __init__(self, name):
        self.name = name
        self.w = None
        self.r = []
        self.dsem = None
        self.dcount = 0


class Tl:
    __slots__ = ("ap", "buf")

    def __init__(self, ap, buf):
        self.ap = ap
        self.buf = buf

    def __getitem__(self, k):
        return Tl(self.ap[k], self.buf)

    def v(self, fn):
        return Tl(fn(self.ap), self.buf)

    @property
    def shape(self):
        return self.ap.shape


class Prog:
    ENGS = ("pe", "act", "dve", "pool", "sp")

    def __init__(self, nc, same_eng_sync=True):
        self.nc = nc
        self.es = ExitStack()
        self.streams = {e: [] for e in self.ENGS}
        self.count = {e: 0 for e in self.ENGS}
        self.seen = {e: {} for e in self.ENGS}
        self.sem = {}
        for e in self.ENGS:
            self.sem[e] = self.es.enter_context(nc.semaphore("sem_" + e))
        self.same_eng_sync = same_eng_sync
        self.nbuf = 0
        self.n_wait = 0
        self.n_ins = 0

    def sbuf(self, name, shape, dtype):
        t = self.es.enter_context(self.nc.sbuf_tensor(name, list(shape), dtype))
        return Tl(t[:], Buf(name))

    def psum(self, name, shape, dtype=F32):
        t = self.es.enter_context(self.nc.psum_tensor(name, list(shape), dtype))
        return Tl(t[:], Buf(name))

    def dram(self, name, shape, dtype, kind="Internal"):
        t = self.nc.dram_tensor(name, list(shape), dtype, kind=kind)
        return Tl(t.ap(), Buf(name))

    def newbuf(self, name="b"):
        self.nbuf += 1
        return Buf(f"{name}{self.nbuf}")

    def _dsem(self, buf):
        if buf.dsem is None:
            buf.dsem = self.es.enter_context(self.nc.semaphore("d_" + buf.name))
        return buf.dsem

    def _need(self, eng, reads, writes):
        need = {}

        def add(tok):
            if tok is None:
                return
            kind = tok[0]
            if kind == "E":
                _, e2, seq = tok
                if e2 == eng and (eng == "pe" or not self.same_eng_sync):
                    return
                key = ("E", e2)
                sem, val = self.sem[e2], seq
            else:
                _, b, n = tok
                key = ("D", id(b))
                sem, val = b.dsem, 16 * n
            if key not in need or need[key][1] < val:
                need[key] = (sem, val)

        for b in reads:
            add(b.w)
        for b in writes:
            add(b.w)
            for t in b.r:
                add(t)
        out = []
        seen = self.seen[eng]
        for key, (sem, val) in need.items():
            if seen.get(key, 0) >= val:
                continue
            seen[key] = val
            out.append((sem, val))
        return out

    def op(self, eng, fn, reads=(), writes=()):
        reads = [t.buf for t in reads if t is not None and isinstance(t, Tl)]
        writes = [t.buf for t in writes if t is not None and isinstance(t, Tl)]
        for sem, val in self._need(eng, reads, writes):
            self.streams[eng].append(("w", sem, val))
            self.n_wait += 1
        self.count[eng] += 1
        seq = self.count[eng]
        self.streams[eng].append(("i", fn, self.sem[eng], 1))
        self.n_ins += 1
        tok = ("E", eng, seq)
        for b in reads:
            b.r.append(tok)
        for b in writes:
            b.w = tok
            b.r = []
        return tok

    def dma(self, q, out, in_, sem_tl):
        reads = [in_.buf]
        writes = [out.buf]
        sb = sem_tl.buf
        sem = self._dsem(sb)
        for s, val in self._need(q, reads, writes):
            self.streams[q].append(("w", s, val))
            self.n_wait += 1
        sb.dcount += 1
        oap, iap = out.ap, in_.ap
        self.streams[q].append(("i", lambda e: e.dma_start(out=oap, in_=iap), sem, 16))
        self.n_ins += 1
        tok = ("D", sb, sb.dcount)
        in_.buf.r.append(tok)
        out.buf.w = tok
        out.buf.r = []
        return tok

    def wait_all_dma(self, q, tls):
        for t in tls:
            b = t.buf
            if b.dsem is not None and b.dcount > 0:
                self.streams[q].append(("w", b.dsem, 16 * b.dcount))

    def mm(self, out, lhsT, rhs, start=True, stop=True):
        o, l, r = out.ap, lhsT.ap, rhs.ap
        return self.op("pe", lambda e: e.matmul(o, lhsT=l, rhs=r, start=start, stop=stop),
                       reads=[lhsT, rhs], writes=[out])

    def transpose(self, out, in_, ident):
        o, i, d = out.ap, in_.ap, ident.ap
        return self.op("pe", lambda e: e.transpose(o, i, d), reads=[in_, ident], writes=[out])

    def act(self, out, in_, func, bias=0.0, scale=1.0, accum=None, eng="act"):
        o, i = out.ap, in_.ap
        b = bias.ap if isinstance(bias, Tl) else bias
        s = scale.ap if isinstance(scale, Tl) else scale
        reads = [in_] + [x for x in (bias, scale) if isinstance(x, Tl)]
        writes = [out]
        if accum is not None:
            a = accum.ap
            writes.append(accum)
            return self.op(eng, lambda e: e.activation(out=o, in_=i, func=func, bias=b, scale=s, accum_out=a),
                           reads=reads, writes=writes)
        return self.op(eng, lambda e: e.activation(out=o, in_=i, func=func, bias=b, scale=s),
                       reads=reads, writes=writes)

    def tt(self, eng, out, a, b, op):
        o, x, y = out.ap, a.ap, b.ap
        return self.op(eng, lambda e: e.tensor_tensor(out=o, in0=x, in1=y, op=op), reads=[a, b], writes=[out])

    def ts(self, eng, out, a, s1, op0, s2=None, op1=None):
        o, x = out.ap, a.ap
        v1 = s1.ap if isinstance(s1, Tl) else s1
        v2 = s2.ap if isinstance(s2, Tl) else s2
        reads = [a] + [x_ for x_ in (s1, s2) if isinstance(x_, Tl)]
        if op1 is None:
            return self.op(eng, lambda e: e.tensor_scalar(out=o, in0=x, scalar1=v1, scalar2=None, op0=op0),
                           reads=reads, writes=[out])
        return self.op(eng, lambda e: e.tensor_scalar(out=o, in0=x, scalar1=v1, scalar2=v2, op0=op0, op1=op1),
                       reads=reads, writes=[out])

    def stt(self, eng, out, a, s, b, op0, op1):
        o, x, y = out.ap, a.ap, b.ap
        sv = s.ap if isinstance(s, Tl) else s
        reads = [a, b] + ([s] if isinstance(s, Tl) else [])
        return self.op(eng, lambda e: e.scalar_tensor_tensor(out=o, in0=x, scalar=sv, in1=y, op0=op0, op1=op1),
                       reads=reads, writes=[out])

    def copy(self, eng, out, in_):
        o, i = out.ap, in_.ap
        if eng == "act":
            return self.op(eng, lambda e: e.copy(out=o, in_=i), reads=[in_], writes=[out])
        return self.op(eng, lambda e: e.tensor_copy(out=o, in_=i), reads=[in_], writes=[out])

    def memset(self, eng, out, val):
        o = out.ap
        return self.op(eng, lambda e: e.memset(o, val), reads=[], writes=[out])

    def recip(self, out, in_):
        o, i = out.ap, in_.ap
        return self.op("dve", lambda e: e.reciprocal(out=o, in_=i), reads=[in_], writes=[out])

    def emit(self, final_waits=()):
        nc = self.nc
        streams = self.streams
        for e in self.ENGS:
            if e != "sp" and self.count[e] > 0:
                streams["sp"].append(("w", self.sem[e], self.count[e]))
        self.wait_all_dma("sp", final_waits)

        def run(eng_handle, lst):
            for it in lst:
                if it[0] == "w":
                    eng_handle.wait_ge(it[1], it[2])
                else:
                    _, fn, sem, inc = it
                    fn(eng_handle).then_inc(sem, inc)

        with nc.Block() as block:
            @block.tensor
            def _(e):
                run(e, streams["pe"])

            @block.scalar
            def _(e):
                run(e, streams["act"])

            @block.vector
            def _(e):
                run(e, streams["dve"])

            @block.gpsimd
            def _(e):
                run(e, streams["pool"])

            @block.sync
            def _(e):
                run(e, streams["sp"])
        self.es.close()


from concourse.bass_utils import run_bass_kernel_spmd

D = 1024
KC_D = 8
NL = 2
RW_COLS = 3360
OFF_HGRN = 3360
OFF_SSM = 7456
OFF_GATE = 12608
IN_COLS = 15680
FFN_H = 2816
EPS = 1e-5
SAME_ENG_SYNC = True
CHAIN_DT = BF16

WNAMES = ["w_in", "w_branch", "w_out", "w_ffn_in", "w_ffn_out", "rwkv_w_up", "rwkv_a_up", "rwkv_g_up"]
WSHAPES = {"w_in": (1024, IN_COLS), "w_branch": (4096, 1024), "w_out": (1024, 1024),
           "w_ffn_in": (1024, 2 * FFN_H), "w_ffn_out": (FFN_H, 1024),
           "rwkv_w_up": (64, 1024), "rwkv_a_up": (64, 1024), "rwkv_g_up": (160, 1024)}
VEC_SHAPES = {"norm_mix": (NL, 1024), "rwkv_mu": (NL, 3360), "rwkv_w0": (NL, 1024), "rwkv_a0": (NL, 1024),
              "rwkv_k_k": (NL, 1024), "rwkv_k_a": (NL, 1024), "rwkv_r_k": (NL, 16, 64),
              "rwkv_gn_w": (NL, 1024), "rwkv_gn_b": (NL, 1024), "hgrn_lb_logits": (NL, 1024),
              "hgrn_gn_w": (NL, 1024), "ssm_conv_w": (NL, 3072, 4), "ssm_conv_b": (NL, 3072),
              "ssm_dt_bias": (NL, 32), "ssm_a_log": (NL, 32), "ssm_d": (NL, 32), "ssm_gn_w": (NL, 2048),
              "norm_ffn": (NL, 1024), "norm_final": (1024,)}


class Model:
    def __init__(self, NT=1, T=512, layers=(0, 1), stub=("rwkv", "hgrn", "ssm"), dbg=None):
        self.NT, self.T, self.layers, self.stub = NT, T, tuple(layers), set(stub)
        self.dbg = dbg or []
        self.S = NT * T
        self.NF32 = 27
        self.NBF16 = 23
        nc = bass.Bass("TRN2", target_bir_lowering=False)
        self.nc = nc
        self.P = Prog(nc, same_eng_sync=SAME_ENG_SYNC)
        self.build()

    def build(self):
        P, nc, T = self.P, self.nc, self.T
        S = self.S
        self.x_in = P.dram("x", [S, D], F32, kind="ExternalInput")
        self.out = P.dram("out", [S, D], F32, kind="ExternalOutput")
        self.win = {}
        for n in WNAMES:
            sh = WSHAPES[n]
            self.win[n] = P.dram(n, [NL, sh[0], sh[1]], F32, kind="ExternalInput")
        self.vin = {}
        for n, sh in VEC_SHAPES.items():
            self.vin[n] = P.dram(n, list(sh), F32, kind="ExternalInput")
        self.wbf = {}
        for l in self.layers:
            for n in WNAMES:
                sh = WSHAPES[n]
                self.wbf[(n, l)] = P.dram(f"{n}_bf{l}", [sh[0], sh[1]], BF16)
        self.dbg_out = {}

        self.NCH = T // 64
        self.NSLOT = 3
        self.SLOTE = 4096
        self.wslots = [P.sbuf(f"wslot{i}", [128, self.SLOTE], BF16) for i in range(self.NSLOT)]
        self.wi = 0
        self.psb = [P.psum(f"ps{i}", [128, 512], F32) for i in range(8)]
        self.pi = 0
        self.ident = P.sbuf("ident", [128, 128], F32)
        self.identb = P.sbuf("identb", [128, 128], BF16)
        self.ones = P.sbuf("ones", [128, 128], F32)
        self.epsc = P.sbuf("epsc", [128, 1], F32)
        self.xres = self.slabs("xres", 8, F32)
        self.hT = self.slabs("hT", 8, BF16)
        self.f32pool = self.slabs("f32pool", self.NF32, F32)
        self.bf16pool = self.slabs("bf16pool", self.NBF16, BF16)
        self.tokbuf = [P.sbuf(f"tokbuf{i}", [128, D], F32) for i in range(2)]
        self.tki = 0
        self.vecs = {}
        self.setup_consts()
        self.cast_weights()
        for ti in range(self.NT):
            self.load_x(ti)
            for l in self.layers:
                self.layer(l, ti)
            self.final(ti)
        P.emit(final_waits=self.tokbuf + getattr(self, 'dbg_tls', []))

    def slabs(self, name, n, dtype, width=None):
        P = self.P
        w = width or self.T
        t = P.es.enter_context(self.nc.sbuf_tensor(name, [128, n, w], dtype))
        return [Tl(t[:, i, :], Buf(f"{name}{i}")) for i in range(n)]

    def dump(self, name, tl):
        if not self.dbg or name in self.dbg_out:
            return
        P = self.P
        o = P.dram("dbg_" + name, list(tl.shape), F32, kind="ExternalOutput")
        P.dma("pool", o, tl, tl)
        self.dbg_out[name] = o
        self.dbg_tls = getattr(self, "dbg_tls", []) + [tl]

    def ps(self):
        t = self.psb[self.pi % 4]
        self.pi += 1
        return t

    def a32(self, n=None):
        r = self.f32pool.pop() if n is None else [self.f32pool.pop() for _ in range(n)]
        self.min32 = min(getattr(self, "min32", 999), len(self.f32pool))
        return r

    def a16(self, n=None):
        r = self.bf16pool.pop() if n is None else [self.bf16pool.pop() for _ in range(n)]
        self.min16 = min(getattr(self, "min16", 999), len(self.bf16pool))
        return r

    def f32free(self, *ts):
        for t in ts:
            self.f32pool.extend(t if isinstance(t, list) else [t])

    def f16free(self, *ts):
        for t in ts:
            self.bf16pool.extend(t if isinstance(t, list) else [t])

    def setup_consts(self):
        P = self.P
        nc = self.nc
        P.memset("pool", self.ones, 1.0)
        P.memset("pool", self.epsc, EPS)
        self.gnepsc = P.sbuf("gnepsc", [128, 1], F32)
        self.tinyc = P.sbuf("tinyc", [128, 1], F32)
        P.memset("pool", self.tinyc, 1e-24)
        P.memset("pool", self.gnepsc, 64e-5)
        P.memset("pool", self.ident, 1.0)
        ia = self.ident.ap
        P.op("pool", lambda e: e.affine_select(out=ia, in_=ia, pattern=[[-1, 128]], compare_op=ALU.is_equal,
                                               fill=0.0, base=0, channel_multiplier=1),
             reads=[self.ident], writes=[self.ident])
        P.copy("dve", self.identb, self.ident)
        self.vstage = P.sbuf("vstage", [128, 512], F32)
        self.vcol = {}
        T = self.T
        self.rmask = P.sbuf("rmask", [128, T], F32)
        P.memset("pool", self.rmask, 1.0)
        P.memset("pool", self.rmask.v(lambda a: a.rearrange("p (c t) -> p c t", t=64)[:, :, 0:1]), 0.0)
        self.neg8 = P.sbuf("neg8", [64, 8, 64], BF16)
        P.memset("pool", self.neg8, 0.0)
        na = self.neg8.ap
        P.op("pool", lambda e: e.affine_select(out=na, in_=na, pattern=[[0, 8], [1, 64]], compare_op=ALU.is_ge,
                                               fill=-30000.0, base=0, channel_multiplier=-1),
             reads=[self.neg8], writes=[self.neg8])
        self.tri_incl = P.sbuf("tri_incl", [64, 64], F32)
        P.memset("pool", self.tri_incl, 1.0)
        ta = self.tri_incl.ap
        P.op("pool", lambda e: e.affine_select(out=ta, in_=ta, pattern=[[1, 64]], compare_op=ALU.is_ge,
                                               fill=0.0, base=0, channel_multiplier=-1),
             reads=[self.tri_incl], writes=[self.tri_incl])
        self.tri_strict = P.sbuf("tri_strict", [64, 64], F32)
        P.memset("pool", self.tri_strict, 1.0)
        tsa = self.tri_strict.ap
        P.op("pool", lambda e: e.affine_select(out=tsa, in_=tsa, pattern=[[1, 64]], compare_op=ALU.is_gt,
                                               fill=0.0, base=0, channel_multiplier=-1),
             reads=[self.tri_strict], writes=[self.tri_strict])
        self.sel = P.sbuf("sel", [32, 2048], F32)
        P.memset("pool", self.sel, 1.0)
        sa = self.sel.ap
        P.op("pool", lambda e: e.affine_select(out=sa, in_=sa, pattern=[[1, 2048]], compare_op=ALU.is_ge,
                                               fill=0.0, base=0, channel_multiplier=-64),
             reads=[self.sel], writes=[self.sel])
        P.op("pool", lambda e: e.affine_select(out=sa, in_=sa, pattern=[[-1, 2048]], compare_op=ALU.is_ge,
                                               fill=0.0, base=63, channel_multiplier=64),
             reads=[self.sel], writes=[self.sel])
        for l in self.layers:
            specs = [("norm_mix", "norm_mix", 8), ("norm_ffn", "norm_ffn", 8),
                     ("ssm_conv_b", "ssm_conv_b", 24), ("ssm_gn_w", "ssm_gn_w", 16),
                     ("hgrn_gn_w", "hgrn_gn_w", 8)]
            self.load_cols(l, f"vcA{l}", specs)
            self.setup_ssm(l)
            self.setup_hgrn(l)
            self.setup_rwkv(l)
        l0 = self.layers[0]
        self.load_cols(None, "vcF", [("norm_final", "norm_final", 8)])

    def load_cols(self, l, name, specs, srcs=None):
        P = self.P
        tot = sum(x[2] for x in specs)
        assert tot <= 128
        P.memset("dve", self.vstage[:, 0:128], 0.0)
        r = 0
        for si, (key, n, nr) in enumerate(specs):
            if srcs is not None:
                src = srcs[si]
            elif l is None:
                src = self.vin[n].v(lambda a: a.rearrange("(r c) -> r c", c=128))
            else:
                src = self.vin[n].v(lambda a: a[l].rearrange("(r c) -> r c", c=128))
            P.dma("sp", self.vstage[r:r + nr, 0:src.shape[1]], src, self.vstage)
            r += nr
        pt = self.ps()
        P.transpose(pt[:, 0:128], self.vstage[:, 0:128], self.ident)
        vc = P.sbuf(name, [128, tot], F32)
        P.copy("dve", vc, pt[:, 0:tot])
        r = 0
        for key, n, nr in specs:
            self.vcol[(key, l)] = vc[:, r:r + nr]
            r += nr

    def cast_weights(self):
        P = self.P
        for l in self.layers:
            for n in WNAMES:
                K = WSHAPES[n][0]
                dst = self.wbf[(n, l)]
                for r0 in range(0, K, 128):
                    nr = min(128, K - r0)
                    P.dma("pool", dst[r0:r0 + nr, :], self.win[n].v(lambda a: a[l, r0:r0 + nr, :]), dst)

    def load_w(self, wt, KC, c0, cw, r0=0, rows=None):
        P = self.P
        slot = self.wslots[self.wi % self.NSLOT]
        self.wi += 1
        assert KC * cw <= self.SLOTE, (KC, cw)
        view = slot.v(lambda a: a[:, :KC * cw].rearrange("p (k c) -> p k c", c=cw))
        if rows is None:
            src = wt.v(lambda a: a[r0:r0 + KC * 128, c0:c0 + cw].rearrange("(k p) n -> p k n", p=128))
            P.dma("sp", view, src, slot)
        else:
            assert KC == 1
            src = wt.v(lambda a: a[r0:r0 + rows, c0:c0 + cw])
            P.dma("sp", view.v(lambda a: a[0:rows, 0, :]), src, slot)
        return view

    def proj(self, wt, rhs, c0, ncols, consume, r0=0, cw=512, rows=None):
        P, T = self.P, self.T
        KC = len(rhs)
        cw = min(cw, (self.SLOTE // KC) // 128 * 128)
        j = 0
        for cb in range(0, ncols, cw):
            w = min(cw, ncols - cb)
            view = self.load_w(wt, KC, c0 + cb, w, r0=r0, rows=rows)
            for b in range(0, w, 128):
                nb = min(128, w - b)
                pt = self.ps()
                for k in range(KC):
                    P.mm(pt[0:nb, 0:T], view[0:rhs[k].shape[0], k, b:b + nb], rhs[k], start=(k == 0), stop=(k == KC - 1))
                consume(pt[0:nb, 0:T], j, nb)
                j += 1

    def load_x(self, ti):
        P, T = self.P, self.T
        for tb in range(T // 128):
            tk = self.tokbuf[self.tki % 2]
            self.tki += 1
            r0 = ti * T + tb * 128
            P.dma("sp", tk, self.x_in[r0:r0 + 128, :], tk)
            for c in range(8):
                pt = self.ps()
                P.transpose(pt[:, 0:128], tk[:, c * 128:(c + 1) * 128], self.ident)
                P.copy("dve" if c % 2 else "act", self.xres[c][:, tb * 128:(tb + 1) * 128], pt[:, 0:128])

    def rmsnorm(self, gain, dst, last=False):
        P, T = self.P, self.T
        pt = self.ps()
        tmpA = self.a32(2)
        for c in range(8):
            sq = tmpA[c % 2]
            P.act(sq, self.xres[c], AF.Square)
            P.mm(pt[:, 0:T], self.ones, sq, start=(c == 0), stop=(c == 7))
        rstd = tmpA[0]
        pt = pt[:, 0:T]
        P.act(rstd, pt, AF.Ln, bias=self.epsc, scale=1.0 / D)
        P.act(rstd, rstd, AF.Exp, scale=-0.5)
        for c in range(8):
            P.stt("dve", dst[c], self.xres[c], gain[:, c:c + 1], rstd, ALU.mult, ALU.mult)
        self.f32free(tmpA)

    def layer(self, l, ti):
        P, T = self.P, self.T
        w_in = self.wbf[("w_in", l)]
        wb = self.wbf[("w_branch", l)]
        self.rmsnorm(self.vcol[("norm_mix", l)], self.hT)
        self.merged = self.a32(8)
        self.gate = self.a32(4)
        branches = [("rwkv", 0, 1024, 0), ("hgrn", OFF_HGRN, 1024, 1024), ("ssm", OFF_SSM, 2048, 2048)]
        for bi, (name, off, width, brow) in enumerate(branches):
            nchunk = width // 128
            self.yT = self.a16(nchunk)
            if name in self.stub:
                def cons(pt, j, nb):
                    P.copy("act", self.yT[j], pt)
                self.proj(w_in, self.hT, off, width, cons)
            else:
                getattr(self, "mixer_" + name)(l, ti)
            for jj in range(0, 8, 4):
                gts = self.gate

                def cons_g(pt, j, nb, gts=gts):
                    P.act(gts[j], pt, AF.Sigmoid)
                self.proj(w_in, self.hT, OFF_GATE + bi * D + jj * 128, 512, cons_g, cw=512)

                def cons_b(pt, j, nb, gts=gts, jj=jj, bi=bi):
                    if bi == 0:
                        P.tt("dve", self.merged[jj + j], pt, gts[j], ALU.mult)
                    else:
                        P.tt("dve", gts[j], pt, gts[j], ALU.mult)
                        P.tt("pool", self.merged[jj + j], self.merged[jj + j], gts[j], ALU.add)
                self.proj(wb, self.yT[:nchunk], jj * 128, 512, cons_b, r0=brow, cw=512)
            self.f16free(self.yT)
        self.mergedb = self.a16(8)
        for j in range(8):
            P.copy("act", self.mergedb[j], self.merged[j])
        self.f32free(self.merged)
        def cons_o(pt, j, nb):
            P.tt("dve", self.xres[j], self.xres[j], pt, ALU.add)
        self.proj(self.wbf[("w_out", l)], self.mergedb, 0, D, cons_o)
        self.f16free(self.mergedb)
        self.ffn = self.a16(22)
        self.rmsnorm(self.vcol[("norm_ffn", l)], self.hT)
        wf = self.wbf[("w_ffn_in", l)]
        for jj in range(0, 22, 4):
            nbk = min(4, 22 - jj)
            sgs = self.gate

            def cons_gate(pt, j, nb, sgs=sgs):
                P.act(sgs[j], pt, AF.Silu)

            def cons_up(pt, j, nb, sgs=sgs, jj=jj):
                P.tt("dve", self.ffn[jj + j], pt, sgs[j], ALU.mult)
            self.proj(wf, self.hT, jj * 128, nbk * 128, cons_gate, cw=512)
            self.proj(wf, self.hT, FFN_H + jj * 128, nbk * 128, cons_up, cw=512)
        self.proj(self.wbf[("w_ffn_out", l)], self.ffn, 0, D, cons_o, cw=256)
        self.f16free(self.ffn)
        self.f32free(self.gate)

    def final(self, ti):
        P, T = self.P, self.T
        hf = self.a32(8)
        self.rmsnorm(self.vcol[("norm_final", None)], hf)
        for tb in range(T // 128):
            tk = self.tokbuf[self.tki % 2]
            self.tki += 1
            for c in range(8):
                pt = self.ps()
                P.transpose(pt[:, 0:128], hf[c][:, tb * 128:(tb + 1) * 128], self.ident)
                P.copy("dve" if c % 2 else "act", tk[:, c * 128:(c + 1) * 128], pt[:, 0:128])
            r0 = ti * T + tb * 128
            P.dma("sp", self.out[r0:r0 + 128, :], tk, tk)
        self.f32free(hf)


def make_in_map(inputs, b, S):
    m = {"x": np.ascontiguousarray(inputs["x"][b, :S])}
    for n in WNAMES:
        m[n] = np.ascontiguousarray(inputs[n])
    for n in VEC_SHAPES:
        m[n] = np.ascontiguousarray(inputs[n])
    return m


def bc(t, shape):
    return t.v(lambda a: a.broadcast_to(list(shape)))


def v3(t, inner=64):
    return t.v(lambda a: a.rearrange("p (c t) -> p c t", t=inner))


def setup_ssm(self, l):
    P = self.P
    NCH = self.NCH
    if not hasattr(self, "ssm"):
        self.ssm = {}
        T = self.T
        self.ext = [P.sbuf(f"ext{i}", [128, T + 3], F32) for i in range(2)]
        self.exti = 0
        self.rbd = [P.sbuf(f"rbd{i}", [32, 512], F32) for i in range(2)]
        self.e1 = [P.sbuf(f"e1_{i}", [64, 512], F32) for i in range(2)]
        self.cbs = [P.sbuf(f"cbs{i}", [64, 64], F32) for i in range(2)]
        self.Gb = P.sbuf("Gb", [64, NCH, 512], BF16)
        self.xtok = P.sbuf("xtok", [64, NCH, 512], BF16)
        self.xw = P.sbuf("xw", [64, NCH, 512], BF16)
        self.Btok = P.sbuf("Btok", [64, NCH, 128], BF16)
        self.Sb = [P.sbuf(f"Sb{i}", [128, 512], BF16) for i in range(2)]
        self.tokA = P.sbuf("tokA", [64, NCH * 32], F32)
        self.tokW = P.sbuf("tokW", [64, NCH * 32], F32)
        self.elast = P.sbuf("elast", [32, NCH], F32)
        self.rhs_e = P.sbuf("rhs_e", [32, NCH, 32], F32)
        self.elast_bc = P.sbuf("elast_bc", [128, NCH, 32], F32)
    d = {}
    P.dma("sp", self.vstage[0:24, 0:512],
          self.vin["ssm_conv_w"].v(lambda a: a[l].rearrange("(r c) j -> r (c j)", c=128)), self.vstage)
    cw = P.sbuf(f"convw{l}", [128, 4, 24], F32)
    for j in range(4):
        pt = self.ps()
        P.transpose(pt[:, 0:24], self.vstage.v(lambda a: a[0:24, j:512:4]), self.ident[0:24, 0:24])
        P.copy("dve", cw[:, j, :], pt[:, 0:24])
    d["cw"] = cw
    hp = P.sbuf(f"ssmh{l}", [32, 4], F32)
    for i, n in enumerate(["ssm_dt_bias", "ssm_a_log", "ssm_d"]):
        P.dma("sp", hp[:, i:i + 1], self.vin[n].v(lambda a: a[l].rearrange("(h o) -> h o", o=1)), hp)
    P.act(hp[:, 3:4], hp[:, 1:2], AF.Exp)
    P.ts("dve", hp[:, 3:4], hp[:, 3:4], -1.0, ALU.mult)
    d["hp"] = hp
    d2 = P.sbuf(f"ssmd2{l}", [32, 2], F32)
    P.copy("dve", d2[:, 0:1], hp[:, 2:3])
    P.copy("dve", d2[:, 1:2], hp[:, 2:3])
    pt = self.ps()
    for hpi in range(16):
        P.mm(pt[:, 2 * hpi:2 * hpi + 2], self.sel[:, 128 * hpi:128 * hpi + 128], d2)
    dcol = P.sbuf(f"dcol{l}", [128, 16], F32)
    P.copy("dve", dcol, pt.v(lambda a: a[:, 0:32].rearrange("p (h two) -> p h two", two=2)[:, :, 0]))
    d["dcol"] = dcol
    S = P.es.enter_context(self.nc.sbuf_tensor(f"ssmS{l}", [128, 4, 512], F32))
    d["S"] = [Tl(S[:, g, :], Buf(f"ssmS{l}_{g}")) for g in range(4)]
    for g in range(4):
        P.memset("pool", d["S"][g], 0.0)
    carry = P.sbuf(f"carry{l}", [128, 24, 3], F32)
    P.memset("pool", carry, 0.0)
    d["carry"] = carry
    self.ssm[l] = d


def mixer_ssm(self, l, ti):
    P, T, NCH = self.P, self.T, self.NCH
    d = self.ssm[l]
    w_in = self.wbf[("w_in", l)]
    c_z = OFF_SSM
    c_xbc = OFF_SSM + 2048
    c_dt = OFF_SSM + 2048 + 3072
    hpv, cw, cbv, dcol, carry = d["hp"], d["cw"], self.vcol[("ssm_conv_b", l)], d["dcol"], d["carry"]
    gnw = self.vcol[("ssm_gn_w", l)]
    dts = self.a32(5)
    raw, dtT, lndt, cum, wv = [t[0:32, :] for t in dts]

    def cons_dt(pt, j, nb):
        P.act(raw, pt, AF.Exp, bias=hpv[:, 0:1])
    self.proj(w_in, self.hT, c_dt, 32, cons_dt, cw=128)
    P.act(dtT, raw, AF.Ln, bias=self.ones[0:32, 0:1])
    P.act(lndt, dtT, AF.Ln)
    P.ts("dve", raw, dtT, hpv[:, 3:4], ALU.mult)
    ca, ra, rm = cum.ap, raw.ap, self.rmask[0:32, :].ap
    P.op("dve", lambda e: e.tensor_tensor_scan(out=ca, data0=rm, data1=ra, initial=0.0, op0=ALU.mult, op1=ALU.add),
         reads=[self.rmask, raw], writes=[cum])
    cs = lndt
    P.tt("dve", cs, cum, lndt, ALU.subtract)
    lastb = bc(v3(cum)[:, :, 63:64], [32, NCH, 64])
    P.tt("dve", v3(wv), lastb, v3(cs), ALU.subtract)
    P.act(wv, wv, AF.Exp)
    P.act(self.elast, v3(cum)[:, :, 63], AF.Exp)
    ptT = self.ps()
    for c in range(NCH):
        P.transpose(ptT[0:64, c * 32:(c + 1) * 32], cs[:, c * 64:(c + 1) * 64], self.ident[0:32, 0:32])
    P.copy("dve", self.tokA, ptT[0:64, 0:NCH * 32])
    ptT = self.ps()
    for c in range(NCH):
        P.transpose(ptT[0:64, c * 32:(c + 1) * 32], wv[:, c * 64:(c + 1) * 64], self.ident[0:32, 0:32])
    P.copy("dve", self.tokW, ptT[0:64, 0:NCH * 32])
    cs_tok = v3(self.tokA, 32)
    w_tok = v3(self.tokW, 32)
    P.tt("dve", self.rhs_e, bc(self.elast.v(lambda a: a.unsqueeze(2)), [32, NCH, 32]),
         bc(self.ident[0:32, 0:32].v(lambda a: a.unsqueeze(1)), [32, NCH, 32]), ALU.mult)
    pte = self.ps()
    P.mm(pte[:, 0:NCH * 32], self.ones[0:32, :], self.rhs_e.v(lambda a: a.rearrange("p c h -> p (c h)")))
    P.copy("dve", self.elast_bc.v(lambda a: a.rearrange("p c h -> p (c h)")), pte[:, 0:NCH * 32])

    yT = self.yT
    for g in range(4):
        xc = self.a32(4)
        xcb = self.a16(4)
        Bb, Cb = self.a16(2)

        def conv(pt, ci, dst32, dst16):
            ext = self.ext[self.exti % 2]
            self.exti += 1
            P.copy("pool", ext[:, 0:3], carry[:, ci, :])
            P.copy("act", ext[:, 3:3 + T], pt)
            P.copy("pool", carry[:, ci, :], ext[:, T:T + 3])
            acc = self.a32()
            P.ts("dve", acc, ext[:, 0:T], cw[:, 0, ci:ci + 1], ALU.mult, cbv[:, ci:ci + 1], ALU.add)
            for j in range(1, 4):
                P.stt("dve", acc, ext[:, j:j + T], cw[:, j, ci:ci + 1], acc, ALU.mult, ALU.add)
            if dst32 is not None:
                P.act(dst32, acc, AF.Silu)
                P.copy("pool", dst16, dst32)
            else:
                P.act(dst16, acc, AF.Silu)
            self.f32free(acc)

        self.proj(w_in, self.hT, c_xbc + 512 * g, 512, lambda pt, j, nb: conv(pt, 4 * g + j, xc[j], xcb[j]))
        self.proj(w_in, self.hT, c_xbc + 2048 + 128 * g, 128, lambda pt, j, nb: conv(pt, 16 + g, None, Bb), cw=128)
        self.proj(w_in, self.hT, c_xbc + 2560 + 128 * g, 128, lambda pt, j, nb: conv(pt, 20 + g, None, Cb), cw=128)
        for c in range(NCH):
            cs_ = slice(c * 64, (c + 1) * 64)
            rbd = self.rbd[c % 2]
            P.tt("dve", v3(rbd), bc(cum[:, cs_].v(lambda a: a.unsqueeze(1)), [32, 8, 64]),
                 bc(self.ident[0:32, 8 * g:8 * g + 8].v(lambda a: a.unsqueeze(2)), [32, 8, 64]), ALU.mult)
            pe = self.ps()
            P.mm(pe[0:64, :], self.ones[0:32, 0:64], rbd, start=True, stop=False)
            P.mm(pe[0:64, :], self.identb[0:64, 0:64], self.neg8.v(lambda a: a.rearrange("p h t -> p (h t)")),
                 start=False, stop=True)
            e1 = self.e1[c % 2]
            P.tt("dve", v3(e1), v3(pe[0:64, :]),
                 bc(cs_tok[:, c, 8 * g:8 * g + 8].v(lambda a: a.unsqueeze(2)), [64, 8, 64]), ALU.subtract)
            P.act(e1, e1, AF.Exp)
            pcb = self.ps()
            P.mm(pcb[0:64, 0:64], Bb[:, cs_], Cb[:, cs_])
            cbs = self.cbs[c % 2]
            P.copy("act", cbs, pcb[0:64, 0:64])
            P.tt("pool", v3(self.Gb[:, c, :]), v3(e1), bc(cbs.v(lambda a: a.unsqueeze(1)), [64, 8, 64]), ALU.mult)
            ptx = self.ps().v(lambda a: a.bitcast(BF16))
            for j in range(4):
                P.transpose(ptx[0:64, j * 128:(j + 1) * 128], xcb[j][:, cs_], self.identb)
            P.transpose(ptx[0:64, 512:640], Bb[:, cs_], self.identb)
            P.copy("act", self.xtok[:, c, :], ptx[0:64, 0:512])
            P.copy("act", self.Btok[:, c, :], ptx[0:64, 512:640])
            P.tt("pool", v3(self.xw[:, c, :]), v3(self.xtok[:, c, :]),
                 bc(w_tok[:, c, 8 * g:8 * g + 8].v(lambda a: a.unsqueeze(2)), [64, 8, 64]), ALU.mult)
        S = d["S"][g]
        inter = self.psb[4:8]
        for c in range(NCH):
            cs_ = slice(c * 64, (c + 1) * 64)
            Sb = self.Sb[c % 2]
            P.copy("act", Sb, S)
            for hp in range(4):
                P.mm(inter[hp][:, cs_], Sb[:, hp * 128:(hp + 1) * 128], Cb[:, cs_])
            pd = self.ps()
            P.mm(pd[:, 0:512], self.Btok[:, c, :], self.xw[:, c, :])
            P.tt("dve", v3(S), v3(S), bc(self.elast_bc[:, c, 8 * g:8 * g + 8].v(lambda a: a.unsqueeze(2)), [128, 8, 64]),
                 ALU.mult)
            P.tt("dve", S, S, pd[:, 0:512], ALU.add)
        ys = []
        pn = self.psb[4]
        for hp in range(4):
            hh = 4 * g + hp
            pe2 = self.ps()
            P.mm(pe2[:, 0:T], self.sel[:, 128 * hh:128 * hh + 128], cum)
            ecb = self.a32()
            P.act(ecb, pe2[:, 0:T], AF.Exp)
            tmp = self.a32()
            P.tt("dve", tmp, inter[hp][:, 0:T], ecb, ALU.mult)
            self.f32free(ecb)
            pin = self.ps()
            for c in range(NCH):
                for q in range(2):
                    hs = slice((2 * hp + q) * 64, (2 * hp + q) * 64 + 64)
                    P.mm(pin[64 * q:64 * q + 64, c * 64:(c + 1) * 64], self.xtok[:, c, hs], self.Gb[:, c, hs])
            P.stt("dve", tmp, xc[hp], dcol[:, hh:hh + 1], tmp, ALU.mult, ALU.add)
            P.tt("dve", tmp, tmp, pin[:, 0:T], ALU.add)
            zs = self.a32()
            self.proj(w_in, self.hT, c_z + 128 * hh, 128, lambda pt, j, nb: P.act(zs, pt, AF.Silu), cw=128)
            P.tt("pool", tmp, tmp, zs, ALU.mult)
            P.act(zs, tmp, AF.Square)
            P.mm(pn[:, 0:T], self.ones, zs, start=(hp == 0), stop=(hp == 3))
            self.f32free(zs)
            ys.append(tmp)
        rstd = self.a32()
        P.act(rstd, pn[:, 0:T], AF.Ln, bias=self.epsc, scale=1.0 / 512)
        P.act(rstd, rstd, AF.Exp, scale=-0.5)
        for hp in range(4):
            hh = 4 * g + hp
            P.stt("dve", yT[hh], ys[hp], gnw[:, hh:hh + 1], rstd, ALU.mult, ALU.mult)
        self.f32free(rstd, ys, xc)
        self.f16free(xcb, [Bb, Cb])
    self.f32free(dts)


Model.setup_ssm = setup_ssm
Model.mixer_ssm = mixer_ssm


def setup_hgrn(self, l):
    P = self.P
    NCH = self.NCH
    if not hasattr(self, "hg"):
        self.hg = {}
        self.load_cols(None, "vcLB", [("lb0", None, 8), ("lb1", None, 8)], srcs=[
            self.vin["hgrn_lb_logits"].v(lambda a: a[0].rearrange("(r c) -> r c", c=128)),
            self.vin["hgrn_lb_logits"].v(lambda a: a[1].rearrange("(r c) -> r c", c=128))])
        lb = P.sbuf("hg_lb", [128, 2, 8], F32)
        P.memset("dve", lb, 0.0)
        P.tt("dve", lb[:, 1, :], self.vcol[("lb1", None)], self.vcol[("lb0", None)], ALU.subtract)
        P.act(lb[:, 1, :], lb[:, 1, :], AF.Sigmoid)
        oml = P.sbuf("hg_oml", [128, 2, 8], F32)
        P.ts("dve", oml, lb, -1.0, ALU.mult, 1.0, ALU.add)
        self.hg_lb, self.hg_oml = lb, oml
        self.hg_vtok = P.sbuf("hg_vtok", [64, NCH, 128], BF16)
        self.hg_ktok = P.sbuf("hg_ktok", [64, NCH, 128], BF16)
        self.hg_scT = [P.sbuf(f"hg_scT{i}", [64, 64], BF16) for i in range(2)]
        self.hg_Sb = [P.sbuf(f"hg_Sb{i}", [128, 128], BF16) for i in range(2)]
        self.hg_cols = P.sbuf("hg_cols", [128, 5, NCH], F32)
    S = P.es.enter_context(self.nc.sbuf_tensor(f"hgS{l}", [128, 8, 128], F32))
    d = {"S": [Tl(S[:, h, :], Buf(f"hgS{l}_{h}")) for h in range(8)]}
    for h in range(8):
        P.memset("pool", d["S"][h], 0.0)
    self.hg[l] = d


def mixer_hgrn(self, l, ti):
    P, T, NCH = self.P, self.T, self.NCH
    d = self.hg[l]
    w_in = self.wbf[("w_in", l)]
    li = l
    gnw = self.vcol[("hgrn_gn_w", l)]
    for h in range(8):
        S = d["S"][h]
        f, cum, eq, ek, qs = self.a32(5)
        qb, kb, ib = self.a16(3)
        self.proj(w_in, self.hT, OFF_HGRN + 1024 + 128 * h, 128, lambda pt, j, nb: P.act(f, pt, AF.Sigmoid), cw=128)
        P.ts("dve", f, f, self.hg_oml[:, li, h:h + 1], ALU.mult, self.hg_lb[:, li, h:h + 1], ALU.add)
        if h == 0 and ti == 0:
            self.dump(f"hg_f{l}", f)
        P.act(eq, f, AF.Ln)
        if h == 0 and ti == 0:
            self.dump(f"hg_lnf{l}", eq)
        ca, la, rm = cum.ap, eq.ap, self.rmask.ap
        P.op("dve", lambda e, ca=ca, la=la, rm=rm: e.tensor_tensor_scan(out=ca, data0=rm, data1=la, initial=0.0,
                                                                     op0=ALU.mult, op1=ALU.add),
             reads=[self.rmask, eq], writes=[cum])
        if h == 0 and ti == 0:
            self.dump(f"hg_cumraw{l}", cum)
        P.ts("dve", f, f, -1.0, ALU.mult, 1.0, ALU.add)
        cols = self.hg_cols
        c3 = v3(cum)
        P.act(cols[:, 0, :], c3[:, :, 32], AF.Exp)
        P.act(cols[:, 1, :], c3[:, :, 63], AF.Exp)
        P.tt("dve", cols[:, 3, :], c3[:, :, 63], c3[:, :, 32], ALU.subtract)
        P.act(cols[:, 2, :], cols[:, 3, :], AF.Exp)
        P.copy("dve", cols[:, 4, :], c3[:, :, 32])
        P.tt("dve", c3, c3, bc(cols[:, 4, :].v(lambda a: a.unsqueeze(2)), [128, NCH, 64]), ALU.subtract)
        P.act(eq, cum, AF.Exp)
        P.act(ek, cum, AF.Exp, scale=-1.0)
        self.proj(w_in, self.hT, OFF_HGRN + 128 * h, 128, lambda pt, j, nb: P.act(qs, pt, AF.Silu), cw=128)
        P.tt("dve", qb, qs, eq, ALU.mult)
        P.tt("pool", kb, f, ek, ALU.mult)
        self.proj(w_in, self.hT, OFF_HGRN + 2048 + 128 * h, 128, lambda pt, j, nb: P.copy("act", ib, pt), cw=128)
        ptv = self.ps().v(lambda a: a.bitcast(BF16))
        for c in range(NCH):
            P.transpose(ptv[0:64, c * 128:(c + 1) * 128], ib[:, c * 64:(c + 1) * 64], self.identb)
        P.copy("act", self.hg_vtok.v(lambda a: a.rearrange("p c v -> p (c v)")), ptv[0:64, 0:NCH * 128])
        ptk = self.ps().v(lambda a: a.bitcast(BF16))
        for c in range(NCH):
            P.transpose(ptk[0:64, c * 128:(c + 1) * 128], kb[:, c * 64:(c + 1) * 64], self.identb)
        P.copy("dve", self.hg_ktok.v(lambda a: a.rearrange("p c v -> p (c v)")), ptk[0:64, 0:NCH * 128])
        po = self.psb[5]
        for c in range(NCH):
            cs_ = slice(c * 64, (c + 1) * 64)
            psc = self.ps()
            P.mm(psc[0:64, 0:64], kb[:, cs_], qb[:, cs_])
            scT = self.hg_scT[c % 2]
            P.tt("dve", scT, psc[0:64, 0:64], self.tri_incl, ALU.mult)
            Sb = self.hg_Sb[c % 2]
            P.ts("dve", Sb, S, cols[:, 0, c:c + 1], ALU.mult)
            P.mm(po[:, cs_], self.hg_vtok[:, c, :], scT, start=True, stop=False)
            P.mm(po[:, cs_], Sb, qb[:, cs_], start=False, stop=True)
            pd = self.ps()
            P.mm(pd[:, 0:128], self.hg_ktok[:, c, :], self.hg_vtok[:, c, :])
            P.ts("dve", S, S, cols[:, 1, c:c + 1], ALU.mult)
            P.stt("dve", S, pd[:, 0:128], cols[:, 2, c:c + 1], S, ALU.mult, ALU.add)
        o32 = qs
        P.copy("act", o32, po[:, 0:T])
        if h == 0 and ti == 0:
            self.dump(f"hg_o{l}", o32)
        P.act(eq, o32, AF.Square)
        pn = self.ps()
        P.mm(pn[:, 0:T], self.ones, eq)
        rstd = ek
        P.act(rstd, pn[:, 0:T], AF.Ln, bias=self.epsc, scale=1.0 / 128)
        P.act(rstd, rstd, AF.Exp, scale=-0.5)
        P.stt("dve", o32, o32, gnw[:, h:h + 1], rstd, ALU.mult, ALU.mult)
        self.proj(w_in, self.hT, OFF_HGRN + 3072 + 128 * h, 128, lambda pt, j, nb: P.act(f, pt, AF.Sigmoid), cw=128)
        P.tt("dve", self.yT[h], o32, f, ALU.mult)
        if h == 0 and ti == 0:
            self.dump(f"hg_y{l}", self.yT[h])
            self.dump(f"hg_cum{l}", cum)
            self.dump(f"hg_qb{l}", qb)
            self.dump(f"hg_kb{l}", kb)
            self.dump(f"hg_ib{l}", ib)
        self.f32free(f, cum, eq, ek, qs)
        self.f16free(qb, kb, ib)


Model.setup_hgrn = setup_hgrn
Model.mixer_hgrn = mixer_hgrn


C0 = float(np.exp(-0.5))


def setup_rwkv(self, l):
    P = self.P
    NCH, T = self.NCH, self.T
    if not hasattr(self, "rw"):
        self.rw = {}
        self.bones = P.sbuf("bones", [128, 128], F32)
        P.memset("pool", self.bones, 0.0)
        P.memset("pool", self.bones[0:64, 0:64], 1.0)
        P.memset("pool", self.bones[64:128, 64:128], 1.0)
        self.tri_ls = P.sbuf("tri_ls", [64, 64], F32)
        P.memset("pool", self.tri_ls, 1.0)
        ta = self.tri_ls.ap
        P.op("pool", lambda e: e.affine_select(out=ta, in_=ta, pattern=[[-1, 64]], compare_op=ALU.is_gt,
                                               fill=0.0, base=0, channel_multiplier=1),
             reads=[self.tri_ls], writes=[self.tri_ls])
        self.rext = [P.sbuf(f"rext{i}", [128, T + 1], F32) for i in range(2)]
        self.rexti = 0
        self.rw_Vtok = P.sbuf("rw_Vtok", [64, NCH, 128], BF16)
        self.rw_btok = P.sbuf("rw_btok", [64, NCH, 128], BF16)
        self.rw_ktok = P.sbuf("rw_ktok", [64, NCH, 128], BF16)
        self.rw_cols = P.sbuf("rw_cols", [128, 5, NCH], F32)
        self.rw_Sb = [P.sbuf(f"rw_Sb{i}", [128, 64], BF16) for i in range(2)]
        self.rw_lr = [P.sbuf(f"rw_lr{i}", [128, T], BF16) for i in range(4)]
        U = 2 * NCH
        self.rw_PU = [P.sbuf(f"rw_PU{i}", [64, 512], CHAIN_DT) for i in range(2)]
        self.rw_QU = [P.sbuf(f"rw_QU{i}", [64, 512], CHAIN_DT) for i in range(2)]
        self.rw_MTU = [P.sbuf(f"rw_MTU{i}", [64, 512], CHAIN_DT) for i in range(2)]
        self.rw_MTbU = P.sbuf("rw_MTbU", [64, U * 64], BF16)
        self.rw_A3 = P.sbuf("rw_A3", [64, 3, U * 64], BF16)
        self.rw_Xb2 = [P.sbuf(f"rw_Xb2{i}", [64, 128], BF16) for i in range(2)]
        self.rw_Ub2 = [P.sbuf(f"rw_Ub2{i}", [64, 128], BF16) for i in range(2)]
    d = {}
    specs = [("mu", None, 26), ("mu26", None, 1), ("muxa", None, 1)]
    srcs = [self.vin["rwkv_mu"].v(lambda a: a[l, 0:3328].rearrange("(r c) -> r c", c=128)),
            self.vin["rwkv_mu"].v(lambda a: a[l, 3328:3360].rearrange("(r c) -> r c", c=32)),
            self.vin["rwkv_mu"].v(lambda a: a[l, 3136:3200].rearrange("(r c) -> r c", c=64))]
    for n in ["w0", "a0", "k_k", "k_a", "gn_w", "gn_b"]:
        specs.append((n, None, 8))
        srcs.append(self.vin["rwkv_" + n].v(lambda a: a[l].rearrange("(r c) -> r c", c=128)))
    specs.append(("r_k", None, 8))
    srcs.append(self.vin["rwkv_r_k"].v(lambda a: a[l].rearrange("h (two c) -> (h two) c", two=1).rearrange("(r x) c -> r (x c)", x=2)))
    self.load_cols(("rw", l), f"vcR{l}", specs, srcs=srcs)
    for key, _, _ in specs:
        d[key] = self.vcol[(key, ("rw", l))]
    omk = P.sbuf(f"rw_omk{l}", [128, 8], F32)
    P.ts("dve", omk, d["k_a"], -1.0, ALU.mult, 1.0, ALU.add)
    d["omk"] = omk
    S = P.es.enter_context(self.nc.sbuf_tensor(f"rwS{l}", [128, 8, 64], F32))
    d["S"] = [Tl(S[:, p, :], Buf(f"rwS{l}_{p}")) for p in range(8)]
    for p in range(8):
        P.memset("pool", d["S"][p], 0.0)
    carry = P.sbuf(f"rw_carry{l}", [128, 28], F32)
    P.memset("pool", carry, 0.0)
    d["carry"] = carry
    self.rw[l] = d


def mixer_rwkv(self, l, ti):
    P, T, NCH = self.P, self.T, self.NCH
    d = self.rw[l]
    w_in = self.wbf[("w_in", l)]
    carry = d["carry"]

    def lerp(pt, np_, mucol, cidx, dst):
        ext = self.rext[self.rexti % 2]
        self.rexti += 1
        P.copy("pool", ext[0:np_, 0:1], carry[0:np_, cidx:cidx + 1])
        P.copy("act", ext[0:np_, 1:T + 1], pt)
        P.copy("pool", carry[0:np_, cidx:cidx + 1], ext[0:np_, T:T + 1])
        dd = self.a32()
        P.tt("dve", dd[0:np_, :], ext[0:np_, 0:T], ext[0:np_, 1:T + 1], ALU.subtract)
        P.stt("dve", dst, dd[0:np_, :], mucol, ext[0:np_, 1:T + 1], ALU.mult, ALU.add)
        self.f32free(dd)

    tmp = self.a32()
    txw, xab, sg0, sg1 = self.rw_lr
    self.proj(w_in, self.hT, 3072, 64, lambda pt, j, nb: lerp(pt, 64, d["mu"][0:64, 24:25], 24, tmp[0:64, :]), cw=128)
    P.act(txw[0:64, :], tmp[0:64, :], AF.Tanh)
    self.proj(w_in, self.hT, 3136, 64, lambda pt, j, nb: lerp(pt, 64, d["muxa"][0:64, 0:1], 27, tmp[0:64, :]), cw=128)
    P.copy("act", xab[0:64, :], tmp[0:64, :])
    self.proj(w_in, self.hT, 3200, 128, lambda pt, j, nb: lerp(pt, 128, d["mu"][:, 25:26], 25, tmp), cw=128)
    P.act(sg0, tmp, AF.Sigmoid)
    self.proj(w_in, self.hT, 3328, 32, lambda pt, j, nb: lerp(pt, 32, d["mu26"][0:32, 0:1], 26, tmp[0:32, :]), cw=128)
    P.act(sg1[0:32, :], tmp[0:32, :], AF.Sigmoid)
    self.f32free(tmp)
    cols = self.rw_cols
    for p in range(8):
        r, k, v, sg, Sc, a, kkn, g = self.a32(8)
        ab, rb, bb, kb, vb = self.a16(5)
        self.proj(w_in, self.hT, 128 * p, 128, lambda pt, j, nb: lerp(pt, 128, d["mu"][:, p:p + 1], p, r), cw=128)
        self.proj(w_in, self.hT, 1024 + 128 * p, 128, lambda pt, j, nb: lerp(pt, 128, d["mu"][:, 8 + p:9 + p], 8 + p, k), cw=128)
        self.proj(w_in, self.hT, 2048 + 128 * p, 128, lambda pt, j, nb: lerp(pt, 128, d["mu"][:, 16 + p:17 + p], 16 + p, v), cw=128)
        self.proj(self.wbf[("rwkv_w_up", l)], [txw[0:64, :]], 128 * p, 128,
                  lambda pt, j, nb: P.act(sg, pt, AF.Sigmoid, bias=d["w0"][:, p:p + 1]), cw=128, rows=64)
        self.proj(self.wbf[("rwkv_a_up", l)], [xab[0:64, :]], 128 * p, 128,
                  lambda pt, j, nb: P.act(a, pt, AF.Sigmoid, bias=d["a0"][:, p:p + 1]), cw=128, rows=64)
        vw0 = self.load_w(self.wbf[("rwkv_g_up", l)], 1, 128 * p, 128, r0=0, rows=128)
        vw1 = self.load_w(self.wbf[("rwkv_g_up", l)], 1, 128 * p, 128, r0=128, rows=32)
        pg = self.ps()
        P.mm(pg[:, 0:T], vw0[:, 0, :], sg0, start=True, stop=False)
        P.mm(pg[:, 0:T], vw1[0:32, 0, :], sg1[0:32, :], start=False, stop=True)
        P.copy("act", g, pg[:, 0:T])
        S_ = Sc
        sa, ga, rm = S_.ap, sg.ap, self.rmask.ap
        P.op("dve", lambda e, sa=sa, ga=ga, rm=rm: e.tensor_tensor_scan(out=sa, data0=rm, data1=ga, initial=0.0,
                                                                     op0=ALU.mult, op1=ALU.add),
             reads=[self.rmask, sg], writes=[S_])
        s3 = v3(S_)
        P.act(cols[:, 0, :], s3[:, :, 32], AF.Exp, scale=-C0)
        P.act(cols[:, 1, :], s3[:, :, 63], AF.Exp, scale=-C0)
        P.tt("dve", cols[:, 3, :], s3[:, :, 63], s3[:, :, 32], ALU.subtract)
        P.act(cols[:, 2, :], cols[:, 3, :], AF.Exp, scale=-C0)
        P.copy("dve", cols[:, 4, :], s3[:, :, 32])
        P.tt("dve", s3, s3, bc(cols[:, 4, :].v(lambda a_: a_.unsqueeze(2)), [128, NCH, 64]), ALU.subtract)
        e1, e2, t1 = self.a32(3)
        P.tt("dve", t1, Sc, sg, ALU.subtract)
        P.ts("dve", kkn, k, d["k_k"][:, p:p + 1], ALU.mult)
        P.act(e1, kkn, AF.Square)
        pn = self.ps()
        P.mm(pn[:, 0:T], self.bones, e1)
        P.act(e1, pn[:, 0:T], AF.Ln, bias=self.tinyc)
        P.act(e1, e1, AF.Exp, scale=-0.5)
        P.tt("dve", kkn, kkn, e1, ALU.mult)
        P.act(e2, t1, AF.Exp, scale=-C0)
        P.stt("dve", ab, kkn, -1.0, e2, ALU.mult, ALU.mult)
        P.act(e1, Sc, AF.Exp, scale=-C0)
        P.tt("pool", rb, r, e1, ALU.mult)
        P.act(e2, Sc, AF.Exp, scale=C0)
        P.tt("dve", t1, kkn, a, ALU.mult)
        P.tt("pool", bb, t1, e2, ALU.mult)
        P.ts("dve", t1, a, d["k_a"][:, p:p + 1], ALU.mult, d["omk"][:, p:p + 1], ALU.add)
        P.tt("dve", k, k, t1, ALU.mult)
        P.tt("pool", kb, k, e2, ALU.mult)
        P.copy("act", vb, v)
        P.stt("dve", t1, r, d["r_k"][:, p:p + 1], k, ALU.mult, ALU.mult)
        pbn = self.ps()
        P.mm(pbn[:, 0:T], self.bones, t1)
        bonus = r
        P.tt("dve", bonus, pbn[:, 0:T], v, ALU.mult)
        self.f32free(e1, e2, t1)
        for src, dstt in ((vb, self.rw_Vtok), (bb, self.rw_btok), (kb, self.rw_ktok)):
            ptt = self.ps().v(lambda a_: a_.bitcast(BF16))
            for c in range(NCH):
                P.transpose(ptt[0:64, c * 128:(c + 1) * 128], src[:, c * 64:(c + 1) * 64], self.identb)
            P.copy("act", dstt.v(lambda a_: a_.rearrange("p c v -> p (c v)")), ptt[0:64, 0:NCH * 128])
        S = d["S"][p]
        U = 2 * NCH
        A3, MTb = self.rw_A3, self.rw_MTbU
        def uidx(c, q):
            ug_, cc = divmod(c, 4)
            return ug_ * 8 + q * 4 + cc
        m_st = bc(self.tri_strict.v(lambda a_: a_.unsqueeze(1)), [64, 4, 64])
        m_ls = bc(self.tri_ls.v(lambda a_: a_.unsqueeze(1)), [64, 4, 64])
        m_in = bc(self.tri_incl.v(lambda a_: a_.unsqueeze(1)), [64, 4, 64])
        idb = bc(self.ident[0:64, 0:64].v(lambda a_: a_.unsqueeze(1)), [64, 8, 64])
        bk = self.psb
        for ug in range(U // 8):
            gsl = slice(ug * 512, (ug + 1) * 512)
            Pm, Qm, MT = self.rw_PU[0], self.rw_QU[0], self.rw_MTU[0]

            def batch(bank_pair, lhs, rhs):
                for q in range(2):
                    hs = slice(64 * q, 64 * q + 64)
                    for cc in range(4):
                        c = ug * 4 + cc
                        cs_ = slice(c * 64, (c + 1) * 64)
                        P.mm(bank_pair[q][0:64, cc * 64:(cc + 1) * 64], lhs[hs, cs_], rhs[hs, cs_])
            batch((bk[0], bk[1]), bb, ab)
            batch((bk[2], bk[3]), ab, bb)
            for q in range(2):
                qs_ = slice(q * 256, (q + 1) * 256)
                P.tt("dve", v3(Pm[:, qs_]), v3(bk[0 + q][0:64, 0:256]), m_st, ALU.mult)
                P.tt("dve", v3(Qm[:, qs_]), v3(bk[2 + q][0:64, 0:256]), m_ls, ALU.mult)
            P.tt("pool", v3(MT), v3(Pm), idb, ALU.add)
            batch((bk[0], bk[1]), kb, ab)
            batch((bk[2], bk[3]), bb, rb)
            batch((bk[6], bk[7]), kb, rb)
            for q in range(2):
                qs_ = slice(ug * 512 + q * 256, ug * 512 + (q + 1) * 256)
                P.tt("dve", v3(A3[:, 0, qs_]), v3(bk[0 + q][0:64, 0:256]), m_st, ALU.mult)
                P.tt("dve", v3(A3[:, 1, qs_]), v3(bk[2 + q][0:64, 0:256]), m_in, ALU.mult)
                P.tt("dve", v3(A3[:, 2, qs_]), v3(bk[6 + q][0:64, 0:256]), m_in, ALU.mult)
            b0, b1, b2 = bk[0], bk[1], bk[2]
            for it in range(1, 6):
                Pn, Qn, MTn = self.rw_PU[it % 2], self.rw_QU[it % 2], self.rw_MTU[it % 2]
                for ui in range(8):
                    us = slice(ui * 64, ui * 64 + 64)
                    if it < 5:
                        P.mm(b0[0:64, us], Qm[:, us], Pm[:, us])
                    P.mm(b1[0:64, us], Pm[:, us], Qm[:, us])
                if it < 5:
                    P.copy("act", Pn, b0[0:64, :])
                P.copy("dve", Qn, b1[0:64, :])
                for ui in range(8):
                    us = slice(ui * 64, ui * 64 + 64)
                    P.mm(b2[0:64, us], Qn[:, us], MT[:, us])
                P.tt("dve", MTn, MT, b2[0:64, :], ALU.add)
                Pm, Qm, MT = Pn, Qn, MTn
            P.copy("act", MTb[:, gsl], MT)
        pO = (bk[4], bk[6])
        for c in range(NCH):
            cs_ = slice(c * 64, (c + 1) * 64)
            Sb = self.rw_Sb[c % 2]
            P.ts("dve", Sb, S, cols[:, 0, c:c + 1], ALU.mult)
            pS = bk[5]
            pX = (bk[(c % 2) * 2], bk[(c % 2) * 2 + 1])
            pU = bk[7]
            Xb, Ub = self.rw_Xb2[c % 2], self.rw_Ub2[c % 2]
            for q in range(2):
                hs = slice(64 * q, 64 * q + 64)
                us = slice(uidx(c, q) * 64, uidx(c, q) * 64 + 64)
                Vt = self.rw_Vtok[:, c, hs]
                P.mm(pX[q][0:64, 0:64], ab[hs, cs_], Sb[hs, :], start=True, stop=False)
                P.mm(pX[q][0:64, 0:64], A3[:, 0, us], Vt, start=False, stop=True)
                P.copy("act", Xb[:, hs], pX[q][0:64, 0:64])
            for q in range(2):
                hs = slice(64 * q, 64 * q + 64)
                us = slice(uidx(c, q) * 64, uidx(c, q) * 64 + 64)
                P.mm(pU[0:64, hs], MTb[:, us], Xb[:, hs])
            P.copy("act", Ub, pU[0:64, 0:128])
            for q in range(2):
                hs = slice(64 * q, 64 * q + 64)
                Vt = self.rw_Vtok[:, c, hs]
                P.mm(pS[hs, 0:64], self.rw_btok[:, c, hs], Ub[:, hs], start=True, stop=False)
                P.mm(pS[hs, 0:64], self.rw_ktok[:, c, hs], Vt, start=False, stop=True)
            for q in range(2):
                hs = slice(64 * q, 64 * q + 64)
                us = slice(uidx(c, q) * 64, uidx(c, q) * 64 + 64)
                Vt = self.rw_Vtok[:, c, hs]
                P.mm(pO[q][hs, cs_], Sb[hs, :], rb[hs, cs_], start=True, stop=False)
                P.mm(pO[q][hs, cs_], Ub[:, hs], A3[:, 1, us], start=False, stop=False)
                P.mm(pO[q][hs, cs_], Vt, A3[:, 2, us], start=False, stop=True)
            P.ts("dve", S, S, cols[:, 1, c:c + 1], ALU.mult)
            P.stt("dve", S, pS[:, 0:64], cols[:, 2, c:c + 1], S, ALU.mult, ALU.add)
        o = k
        P.copy("act", o[0:64, :], pO[0][0:64, 0:T])
        P.copy("act", o[64:128, :], pO[1][64:128, 0:T])
        pm = self.ps()
        P.mm(pm[:, 0:T], self.bones, o)
        P.stt("dve", o, pm[:, 0:T], -1.0 / 64, o, ALU.mult, ALU.add)
        P.act(sg, o, AF.Square)
        pv = self.ps()
        P.mm(pv[:, 0:T], self.bones, sg)
        P.act(sg, pv[:, 0:T], AF.Ln, bias=self.gnepsc, scale=1.0 / 64)
        P.act(sg, sg, AF.Exp, scale=-0.5)
        P.stt("dve", o, o, d["gn_w"][:, p:p + 1], sg, ALU.mult, ALU.mult)
        P.stt("dve", o, o, d["gn_b"][:, p:p + 1], bonus, ALU.add, ALU.add)
        P.tt("dve", self.yT[p], o, g, ALU.mult)
        self.f32free(r, k, v, sg, Sc, a, kkn, g)
        self.f16free(ab, rb, bb, kb, vb)


Model.setup_rwkv = setup_rwkv
Model.mixer_rwkv = mixer_rwkv


def kernel(**inputs):
    inputs = {k_: np.asarray(v_) for k_, v_ in inputs.items()}
    S = 4096
    m = Model(NT=S // 256, T=256, layers=(0, 1), stub=())
    maps = [make_in_map(inputs, b, S) for b in range(8)]
    res = run_bass_kernel_spmd(m.nc, maps, core_ids=list(range(8)))
    out = np.stack([np.asarray(r["out"]) for r in res.results]).astype(np.float32)
    return out
```
